# Optimizing a Trainium2 kernel written in Bass

```python
import math
import jax, jax.numpy as jnp
from jax import lax
import numpy as np

D_MODEL = 1024
BATCH = 4
SEQ = 8192
DEPTH = 2

HG_HEADS = 4
HG_DK = 128
HG_DV = 128
HG_KW = HG_HEADS * HG_DK
HG_WIDTH = HG_HEADS * HG_DV
HG_CHUNK = 32
FX_HEADS = 4
FX_DH = 128
FX_WIDTH = FX_HEADS * FX_DH
Q_BLOCK = 128
ML_HEADS = 4
ML_DK = 128
ML_DV = 256
ML_KW = ML_HEADS * ML_DK
ML_WIDTH = ML_HEADS * ML_DV
ML_CONV = 4
ML_CHUNK = 64

N_AB = (DEPTH + 1) // 2
N_C = DEPTH // 2
ALPHA = (2 * DEPTH) ** 0.25
BETA = (8 * DEPTH) ** -0.25
LN_EPS = 1e-5
RMS_EPS = 1e-6

AB_SIZES = [HG_KW, HG_KW, HG_WIDTH, HG_WIDTH, FX_WIDTH, FX_WIDTH, FX_WIDTH, FX_HEADS, FX_WIDTH]
AB_VALUE_SLOTS = (2, 6)
AB_COLS = sum(AB_SIZES)
C_SIZES = [ML_KW, ML_KW, ML_WIDTH, ML_HEADS, ML_HEADS, ML_WIDTH, ML_WIDTH]
C_VALUE_SLOTS = (2,)
C_COLS = sum(C_SIZES)
AB_OUT_IN = HG_WIDTH + FX_WIDTH

kernel_name = "hybrid_hgrn2_fox_mlstm_deepnorm"


def _split(h, sizes):
    idx = np.cumsum(sizes)[:-1].tolist()
    return jnp.split(h, idx, axis=-1)


def _to_chunks(t, n_heads, d, L):
    B, S, _ = t.shape
    return t.reshape(B, S // L, L, n_heads, d).transpose(1, 0, 3, 2, 4)


def _gate_chunks(t, L):
    B, S, H = t.shape
    return t.reshape(B, S // L, L, H).transpose(1, 0, 3, 2)


def _from_chunks(t):
    NC, B, H, L, d = t.shape
    return t.transpose(1, 0, 3, 2, 4).reshape(B, NC * L, H * d)


def _rmsnorm_heads(t, n_heads, g):
    B, S, W = t.shape
    th = t.reshape(B, S, n_heads, W // n_heads)
    th = th * lax.rsqrt(jnp.mean(th * th, axis=-1, keepdims=True) + RMS_EPS)
    return th.reshape(B, S, W) * g.astype(jnp.float32)


def _layernorm(t, g, b):
    mu = jnp.mean(t, axis=-1, keepdims=True)
    var = jnp.mean(jnp.square(t - mu), axis=-1, keepdims=True)
    return (t - mu) * lax.rsqrt(var + LN_EPS) * g.astype(jnp.float32) + b.astype(jnp.float32)


def _hgrn2_mix(q, f_logit, v, lb):
    B = q.shape[0]
    f = lb + (1.0 - lb) * jax.nn.sigmoid(f_logit)
    log_f = jnp.log(f)
    k = 1.0 - f
    qc = _to_chunks(q, HG_HEADS, HG_DK, HG_CHUNK)
    kc = _to_chunks(k, HG_HEADS, HG_DK, HG_CHUNK)
    gc = _to_chunks(log_f, HG_HEADS, HG_DK, HG_CHUNK)
    vc = _to_chunks(v, HG_HEADS, HG_DV, HG_CHUNK)
    mask = jnp.tril(jnp.ones((HG_CHUNK, HG_CHUNK), dtype=bool))

    def step(state, inp):
        qb, kb, vb, gb = inp
        b = jnp.cumsum(gb, axis=2)
        q_dec = qb * jnp.exp(b)
        k_inv = kb * jnp.exp(-b)
        a = jnp.where(mask, jnp.einsum('bhtk,bhsk->bhts', q_dec, k_inv), 0.0)
        o = jnp.einsum('bhts,bhsv->bhtv', a, vb) + jnp.einsum('bhtk,bhkv->bhtv', q_dec, state)
        b_last = b[:, :, -1:, :]
        k_end = kb * jnp.exp(b_last - b)
        new_state = jnp.exp(b_last[:, :, 0, :])[..., None] * state + jnp.einsum('bhsk,bhsv->bhkv', k_end, vb)
        return new_state, o

    s0 = jnp.zeros((B, HG_HEADS, HG_DK, HG_DV), jnp.float32)
    _, o = lax.scan(step, s0, (qc, kc, vc, gc))
    return _from_chunks(o)


def _fox_mix(q, k, v, f_logit, b_f):
    B, S, _ = q.shape
    log_f = jax.nn.log_sigmoid(f_logit + b_f.astype(jnp.float32))
    c = jnp.cumsum(log_f, axis=1).transpose(0, 2, 1)
    scale = FX_DH ** -0.5
    qh = q.reshape(B, S, FX_HEADS, FX_DH).transpose(0, 2, 1, 3) * scale
    kh = k.reshape(B, S, FX_HEADS, FX_DH).transpose(0, 2, 1, 3)
    vh = v.reshape(B, S, FX_HEADS, FX_DH).transpose(0, 2, 1, 3)
    n_blk = S // Q_BLOCK
    q_blocks = qh.reshape(B, FX_HEADS, n_blk, Q_BLOCK, FX_DH).transpose(2, 0, 1, 3, 4)
    c_blocks = c.reshape(B, FX_HEADS, n_blk, Q_BLOCK).transpose(2, 0, 1, 3)
    key_pos = jnp.arange(S)

    def block(args):
        qb, cb, i = args
        s = jnp.einsum('bhtd,bhsd->bhts', qb, kh) + cb[..., :, None] - c[:, :, None, :]
        q_pos = i * Q_BLOCK + jnp.arange(Q_BLOCK)
        s = jnp.where(key_pos[None, :] <= q_pos[:, None], s, -jnp.inf)
        p = jax.nn.softmax(s, axis=-1)
        return jnp.einsum('bhts,bhsd->bhtd', p, vh)

    out = lax.map(block, (q_blocks, c_blocks, jnp.arange(n_blk)))
    return out.transpose(1, 2, 0, 3, 4).reshape(B, FX_HEADS, S, FX_DH).transpose(0, 2, 1, 3).reshape(B, S, FX_WIDTH)


def _causal_conv(u, w, b):
    C = u.shape[-1]
    out = lax.conv_general_dilated(u, w.astype(jnp.float32)[:, None, :], window_strides=(1,),
                                   padding=[(ML_CONV - 1, 0)], dimension_numbers=('NWC', 'WIO', 'NWC'),
                                   feature_group_count=C)
    return out + b.astype(jnp.float32)


def _mlstm_mix(q, k, v, i_pre, f_pre):
    B = q.shape[0]
    k = k * (ML_DK ** -0.5)
    qc = _to_chunks(q, ML_HEADS, ML_DK, ML_CHUNK)
    kc = _to_chunks(k, ML_HEADS, ML_DK, ML_CHUNK)
    vc = _to_chunks(v, ML_HEADS, ML_DV, ML_CHUNK)
    ic = _gate_chunks(i_pre, ML_CHUNK)
    gc = _gate_chunks(jax.nn.log_sigmoid(f_pre), ML_CHUNK)
    mask = jnp.tril(jnp.ones((ML_CHUNK, ML_CHUNK), dtype=bool))

    def step(carry, inp):
        Cm, n, m = carry
        qb, kb, vb, ib, gb = inp
        b = jnp.cumsum(gb, axis=-1)
        d = jnp.where(mask, b[..., :, None] - b[..., None, :] + ib[..., None, :], -jnp.inf)
        inter = b + m[..., None]
        m_t = jnp.maximum(inter, jnp.max(d, axis=-1))
        w = jnp.exp(d - m_t[..., None])
        s_inter = jnp.exp(inter - m_t)
        qk = jnp.einsum('bhtk,bhsk->bhts', qb, kb) * w
        num = jnp.einsum('bhts,bhsv->bhtv', qk, vb) + s_inter[..., None] * jnp.einsum('bhtk,bhkv->bhtv', qb, Cm)
        den = jnp.sum(qk, axis=-1) + s_inter * jnp.einsum('bhtk,bhk->bht', qb, n)
        h = num / jnp.maximum(jnp.abs(den), jnp.exp(-m_t))[..., None]
        b_last = b[..., -1]
        g_end = b_last[..., None] - b + ib
        m_new = jnp.maximum(b_last + m, jnp.max(g_end, axis=-1))
        wk = jnp.exp(g_end - m_new[..., None])
        decay = jnp.exp(b_last + m - m_new)
        C_new = decay[..., None, None] * Cm + jnp.einsum('bhs,bhsk,bhsv->bhkv', wk, kb, vb)
        n_new = decay[..., None] * n + jnp.einsum('bhs,bhsk->bhk', wk, kb)
        return (C_new, n_new, m_new), h

    carry0 = (jnp.zeros((B, ML_HEADS, ML_DK, ML_DV), jnp.float32),
              jnp.zeros((B, ML_HEADS, ML_DK), jnp.float32),
              jnp.zeros((B, ML_HEADS), jnp.float32))
    _, h = lax.scan(step, carry0, (qc, kc, vc, ic, gc))
    return _from_chunks(h)


def _ab_layer(x32, lb, w_in, fox_bf, hg_norm_g, w_out):
    h = jnp.einsum('bsd,dc->bsc', x32, w_in.astype(jnp.float32))
    hq, hf, hi, hz, fq, fk, fv, ff, fz = _split(h, AB_SIZES)
    hg = _hgrn2_mix(hq, hf, hi, lb)
    hg = _rmsnorm_heads(hg, HG_HEADS, hg_norm_g) * jax.nn.silu(hz)
    fx = _fox_mix(fq, fk, fv, ff, fox_bf) * jax.nn.silu(fz)
    return jnp.einsum('bsc,cd->bsd', jnp.concatenate([hg, fx], axis=-1), w_out.astype(jnp.float32))


def _c_layer(x32, w_in, conv_w, conv_b, b_i, b_f, norm_g, w_out):
    h = jnp.einsum('bsd,dc->bsc', x32, w_in.astype(jnp.float32))
    mq, mk, mv, mi, mf, mo, mz = _split(h, C_SIZES)
    qk = jax.nn.silu(_causal_conv(jnp.concatenate([mq, mk], axis=-1), conv_w, conv_b))
    mq, mk = jnp.split(qk, [ML_KW], axis=-1)
    ht = _mlstm_mix(mq, mk, mv, mi + b_i.astype(jnp.float32), mf + b_f.astype(jnp.float32))
    ht = jax.nn.sigmoid(mo) * ht
    ht = _rmsnorm_heads(ht, ML_HEADS, norm_g) * jax.nn.silu(mz)
    return jnp.einsum('bsc,cd->bsd', ht, w_out.astype(jnp.float32))


def setup_inputs(seed: int = 0) -> dict:
    key = jax.random.key(seed)
    ks = jax.random.split(key, 16)
    f32 = jnp.float32

    def col_scale(sizes, slots):
        return jnp.concatenate([jnp.full((s,), BETA if j in slots else 1.0, f32) for j, s in enumerate(sizes)])

    x = jax.random.normal(ks[0], (BATCH, SEQ, D_MODEL), f32)
    hgrn_lb_logits = 1.0 + 0.1 * jax.random.normal(ks[1], (DEPTH + 1, HG_KW), f32)
    ab_w_in = jax.random.normal(ks[2], (N_AB, D_MODEL, AB_COLS), f32) * (D_MODEL ** -0.5) * col_scale(AB_SIZES, AB_VALUE_SLOTS)
    ab_fox_bf = 0.1 * jax.random.normal(ks[3], (N_AB, FX_HEADS), f32)
    ab_hgrn_norm_g = 1.0 + 0.02 * jax.random.normal(ks[4], (N_AB, HG_WIDTH), f32)
    ab_w_out = jax.random.normal(ks[5], (N_AB, AB_OUT_IN, D_MODEL), f32) * (AB_OUT_IN ** -0.5) * BETA
    c_w_in = jax.random.normal(ks[6], (N_C, D_MODEL, C_COLS), f32) * (D_MODEL ** -0.5) * col_scale(C_SIZES, C_VALUE_SLOTS)
    c_conv_w = jax.random.normal(ks[7], (N_C, ML_CONV, 2 * ML_KW), f32) * (ML_CONV ** -0.5)
    c_conv_b = 0.02 * jax.random.normal(ks[8], (N_C, 2 * ML_KW), f32)
    c_bi = 0.1 * jax.random.normal(ks[9], (N_C, ML_HEADS), f32)
    c_bf = jax.random.uniform(ks[10], (N_C, ML_HEADS), f32, 3.0, 6.0)
    c_norm_g = 1.0 + 0.02 * jax.random.normal(ks[11], (N_C, ML_WIDTH), f32)
    c_w_out = jax.random.normal(ks[12], (N_C, ML_WIDTH, D_MODEL), f32) * (ML_WIDTH ** -0.5) * BETA
    ln_g = 1.0 + 0.02 * jax.random.normal(ks[13], (DEPTH, D_MODEL), f32)
    ln_b = 0.02 * jax.random.normal(ks[14], (DEPTH, D_MODEL), f32)
    return {"x": x, "hgrn_lb_logits": hgrn_lb_logits, "ab_w_in": ab_w_in, "ab_fox_bf": ab_fox_bf,
            "ab_hgrn_norm_g": ab_hgrn_norm_g, "ab_w_out": ab_w_out, "c_w_in": c_w_in,
            "c_conv_w": c_conv_w, "c_conv_b": c_conv_b, "c_bi": c_bi, "c_bf": c_bf,
            "c_norm_g": c_norm_g, "c_w_out": c_w_out, "ln_g": ln_g, "ln_b": ln_b}


def reference(x, hgrn_lb_logits, ab_w_in, ab_fox_bf, ab_hgrn_norm_g, ab_w_out, c_w_in,
              c_conv_w, c_conv_b, c_bi, c_bf, c_norm_g, c_w_out, ln_g, ln_b):
    lb_table = jnp.cumsum(jax.nn.softmax(hgrn_lb_logits.astype(jnp.float32), axis=0), axis=0)
    h = x.astype(jnp.float32)
    for l in range(DEPTH):
        if l % 2 == 0:
            j = l // 2
            y = _ab_layer(h, lb_table[l], ab_w_in[j], ab_fox_bf[j], ab_hgrn_norm_g[j], ab_w_out[j])
        else:
            j = l // 2
            y = _c_layer(h, c_w_in[j], c_conv_w[j], c_conv_b[j], c_bi[j], c_bf[j], c_norm_g[j], c_w_out[j])
        h = _layernorm(ALPHA * h + y, ln_g[l], ln_b[l])
    return h.astype(x.dtype)
```

```python
import numpy as np
from contextlib import ExitStack
import concourse.bass as bass
import concourse.mybir as mybir
from concourse.bass_utils import run_bass_kernel_spmd

F32 = mybir.dt.float32
BF16 = mybir.dt.bfloat16
AF = mybir.ActivationFunctionType
ALU = mybir.AluOpType

D = 1024
ALPHA = 4.0 ** 0.25
LN_EPS = 1e-5
RMS_EPS = 1e-6


class _Op:
    __slots__ = ("fn", "waits", "inc", "dma", "val")

    def __init__(self, fn, waits, dma):
        self.fn = fn
        self.waits = waits
        self.inc = False
        self.dma = dma
        self.val = 0


class Prog:
    ENGS = ("pe", "act", "dve", "pool", "sp")

    def __init__(self, nc, es):
        self.nc = nc
        self.es = es
        self.ops = {e: [] for e in self.ENGS}
        self.lastw = {}
        self.readers = {}
        self.known = {e: {} for e in self.ENGS}
        self.dma_count = {}

    def op(self, eng, fn, reads=(), writes=(), dma=None):
        writes = list(writes) + [r for r in reads if r.startswith("pb") and r not in writes]
        reads = [r for r in reads if not r.startswith("pb")]
        if dma is not None:
            cnt = self.dma_count.get(dma, 0) + 1
            self.dma_count[dma] = cnt
            me = ("dma:" + dma, cnt)
        else:
            me = (eng, len(self.ops[eng]))
        deps = []
        for r in reads:
            w = self.lastw.get(r)
            if w is not None:
                deps.append(w)
        for r in writes:
            w = self.lastw.get(r)
            if w is not None and not (w[0] == me[0] == "pe"):
                deps.append(w)
            for rn, ri in self.readers.get(r, {}).items():
                if rn != me[0] or dma is not None:
                    deps.append((rn, ri))
        kn = self.known[eng]
        wm = {}
        for name, idx in deps:
            if name.startswith("dma:"):
                idx = self.dma_count[name[4:]] - (1 if (dma is not None and name[4:] == dma) else 0)
            if kn.get(name, -1) >= idx:
                continue
            kn[name] = idx
            wm[name] = max(wm.get(name, -1), idx)
            if not name.startswith("dma:"):
                self.ops[name][idx].inc = True
        self.ops[eng].append(_Op(fn, list(wm.items()), dma))
        for r in reads:
            self.readers.setdefault(r, {})[me[0]] = me[1]
        for r in writes:
            self.lastw[r] = me
            self.readers[r] = {}
        return me

    def finish(self, resources):
        self.op("sp", None, reads=list(resources))

    def barrier(self):
        lasts = {}
        for e in self.ENGS:
            for i in range(len(self.ops[e]) - 1, -1, -1):
                if self.ops[e][i].fn is not None and self.ops[e][i].dma is None:
                    lasts[e] = i
                    break
        for e in self.ENGS:
            waits = []
            kn = self.known[e]
            for s_, idx in lasts.items():
                if s_ != e and kn.get(s_, -1) < idx:
                    waits.append((s_, idx))
                    kn[s_] = idx
                    self.ops[s_][idx].inc = True
            for k, cnt in self.dma_count.items():
                if kn.get("dma:" + k, -1) < cnt:
                    waits.append(("dma:" + k, cnt))
                    kn["dma:" + k] = cnt
            self.ops[e].append(_Op(None, waits, None))

    SEM_LIM = 4000

    def emit(self):
        nc = self.nc
        ops = self.ops
        LIM = self.SEM_LIM
        nep = {}
        for e in self.ENGS:
            c = 0
            for o in ops[e]:
                if o.inc:
                    c += 1
                o.val = c
            nep[e] = (c + LIM - 1) // LIM + 1
        sems = {e: [self.es.enter_context(nc.semaphore("s_%s%d" % (e, i))) for i in range(nep[e])] for e in self.ENGS}
        dsems = {k: self.es.enter_context(nc.semaphore("d_" + k)) for k in self.dma_count}
        for k, cnt in self.dma_count.items():
            assert 16 * cnt < 2 * LIM, (k, cnt)

        def run(ename, engine):
            for o in ops[ename]:
                for name, idx in o.waits:
                    if name.startswith("dma:"):
                        engine.wait_ge(dsems[name[4:]], 16 * idx)
                    else:
                        tgt = ops[name][idx]
                        assert tgt.inc and tgt.val > 0
                        engine.wait_ge(sems[name][(tgt.val - 1) // LIM], (tgt.val - 1) % LIM + 1)
                if o.fn is None:
                    continue
                ins = o.fn(engine)
                if o.dma is not None:
                    ins.then_inc(dsems[o.dma], 16)
                elif o.inc:
                    ins.then_inc(sems[ename][(o.val - 1) // LIM], 1)

        with nc.Block() as block:
            @block.tensor
            def _(e):
                run("pe", e)

            @block.scalar
            def _(e):
                run("act", e)

            @block.vector
            def _(e):
                run("dve", e)

            @block.gpsimd
            def _(e):
                run("pool", e)

            @block.sync
            def _(e):
                run("sp", e)

    def stats(self):
        return {e: (len(self.ops[e]), sum(1 for o in self.ops[e] if o.inc)) for e in self.ENGS}


class Arena:
    def __init__(self, t, nwords):
        self.t = t
        self.n = nwords
        self.off = 0
        self.base = 0

    def alloc(self, free_shape, dt, parts=128):
        n = int(np.prod(free_shape))
        words = n if dt == F32 else (n + 1) // 2
        words = (words + 7) // 8 * 8
        assert self.off + words <= self.n, ("SBUF arena overflow", self.off, words, self.n)
        ap = self.t[0:parts, self.off:self.off + words]
        self.off += words
        if dt != F32:
            ap = ap.bitcast(dt)
        ap = ap[:, 0:n]
        if len(free_shape) == 2:
            ap = ap.rearrange("p (a b) -> p a b", a=free_shape[0])
        elif len(free_shape) == 3:
            ap = ap.rearrange("p (a b c) -> p a b c", a=free_shape[0], b=free_shape[1])
        return ap

    def mark(self):
        self.base = self.off

    def reset(self):
        self.off = self.base


def build(T, dbg=False, phases=("A0", "B0", "C0", "A1", "C1")):
    NT = T // 128
    NST = T // 512
    nc = bass.Bass("TRN2", target_bir_lowering=False)
    din = lambda name, shape, dt=F32: nc.dram_tensor(name, shape, dt, kind="ExternalInput").ap()
    dsc = lambda name, shape, dt: nc.dram_tensor(name, shape, dt, kind="ExternalOutput" if dbg else "Internal").ap()
    x_d = din("x", [T, D])
    wf0_d = din("wf0", [D, 2052])
    wt0_d = din("wt0", [D, 2048])
    wo0_d = din("wo0", [D, D])
    wf1_d = din("wf1", [D, 1032])
    wt1_d = din("wt1", [D, 3072])
    wo1_d = din("wo1", [D, D])
    lbl_d = din("lbl", [128, 12])
    fbf_d = din("fbf", [4, 1])
    hng_d = din("hng", [128, 512])
    lng_d = din("lng", [128, 2 * D])
    lnb_d = din("lnb", [128, 2 * D])
    cw_d = din("cw", [128, 32])
    cb_d = din("cb", [128, 8])
    bi_d = din("bi", [4, 1])
    bfm_d = din("bfm", [4, 1])
    ng1_d = din("ng1", [128, D])
    ident_d = din("ident", [128, 128])
    triu_d = din("triu", [128, 128])
    dmask_d = din("dmask", [128, 128])
    rmask_d = din("rmask", [128, 512])
    sel_d = din("sel", [4, 512])
    out_d = nc.dram_tensor("out", [T, D], F32, kind="ExternalOutput").ap()
    GH = dsc("GH", [T, D], BF16)
    FQ = dsc("FQ", [4, 128, T], BF16)
    FK = dsc("FK", [4, 128, T], BF16)
    FV = dsc("FV", [4, T, 128], BF16)
    FZ = dsc("FZ", [T, 512], BF16)
    CT = dsc("CT", [4, T], F32)
    H1 = dsc("H1", [T, D], F32)
    G1 = dsc("G1", [T, D], BF16)

    with ExitStack() as es:
        P = Prog(nc, es)
        AW = 50000
        arena_t = es.enter_context(nc.sbuf_tensor("arena", [128, AW], F32))
        A = Arena(arena_t, AW)
        banks = [es.enter_context(nc.psum_tensor("pb%d" % i, [128, 512], F32)) for i in range(8)]
        BN = {id(b): "pb%d" % i for i, b in enumerate(banks)}

        ucnt = [0]

        def dma(out, in_, reads, writes, key, eng="sp"):
            if key == "cst":
                key = "cst%d" % ucnt[0]
                ucnt[0] += 1
            P.op(eng, lambda e: e.dma_start(out=out, in_=in_), reads=reads, writes=writes, dma=key)

        identf = A.alloc([128], F32)
        identb = A.alloc([128], BF16)
        triu = A.alloc([128], F32)
        dmask = A.alloc([128], F32)
        rmask = A.alloc([512], F32)
        ones = A.alloc([512], F32)
        sel = A.alloc([512], F32, parts=4)
        dma(identf, ident_d, [], ["identf"], "cst")
        dma(triu, triu_d, [], ["triu"], "cst")
        dma(dmask, dmask_d, [], ["dmask"], "cst")
        dma(rmask, rmask_d, [], ["rmask"], "cst")
        dma(sel, sel_d, [], ["sel"], "cst")
        P.op("pool", lambda e: e.tensor_copy(out=identb, in_=identf), ["identf"], ["identb"])
        P.op("pool", lambda e: e.memset(ones, 1.0), [], ["ones"])
        A.mark()

        def load_weights(w_d, ncols, wbf, name, wst):
            for k in range(8):
                s = k % 2
                dma(wst[s][:, 0:ncols], w_d[k * 128:(k + 1) * 128, :], [], ["wst%d" % s], "wld%d" % s)
                eng = ("dve", "act", "pool")[k % 3]
                if eng == "act":
                    P.op("act", lambda e, k=k, s=s: e.copy(out=wbf[:, k, :], in_=wst[s][:, 0:ncols]), ["wst%d" % s], [name])
                else:
                    P.op(eng, lambda e, k=k, s=s: e.tensor_copy(out=wbf[:, k, :], in_=wst[s][:, 0:ncols]), ["wst%d" % s], [name])

        def front(src_d, st, xin, xb, xT, tpb):
            for q in range(4):
                r0 = (st * 4 + q) * 128
                s_ = q % 2
                dma(xin[:, s_, :], src_d[r0:r0 + 128, :], [], ["xin%d" % s_], "xin%d" % s_)
                P.op("pool", lambda e, s_=s_: e.tensor_copy(out=xb[:, s_, :], in_=xin[:, s_, :]), ["xin%d" % s_], ["xb%d" % s_])
                tb = tpb[q % len(tpb)]
                tn = BN[id(tb)]
                tv = tb[:, 0:512].bitcast(BF16).rearrange("p (k t) -> p k t", k=8)
                for k in range(8):
                    P.op("pe", lambda e, s_=s_, k=k, tv=tv: e.transpose(out=tv[:, k, :], in_=xb[:, s_, k * 128:(k + 1) * 128], identity=identb),
                         ["xb%d" % s_, "identb"], [tn])
                P.op("act", lambda e, q=q, tv=tv: e.copy(out=xT[:, :, q * 128:(q + 1) * 128], in_=tv), [tn], ["xT"])

        def fm_mm(bank, bname, M, wbf, wname, c0, xT):
            for k in range(8):
                P.op("pe", lambda e, k=k: e.matmul(bank[0:M, :], lhsT=wbf[:, k, c0:c0 + M], rhs=xT[:, k, :], start=(k == 0), stop=(k == 7)),
                     [wname, "xT"], [bname])

        def tm_mm(bank, bname, q, wbf, wname, c0, n, xT):
            for k in range(8):
                P.op("pe", lambda e, k=k: e.matmul(bank[:, 0:n], lhsT=xT[:, k, q * 128:(q + 1) * 128], rhs=wbf[:, k, c0:c0 + n], start=(k == 0), stop=(k == 7)),
                     [wname, "xT"], [bname])

        def phase_a0():
            A.reset()
            big = A.alloc([4112], F32)
            wst = [big[:, 0:2052], big[:, 2056:2056 + 2052]]
            hqT = big[:, 0:2048].rearrange("p (a b) -> p a b", a=4)
            sg = big[:, 2048:4096].rearrange("p (a b) -> p a b", a=4)
            wf = A.alloc([8, 2052], BF16)
            wt = A.alloc([8, 2048], BF16)
            load_weights(wf0_d, 2052, wf, "wf", wst)
            load_weights(wt0_d, 2048, wt, "wt", wst)
            P.barrier()
            xin = A.alloc([2, D], F32)
            xb = A.alloc([2, D], BF16)
            xT = A.alloc([8, 512], BF16)
            hng = A.alloc([512], F32)
            lbl = A.alloc([12], F32)
            lbe = A.alloc([12], F32)
            lb = A.alloc([4], F32)
            oml = A.alloc([4], F32)
            lbm1 = A.alloc([4], F32)
            fbf = A.alloc([1], F32, parts=4)
            dma(hng, hng_d, [], ["hng"], "cst")
            dma(lbl, lbl_d, [], ["lbl"], "cst")
            dma(fbf, fbf_d, [], ["fbf"], "cst")
            P.op("act", lambda e: e.activation(out=lbe, in_=lbl, func=AF.Exp), ["lbl"], ["lbe"])
            lbe3 = lbe.rearrange("p (h r) -> p h r", h=4)
            P.op("dve", lambda e: e.tensor_tensor(out=lb, in0=lbe3[:, :, 0], in1=lbe3[:, :, 1], op=ALU.add), ["lbe"], ["lbs"])
            P.op("dve", lambda e: e.tensor_tensor(out=lb, in0=lb, in1=lbe3[:, :, 2], op=ALU.add), ["lbe", "lbs"], ["lbs"])
            P.op("dve", lambda e: e.reciprocal(out=lb, in_=lb), ["lbs"], ["lbs"])
            P.op("dve", lambda e: e.tensor_tensor(out=lb, in0=lb, in1=lbe3[:, :, 0], op=ALU.mult), ["lbs", "lbe"], ["lb", "lbs"])
            P.op("dve", lambda e: e.tensor_scalar(out=oml, in0=lb, scalar1=-1.0, scalar2=1.0, op0=ALU.mult, op1=ALU.add), ["lb"], ["oml"])
            P.op("dve", lambda e: e.tensor_scalar(out=lbm1, in0=lb, scalar1=-1.0, scalar2=None, op0=ALU.add), ["lb"], ["lbm1"])

            vb = A.alloc([2, 512], BF16)
            gz = A.alloc([2, 512], F32)
            sil = A.alloc([512], F32)
            fvb = A.alloc([2, 512], BF16)
            fzb = A.alloc([2, 512], BF16)
            gl = A.alloc([4, 512], F32)
            bb = A.alloc([4, 512], F32)
            qdec = A.alloc([4, 512], BF16)
            kinv = A.alloc([4, 512], BF16)
            kend = A.alloc([4, 512], BF16)
            fqb = A.alloc([4, 512], BF16)
            fkb = A.alloc([4, 512], BF16)
            dd = A.alloc([4, 4], F32)
            d1 = A.alloc([4, 4], F32)
            d2 = A.alloc([4, 4], F32)
            d3 = A.alloc([4, 4], F32)
            ffs = A.alloc([512], F32, parts=4)
            ls = A.alloc([512], F32, parts=4)
            cT = A.alloc([512], F32, parts=4)
            ccar = A.alloc([1], F32, parts=4)
            S = A.alloc([4, 128], F32)
            Sbf = A.alloc([4, 128], BF16)
            aTm = [A.alloc([128], BF16), A.alloc([128], BF16)]
            kit = [A.alloc([128], BF16), A.alloc([128], BF16)]
            tmpU = [A.alloc([128], F32), A.alloc([128], F32)]
            junk = A.alloc([128], F32)
            ssq = A.alloc([4], F32)
            rstd = A.alloc([4], F32)
            ghb = A.alloc([512], BF16)
            P.op("pool", lambda e: e.memset(S, 0.0), [], ["S0", "S1", "S2", "S3"])
            tpb = [banks[0]]
            prj = [banks[1], banks[2], banks[3]]
            pji = [0]
            HB = [banks[4], banks[5], banks[6], banks[7]]
            aTm4 = [A.alloc([128], BF16) for _ in range(4)]
            kit4 = [A.alloc([128], BF16) for _ in range(4)]
            tmpU4 = [A.alloc([128], F32) for _ in range(4)]
            junk4 = [A.alloc([128], F32) for _ in range(4)]

            def a0_tile(t0, q):
                r0 = t0 + q * 128
                qs = slice(q * 128, (q + 1) * 128)
                v2 = q % 2
                for c in range(4):
                    bk = prj[pji[0] % 3]
                    bn = BN[id(bk)]
                    pji[0] += 1
                    tm_mm(bk, bn, q, wt, "wt", c * 512, 512, xT)
                    if c == 0:
                        P.op("act", lambda e, bk=bk: e.copy(out=vb[:, v2, :], in_=bk[:, :]), [bn], ["vb%d" % v2])
                    elif c == 1:
                        P.op("act", lambda e, bk=bk: e.activation(out=sil, in_=bk[:, :], func=AF.Silu), [bn], ["sil"])
                        P.op("dve", lambda e: e.tensor_tensor(out=gz[:, v2, :], in0=sil, in1=hng, op=ALU.mult), ["sil", "hng"], ["gz%d" % v2])
                    elif c == 2:
                        P.op("act", lambda e, bk=bk: e.copy(out=fvb[:, v2, :], in_=bk[:, :]), [bn], ["fvb%d" % v2])
                        dma(FV[:, r0:r0 + 128, :].rearrange("h p d -> p h d"), fvb[:, v2, :].rearrange("p (h d) -> p h d", h=4),
                            ["fvb%d" % v2], ["FV"], "fvst%d" % v2)
                    else:
                        P.op("act", lambda e, bk=bk: e.activation(out=fzb[:, v2, :], in_=bk[:, :], func=AF.Silu), [bn], ["fzb%d" % v2])
                        dma(FZ[r0:r0 + 128, :], fzb[:, v2, :], ["fzb%d" % v2], ["FZ"], "fzst%d" % v2)
                hs_ = [slice(h * 128, (h + 1) * 128) for h in range(4)]
                for h in range(4):
                    hb = HB[h]
                    nb = BN[id(hb)]
                    kT_ps = hb[:, 384:448].bitcast(BF16)
                    P.op("pe", lambda e, h=h, hb=hb: e.matmul(hb[:, 0:128], lhsT=kinv[:, h, qs], rhs=qdec[:, h, qs], start=True, stop=True),
                         ["kinv%d" % h, "qdec%d" % h], [nb])
                    P.op("pe", lambda e, h=h, kT_ps=kT_ps: e.transpose(out=kT_ps, in_=kend[:, h, qs], identity=identb), ["kend%d" % h, "identb"], [nb])
                for h in range(4):
                    hb = HB[h]
                    nb = BN[id(hb)]
                    kT_ps = hb[:, 384:448].bitcast(BF16)
                    P.op("dve", lambda e, h=h, hb=hb: e.tensor_tensor(out=aTm4[h], in0=hb[:, 0:128], in1=triu, op=ALU.mult), [nb, "triu"], ["aTm%d" % h])
                    P.op("act", lambda e, h=h, kT_ps=kT_ps: e.copy(out=kit4[h], in_=kT_ps), [nb], ["kit%d" % h])
                    P.op("dve", lambda e, h=h: e.tensor_scalar(out=Sbf[:, h, :], in0=S[:, h, :], scalar1=d3[:, h, q:q + 1], scalar2=None, op0=ALU.mult),
                         ["S%d" % h, "d3_%d" % h], ["Sbf%d" % h])
                for h in range(4):
                    hb = HB[h]
                    nb = BN[id(hb)]
                    P.op("pe", lambda e, h=h, hb=hb: e.matmul(hb[:, 256:384], lhsT=aTm4[h], rhs=vb[:, v2, hs_[h]], start=True, stop=False),
                         ["aTm%d" % h, "vb%d" % v2], [nb])
                    P.op("pe", lambda e, h=h, hb=hb: e.matmul(hb[:, 256:384], lhsT=qdec[:, h, qs], rhs=Sbf[:, h, :], start=False, stop=True),
                         ["qdec%d" % h, "Sbf%d" % h], [nb])
                    P.op("pe", lambda e, h=h, hb=hb: e.matmul(hb[:, 128:256], lhsT=kit4[h], rhs=vb[:, v2, hs_[h]], start=True, stop=True),
                         ["kit%d" % h, "vb%d" % v2], [nb])
                for h in range(4):
                    hb = HB[h]
                    nb = BN[id(hb)]
                    P.op("dve", lambda e, h=h, hb=hb: e.scalar_tensor_tensor(out=S[:, h, :], in0=S[:, h, :], scalar=d1[:, h, q:q + 1], in1=hb[:, 128:256], op0=ALU.mult, op1=ALU.add),
                         ["S%d" % h, "d1_%d" % h, nb], ["S%d" % h])
                    P.op("act", lambda e, h=h, hb=hb: e.activation(out=junk4[h], in_=hb[:, 256:384], func=AF.Square, accum_out=ssq[:, h:h + 1]),
                         [nb], ["junk%d" % h, "ssq"])
                P.op("dve", lambda e: e.tensor_scalar(out=rstd, in0=ssq, scalar1=1.0 / 128, scalar2=RMS_EPS, op0=ALU.mult, op1=ALU.add), ["ssq"], ["rstd"])
                P.op("act", lambda e: e.activation(out=rstd, in_=rstd, func=AF.Sqrt), ["rstd"], ["rstd"])
                P.op("dve", lambda e: e.reciprocal(out=rstd, in_=rstd), ["rstd"], ["rstd"])
                for h in range(4):
                    hb = HB[h]
                    nb = BN[id(hb)]
                    P.op("dve", lambda e, h=h, hb=hb: e.scalar_tensor_tensor(out=ghb[:, hs_[h]], in0=hb[:, 256:384], scalar=rstd[:, h:h + 1], in1=gz[:, v2, hs_[h]], op0=ALU.mult, op1=ALU.mult),
                         [nb, "rstd", "gz%d" % v2], ["ghb"])
                dma(GH[r0:r0 + 128, 0:512], ghb, ["ghb"], ["GH"], "ghst")

            SC = 128 ** -0.5
            fmi = 0
            tmi = 0
            for st in range(NST):
                front(x_d, st, xin, xb, xT, tpb)
                t0 = st * 512
                for h in range(4):
                    for slot in range(4):
                        bk = prj[pji[0] % 3]
                        bn = BN[id(bk)]
                        pji[0] += 1
                        fm_mm(bk, bn, 128, wf, "wf", slot * 512 + h * 128, xT)
                        if slot == 0:
                            P.op("act", lambda e, h=h, bk=bk: e.copy(out=hqT[:, h, :], in_=bk[:, :]), [bn], ["hqT%d" % h])
                        elif slot == 1:
                            P.op("act", lambda e, h=h, bk=bk: e.activation(out=sg[:, h, :], in_=bk[:, :], func=AF.Sigmoid), [bn], ["sg%d" % h])
                            P.op("act", lambda e, h=h: e.activation(out=gl[:, h, :], in_=sg[:, h, :], func=AF.Ln, scale=oml[:, h:h + 1], bias=lb[:, h:h + 1]),
                                 ["sg%d" % h, "oml", "lb"], ["gl%d" % h])
                            P.op("dve", lambda e, h=h: e.tensor_scalar(out=sg[:, h, :], in0=sg[:, h, :], scalar1=-1.0, scalar2=lbm1[:, h:h + 1], op0=ALU.add, op1=ALU.mult),
                                 ["sg%d" % h, "lbm1"], ["sg%d" % h])
                            P.op("dve", lambda e, h=h: e.tensor_tensor_scan(out=bb[:, h, :], data0=rmask, data1=gl[:, h, :], initial=0.0, op0=ALU.mult, op1=ALU.add),
                                 ["gl%d" % h, "rmask"], ["bb%d" % h])
                            bmid = bb[:, h, 63:512:128]
                            blast = bb[:, h, 127:512:128]
                            P.op("act", lambda e, h=h, bmid=bmid: e.activation(out=d3[:, h, :], in_=bmid, func=AF.Exp), ["bb%d" % h], ["d3_%d" % h])
                            P.op("act", lambda e, h=h, blast=blast: e.activation(out=d1[:, h, :], in_=blast, func=AF.Exp), ["bb%d" % h], ["d1_%d" % h])
                            P.op("dve", lambda e, h=h, bmid=bmid, blast=blast: e.tensor_tensor(out=dd[:, h, :], in0=blast, in1=bmid, op=ALU.subtract), ["bb%d" % h], ["dd%d" % h])
                            P.op("act", lambda e, h=h: e.activation(out=d2[:, h, :], in_=dd[:, h, :], func=AF.Exp), ["dd%d" % h], ["d2_%d" % h])
                            P.op("pool", lambda e, h=h, bmid=bmid: e.tensor_copy(out=dd[:, h, :], in_=bmid), ["bb%d" % h, "d2_%d" % h], ["dd%d" % h])
                            bb3 = bb[:, h, :].rearrange("p (q t) -> p q t", q=4)
                            P.op("dve", lambda e, h=h, bb3=bb3: e.tensor_tensor(out=bb3, in0=bb3, in1=dd[:, h, :].unsqueeze(2).to_broadcast([128, 4, 128]), op=ALU.subtract),
                                 ["bb%d" % h, "d3_%d" % h, "d1_%d" % h, "dd%d" % h], ["bb%d" % h])
                            P.op("act", lambda e, h=h: e.activation(out=gl[:, h, :], in_=bb[:, h, :], func=AF.Exp), ["bb%d" % h, "gl%d" % h], ["gl%d" % h])
                            P.op("act", lambda e, h=h: e.activation(out=bb[:, h, :], in_=bb[:, h, :], func=AF.Exp, scale=-1.0), ["bb%d" % h], ["bb%d" % h])
                            P.op("dve", lambda e, h=h: e.tensor_tensor(out=qdec[:, h, :], in0=hqT[:, h, :], in1=gl[:, h, :], op=ALU.mult),
                                 ["hqT%d" % h, "gl%d" % h], ["qdec%d" % h])
                            P.op("dve", lambda e, h=h: e.tensor_tensor(out=kinv[:, h, :], in0=sg[:, h, :], in1=bb[:, h, :], op=ALU.mult),
                                 ["sg%d" % h, "bb%d" % h], ["kinv%d" % h])
                            P.op("dve", lambda e, h=h: e.tensor_tensor(out=kend[:, h, :].rearrange("p (q t) -> p q t", q=4), in0=kinv[:, h, :].rearrange("p (q t) -> p q t", q=4),
                                                                     in1=d2[:, h, :].unsqueeze(2).to_broadcast([128, 4, 128]), op=ALU.mult),
                                 ["kinv%d" % h, "d2_%d" % h], ["kend%d" % h])
                        elif slot == 2:
                            P.op("act", lambda e, h=h, bk=bk: e.mul(out=fqb[:, h, :], in_=bk[:, :], mul=SC), [bn], ["fqb%d" % h])
                            dma(FQ[h, :, t0:t0 + 512], fqb[:, h, :], ["fqb%d" % h], ["FQ"], "fqst%d" % h)
                        else:
                            P.op("act", lambda e, h=h, bk=bk: e.copy(out=fkb[:, h, :], in_=bk[:, :]), [bn], ["fkb%d" % h])
                            dma(FK[h, :, t0:t0 + 512], fkb[:, h, :], ["fkb%d" % h], ["FK"], "fkst%d" % h)
                bk = prj[pji[0] % 3]
                bn = BN[id(bk)]
                pji[0] += 1
                fm_mm(bk, bn, 4, wf, "wf", 2048, xT)
                P.op("act", lambda e, bk=bk: e.activation(out=ffs, in_=bk[0:4, :], func=AF.Sigmoid, bias=fbf), [bn, "fbf"], ["ffs"])
                P.op("act", lambda e: e.activation(out=ls, in_=ffs, func=AF.Ln), ["ffs"], ["ls"])
                if st == 0:
                    P.op("dve", lambda e: e.tensor_tensor_scan(out=cT, data0=ones[0:4, :], data1=ls, initial=0.0, op0=ALU.mult, op1=ALU.add), ["ls", "ones"], ["cT"])
                else:
                    P.op("dve", lambda e: e.tensor_tensor_scan(out=cT, data0=ones[0:4, :], data1=ls, initial=ccar, op0=ALU.mult, op1=ALU.add), ["ls", "ones", "ccar"], ["cT"])
                P.op("pool", lambda e: e.tensor_copy(out=ccar, in_=cT[:, 511:512]), ["cT"], ["ccar"])
                dma(CT[:, t0:t0 + 512], cT, ["cT"], ["CT"], "ctst")
                for q in range(4):
                    a0_tile(t0, q)
            P.barrier()

        def phase_b0():
            A.reset()
            kTh = A.alloc([T], BF16)
            qTh = A.alloc([T], BF16)
            vaug = A.alloc([NT, 130], BF16)
            cq = A.alloc([T], F32)
            cqd = A.alloc([T], F32)
            ctr = A.alloc([128], F32)
            negc = A.alloc([NT], F32)
            fzg = A.alloc([4, 128], BF16)
            tmp = [A.alloc([512], F32) for _ in range(3)]
            pT = [A.alloc([512], BF16) for _ in range(3)]
            rden = A.alloc([4], F32)
            gfb = A.alloc([4, 128], BF16)
            P.op("pool", lambda e: e.memset(vaug, 1.0), [], ["vaug"])
            sTb = [banks[0], banks[1], banks[7]]
            accb = [banks[2], banks[3], banks[4], banks[5]]
            ngb = banks[6]
            for h in range(4):
                dma(kTh, FK[h], ["FK"], ["kTh"], "bk")
                dma(qTh, FQ[h], ["FQ"], ["qTh"], "bq")
                dma(vaug[:, :, 0:128], FV[h].rearrange("(j p) d -> p j d", p=128), ["FV"], ["vaug"], "bv")
                dma(cq, CT[h].partition_broadcast(128), ["CT"], ["cq"], "bc")
                dma(ctr[0:NT, :], CT[h].rearrange("(j p) -> j p", p=128), ["CT"], ["ctr"], "br")
                P.op("pe", lambda e: e.transpose(out=ngb[:, 0:NT], in_=ctr[0:NT, :], identity=identf[0:NT, 0:NT]), ["ctr", "identf"], ["pb6"])
                P.op("act", lambda e: e.mul(out=negc, in_=ngb[:, 0:NT], mul=-1.0), ["pb6"], ["negc"])
                P.op("pool", lambda e: e.tensor_tensor(out=cqd.rearrange("p (j t) -> p j t", t=128), in0=cq.rearrange("p (j t) -> p j t", t=128),
                                                       in1=dmask.unsqueeze(1).to_broadcast([128, NT, 128]), op=ALU.add), ["cq", "dmask"], ["cqd"])
                steps = [(I, j) for I in range(NST) for j in range(4 * I + 4)]
                gstep = [0]

                def geom(I, j):
                    r = j - 4 * I
                    tq0 = 128 * max(r, 0)
                    return r, tq0, 512 - tq0, I * 512 + tq0

                def emit_qk(n):
                    I, j = steps[n]
                    r, tq0, N, c0 = geom(I, j)
                    sb_ = (gbase + n) % 3
                    sT = sTb[sb_]
                    P.op("pe", lambda e: e.matmul(sT[:, 0:N], lhsT=kTh[:, j * 128:(j + 1) * 128], rhs=qTh[:, c0:c0 + N], start=True, stop=True),
                         ["kTh", "qTh"], [BN[id(sT)]])

                def emit_soft(n):
                    I, j = steps[n]
                    r, tq0, N, c0 = geom(I, j)
                    sb_ = (gbase + n) % 3
                    sT = sTb[sb_]
                    bn = BN[id(sT)]
                    if r < 0:
                        P.op("dve", lambda e: e.tensor_tensor(out=tmp[sb_][:, 0:N], in0=sT[:, 0:N], in1=cq[:, c0:c0 + N], op=ALU.add),
                             [bn, "cq"], ["tmp%d" % sb_])
                    else:
                        P.op("dve", lambda e: e.tensor_tensor(out=tmp[sb_][:, 0:128], in0=sT[:, 0:128], in1=cqd[:, c0:c0 + 128], op=ALU.add),
                             [bn, "cqd"], ["tmp%d" % sb_])
                        if N > 128:
                            P.op("dve", lambda e: e.tensor_tensor(out=tmp[sb_][:, 128:N], in0=sT[:, 128:N], in1=cq[:, c0 + 128:c0 + N], op=ALU.add),
                                 [bn, "cq"], ["tmp%d" % sb_])
                    P.op("act", lambda e: e.activation(out=pT[sb_][:, 0:N], in_=tmp[sb_][:, 0:N], func=AF.Exp, bias=negc[:, j:j + 1]),
                         ["tmp%d" % sb_, "negc"], ["pT%d" % sb_])

                def emit_pv(n):
                    I, j = steps[n]
                    r, tq0, N, c0 = geom(I, j)
                    sb_ = (gbase + n) % 3
                    if j == 0:
                        for qq in range(4):
                            r0 = (I * 4 + qq) * 128
                            dma(fzg[:, qq, :], FZ[r0:r0 + 128, h * 128:(h + 1) * 128], ["FZ"], ["fzg"], "fzld")
                    for qq in range(max(r, 0), 4):
                        o0 = qq * 128 - tq0
                        P.op("pe", lambda e, qq=qq, o0=o0: e.matmul(accb[qq][:, 0:129], lhsT=pT[sb_][:, o0:o0 + 128], rhs=vaug[:, j, 0:129],
                                                                  start=(j == 0), stop=(j == 4 * I + qq)),
                             ["pT%d" % sb_, "vaug"], ["pb%d" % (2 + qq)])
                    if j == 4 * I + 3:
                        for qq in range(4):
                            r0 = (I * 4 + qq) * 128
                            P.op("dve", lambda e, qq=qq: e.reciprocal(out=rden[:, qq:qq + 1], in_=accb[qq][:, 128:129]), ["pb%d" % (2 + qq)], ["rden%d" % qq])
                            P.op("dve", lambda e, qq=qq: e.scalar_tensor_tensor(out=gfb[:, qq, :], in0=accb[qq][:, 0:128], scalar=rden[:, qq:qq + 1], in1=fzg[:, qq, :], op0=ALU.mult, op1=ALU.mult),
                                 ["pb%d" % (2 + qq), "rden%d" % qq, "fzg"], ["gfb%d" % qq])
                            dma(GH[r0:r0 + 128, 512 + h * 128:512 + (h + 1) * 128], gfb[:, qq, :], ["gfb%d" % qq], ["GH"], "gfst%d" % qq)

                gbase = h * len(steps)
                NS = len(steps)
                LOOK = 2
                for n in range(min(LOOK, NS)):
                    emit_qk(n)
                for n in range(NS):
                    emit_soft(n)
                    if n + LOOK < NS:
                        emit_qk(n + LOOK)
                    emit_pv(n)
            P.barrier()

        def phase_c(G_d, wo_d, res_d, dst_d, layer):
            A.reset()
            wst = [A.alloc([D], F32), A.alloc([D], F32)]
            wo = A.alloc([8, D], BF16)
            load_weights(wo_d, D, wo, "wo", wst)
            gt = [A.alloc([D], BF16), A.alloc([D], BF16)]
            gT = [A.alloc([8, 128], BF16), A.alloc([8, 128], BF16)]
            xr = [A.alloc([D], F32), A.alloc([D], F32)]
            z = [A.alloc([D], F32), A.alloc([D], F32)]
            st6 = A.alloc([2, 6], F32)
            mv = A.alloc([2], F32)
            rs = A.alloc([1], F32)
            nmr = A.alloc([1], F32)
            tpb = [banks[0], banks[1]]
            yb = [[banks[2], banks[3]], [banks[4], banks[5]]]
            g_ = A.alloc([D], F32)
            b_ = A.alloc([D], F32)
            dma(g_, lng_d[:, layer * D:(layer + 1) * D], [], ["lng"], "cst")
            dma(b_, lnb_d[:, layer * D:(layer + 1) * D], [], ["lnb"], "cst")
            def loads(i):
                s = i % 2
                r0 = i * 128
                dma(gt[s], G_d[r0:r0 + 128, :], [], ["gt%d" % s], "cg%d" % s)
                dma(xr[s], res_d[r0:r0 + 128, :], [], ["xr%d" % s], "cx%d" % s)

            loads(0)
            for i in range(NT):
                s = i % 2
                r0 = i * 128
                if i + 1 < NT:
                    loads(i + 1)
                tv = tpb[s][:, 0:512].bitcast(BF16).rearrange("p (k t) -> p k t", k=8)
                for k in range(8):
                    P.op("pe", lambda e, k=k, s=s, tv=tv: e.transpose(out=tv[:, k, :], in_=gt[s][:, k * 128:(k + 1) * 128], identity=identb),
                         ["gt%d" % s, "identb"], ["pb%d" % s])
                P.op("act", lambda e, s=s, tv=tv: e.copy(out=gT[s], in_=tv), ["pb%d" % s], ["gT%d" % s])
                for c in range(2):
                    for k in range(8):
                        P.op("pe", lambda e, k=k, c=c, s=s: e.matmul(yb[s][c][:, :], lhsT=gT[s][:, k, :], rhs=wo[:, k, c * 512:(c + 1) * 512], start=(k == 0), stop=(k == 7)),
                             ["gT%d" % s, "wo"], ["pb%d" % (2 + 2 * s + c)])
                    P.op("dve", lambda e, c=c, s=s: e.scalar_tensor_tensor(out=z[s][:, c * 512:(c + 1) * 512], in0=xr[s][:, c * 512:(c + 1) * 512], scalar=ALPHA,
                                                                             in1=yb[s][c][:, :], op0=ALU.mult, op1=ALU.add),
                         ["xr%d" % s, "pb%d" % (2 + 2 * s + c)], ["z%d" % s])
                    P.op("dve", lambda e, c=c, s=s: e.bn_stats(out=st6[:, c, :], in_=z[s][:, c * 512:(c + 1) * 512]), ["z%d" % s], ["st6"])
                P.op("dve", lambda e: e.bn_aggr(out=mv, in_=st6.rearrange("p a b -> p (a b)")), ["st6"], ["mv"])
                P.op("dve", lambda e: e.tensor_scalar(out=rs, in0=mv[:, 1:2], scalar1=LN_EPS, scalar2=None, op0=ALU.add), ["mv"], ["rs"])
                P.op("act", lambda e: e.activation(out=rs, in_=rs, func=AF.Sqrt), ["rs"], ["rs"])
                P.op("dve", lambda e: e.reciprocal(out=rs, in_=rs), ["rs"], ["rs"])
                P.op("dve", lambda e: e.scalar_tensor_tensor(out=nmr, in0=mv[:, 0:1], scalar=-1.0, in1=rs, op0=ALU.mult, op1=ALU.mult), ["mv", "rs"], ["nmr"])
                P.op("act", lambda e, s=s: e.activation(out=z[s], in_=z[s], func=AF.Identity, scale=rs, bias=nmr), ["z%d" % s, "rs", "nmr"], ["z%d" % s])
                P.op("pool", lambda e, s=s: e.tensor_tensor(out=z[s], in0=z[s], in1=g_, op=ALU.mult), ["z%d" % s, "lng"], ["z%d" % s])
                P.op("pool", lambda e, s=s: e.tensor_tensor(out=z[s], in0=z[s], in1=b_, op=ALU.add), ["z%d" % s, "lnb"], ["z%d" % s])
                dma(dst_d[r0:r0 + 128, :], z[s], ["z%d" % s], ["DST"], "co%d" % s, eng="pool")
            P.barrier()


        def phase_a1():
            A.reset()
            big1 = A.alloc([6144], F32)
            wst = [big1[:, 0:3072], big1[:, 3072:6144]]
            wf = A.alloc([8, 1032], BF16)
            wt = A.alloc([8, 3072], BF16)
            load_weights(wf1_d, 1032, wf, "wf", wst)
            load_weights(wt1_d, 3072, wt, "wt", wst)
            P.barrier()
            so = big1[:, 0:2048].rearrange("p (a b) -> p a b", a=2)
            gz1 = big1[:, 2048:4096].rearrange("p (a b) -> p a b", a=2)
            ho = big1[:, 4096:5120]
            sil = big1[:, 5120:5632]
            cacc = big1[:, 5632:6144]
            xin = A.alloc([2, D], F32)
            xb = A.alloc([2, D], BF16)
            xT = A.alloc([8, 512], BF16)
            ng1 = A.alloc([D], F32)
            cw = A.alloc([32], F32)
            cb = A.alloc([8], F32)
            bi = A.alloc([1], F32, parts=4)
            bfm = A.alloc([1], F32, parts=4)
            dma(ng1, ng1_d, [], ["ng1"], "cst")
            dma(cw, cw_d, [], ["cw"], "cst")
            dma(cb, cb_d, [], ["cb"], "cst")
            dma(bi, bi_d, [], ["bi"], "cst")
            dma(bfm, bfm_d, [], ["bfm"], "cst")
            uq = A.alloc([8, 515], F32)
            sq = A.alloc([512], F32)
            qk = A.alloc([8, 512], BF16)
            vf = A.alloc([2, D], BF16)
            ig = A.alloc([512], F32, parts=4)
            lg = A.alloc([512], F32, parts=4)
            Bc = A.alloc([512], F32, parts=4)
            aa = A.alloc([512], F32, parts=4)
            RR = A.alloc([512], F32, parts=4)
            Rpe = A.alloc([4], F32, parts=4)
            ea = A.alloc([512], F32, parts=4)
            er = A.alloc([512], F32, parts=4)
            th = A.alloc([512], F32, parts=4)
            cB = A.alloc([1], F32, parts=4)
            cR = A.alloc([1], F32, parts=4)
            gtok = A.alloc([4, 12], F32)
            dRb = A.alloc([16], F32)
            dRl = A.alloc([4], F32)
            PTm = [A.alloc([128], BF16) for _ in range(4)]
            ktk = [A.alloc([128], BF16) for _ in range(4)]
            vau = [A.alloc([258], BF16) for _ in range(4)]
            Cst = A.alloc([4, 257], F32)
            Cbf = A.alloc([4, 258], BF16)
            junk = A.alloc([256], F32)
            ssq = A.alloc([4], F32)
            rstd = A.alloc([4], F32)
            t1 = A.alloc([4], F32)
            rsv = A.alloc([4], F32)
            g1b = A.alloc([D], BF16)
            P.op("pool", lambda e: e.memset(Cst, 0.0), [], ["Cst0", "Cst1", "Cst2", "Cst3"])
            P.op("pool", lambda e: e.memset(Cbf, 0.0), [], ["Cbf0", "Cbf1", "Cbf2", "Cbf3"])
            P.op("pool", lambda e: e.memset(uq, 0.0), [], ["uq%d" % b for b in range(8)])
            P.op("pool", lambda e: e.memset(dRl, 1.0), [], ["dRl"])
            P.op("pool", lambda e: e.memset(cB, 0.0), [], ["cB"])
            P.op("pool", lambda e: e.memset(cR, 0.0), [], ["cR"])
            tpb = [banks[0]]
            prj = [banks[1], banks[2]]
            fmb = prj
            tmb = prj
            mA = banks[0]
            HB4 = [banks[3], banks[4], banks[5], banks[6]]
            UB = banks[7]
            SC = 128 ** -0.5
            pji = [0]

            class _Ctr:
                pass
            def a1_tile(t0, q):
                if True:
                    r0 = t0 + q * 128
                    qs = slice(q * 128, (q + 1) * 128)
                    v2 = q % 2
                    for c in range(6):
                        bk = prj[pji[0] % len(prj)]
                        bn = BN[id(bk)]
                        pji[0] += 1
                        tm_mm(bk, bn, q, wt, "wt", c * 512, 512, xT)
                        cs = slice((c % 2) * 512, (c % 2) * 512 + 512)
                        if c < 2:
                            P.op("act", lambda e, v2=v2, bk=bk, cs=cs: e.copy(out=vf[:, v2, cs], in_=bk[:, :]), [bn], ["vf%d" % v2])
                        elif c < 4:
                            P.op("act", lambda e, v2=v2, bk=bk, cs=cs: e.activation(out=so[:, v2, cs], in_=bk[:, :], func=AF.Sigmoid), [bn], ["so%d" % v2])
                        else:
                            P.op("act", lambda e, bk=bk: e.activation(out=sil, in_=bk[:, :], func=AF.Silu), [bn], ["sil"])
                            P.op("dve", lambda e, v2=v2, cs=cs: e.tensor_tensor(out=gz1[:, v2, cs], in0=sil, in1=ng1[:, cs], op=ALU.mult), ["sil", "ng1"], ["gz1_%d" % v2])
                    for pair in ((0, 1, 2, 3),):
                        for h in pair:
                            bA = HB4[h]
                            nA = BN[id(bA)]
                            kT_ps = bA[:, 128:192].bitcast(BF16)
                            P.op("pe", lambda e, h=h, bA=bA: e.matmul(bA[:, 0:128], lhsT=qk[:, 4 + h, qs], rhs=qk[:, h, qs], start=True, stop=True),
                                 ["qk%d" % h, "qk%d" % (4 + h)], [nA])
                            P.op("pe", lambda e, h=h, kT_ps=kT_ps: e.transpose(out=kT_ps, in_=qk[:, 4 + h, qs], identity=identb), ["qk%d" % (4 + h), "identb"], [nA])
                        for h in pair:
                            bA = HB4[h]
                            nA = BN[id(bA)]
                            kT_ps = bA[:, 128:192].bitcast(BF16)
                            hv = slice(h * 256, (h + 1) * 256)
                            P.op("dve", lambda e, h=h, bA=bA: e.tensor_tensor(out=PTm[h], in0=bA[:, 0:128], in1=triu, op=ALU.mult), [nA, "triu"], ["PTm%d" % h])
                            P.op("act", lambda e, h=h, kT_ps=kT_ps: e.copy(out=ktk[h], in_=kT_ps), [nA], ["ktk%d" % h])
                            P.op("pool", lambda e, h=h, hv=hv: e.tensor_scalar(out=vau[h][:, 0:256], in0=vf[:, v2, hv], scalar1=gtok[:, q, h:h + 1], scalar2=1.0, op0=ALU.mult, op1=ALU.mult),
                                 ["vf%d" % v2, "gtok"], ["vau%d" % h])
                            P.op("pool", lambda e, h=h: e.tensor_copy(out=vau[h][:, 256:257], in_=gtok[:, q, h:h + 1]), ["gtok", "vau%d" % h], ["vau%d" % h])
                        for h in pair:
                            bA = HB4[h]
                            bB = UB
                            nA = BN[id(bA)]
                            nB = BN[id(bB)]
                            s2 = h
                            P.op("pe", lambda e, s2=s2, bA=bA: e.matmul(bA[:, 192:449], lhsT=PTm[s2], rhs=vau[s2][:, 0:257], start=True, stop=False), ["PTm%d" % s2, "vau%d" % s2], [nA])
                            P.op("pe", lambda e, h=h, bA=bA: e.matmul(bA[:, 192:449], lhsT=qk[:, h, qs], rhs=Cbf[:, h, 0:257], start=False, stop=True), ["qk%d" % h, "Cbf%d" % h], [nA])
                            P.op("pe", lambda e, s2=s2, bB=bB: e.matmul(bB[:, 0:257], lhsT=ktk[s2], rhs=vau[s2][:, 0:257], start=True, stop=True), ["ktk%d" % s2, "vau%d" % s2], [nB])
                            dprev = dRl[:, h:h + 1] if q == 0 else dRb[:, 4 * h + q - 1:4 * h + q]
                            P.op("dve", lambda e, h=h, bB=bB, dprev=dprev: e.scalar_tensor_tensor(out=Cst[:, h, :], in0=Cst[:, h, :], scalar=dprev, in1=bB[:, 0:257], op0=ALU.mult, op1=ALU.add),
                                 ["Cst%d" % h, nB, "dRb", "dRl"], ["Cst%d" % h])
                        for h in pair:
                            bA = HB4[h]
                            bB = UB
                            nA = BN[id(bA)]
                            nB = BN[id(bB)]
                            hv = slice(h * 256, (h + 1) * 256)
                            P.op("act", lambda e, h=h: e.activation(out=Cbf[:, h, 0:257], in_=Cst[:, h, :], func=AF.Copy, scale=dRb[:, 4 * h + q:4 * h + q + 1]),
                                 ["Cst%d" % h, "dRb"], ["Cbf%d" % h])
                            P.op("dve", lambda e, h=h, bA=bA: e.tensor_scalar(out=t1[:, h:h + 1], in0=bA[:, 448:449], scalar1=gtok[:, q, 4 + h:5 + h], scalar2=None, op0=ALU.mult),
                                 [nA, "gtok"], ["t1_%d" % h])
                            P.op("dve", lambda e, h=h: e.scalar_tensor_tensor(out=t1[:, h:h + 1], in0=t1[:, h:h + 1], scalar=-1.0, in1=t1[:, h:h + 1], op0=ALU.mult, op1=ALU.max),
                                 ["t1_%d" % h], ["t1_%d" % h])
                            P.op("dve", lambda e, h=h: e.tensor_tensor(out=t1[:, h:h + 1], in0=t1[:, h:h + 1], in1=gtok[:, q, 8 + h:9 + h], op=ALU.max), ["t1_%d" % h, "gtok"], ["t1_%d" % h])
                            P.op("dve", lambda e, h=h: e.reciprocal(out=t1[:, h:h + 1], in_=t1[:, h:h + 1]), ["t1_%d" % h], ["t1_%d" % h])
                            P.op("dve", lambda e, h=h: e.tensor_tensor(out=rsv[:, h:h + 1], in0=t1[:, h:h + 1], in1=gtok[:, q, 4 + h:5 + h], op=ALU.mult), ["t1_%d" % h, "gtok"], ["rsv%d" % h])
                            P.op("dve", lambda e, h=h, hv=hv, bA=bA: e.scalar_tensor_tensor(out=ho[:, hv], in0=bA[:, 192:448], scalar=rsv[:, h:h + 1], in1=so[:, v2, hv], op0=ALU.mult, op1=ALU.mult),
                                 [nA, "rsv%d" % h, "so%d" % v2], ["ho%d" % h])
                            P.op("act", lambda e, h=h, hv=hv: e.activation(out=junk, in_=ho[:, hv], func=AF.Square, accum_out=ssq[:, h:h + 1]), ["ho%d" % h], ["junk", "ssq"])
                    P.op("dve", lambda e: e.tensor_scalar(out=rstd, in0=ssq, scalar1=1.0 / 256, scalar2=RMS_EPS, op0=ALU.mult, op1=ALU.add), ["ssq"], ["rstd"])
                    P.op("act", lambda e: e.activation(out=rstd, in_=rstd, func=AF.Sqrt), ["rstd"], ["rstd"])
                    P.op("dve", lambda e: e.reciprocal(out=rstd, in_=rstd), ["rstd"], ["rstd"])
                    for h in range(4):
                        hv = slice(h * 256, (h + 1) * 256)
                        P.op("dve", lambda e, h=h, hv=hv: e.scalar_tensor_tensor(out=g1b[:, hv], in0=ho[:, hv], scalar=rstd[:, h:h + 1], in1=gz1[:, v2, hv], op0=ALU.mult, op1=ALU.mult),
                             ["ho%d" % h, "rstd", "gz1_%d" % v2], ["g1b"])
                    dma(G1[r0:r0 + 128, :], g1b, ["g1b"], ["G1"], "g1st")
            fmi = 0
            tmi = 0
            for st in range(NST):
                front(H1, st, xin, xb, xT, tpb)
                t0 = st * 512
                for gi in range(2):
                    bk = prj[pji[0] % len(prj)]
                    bn = BN[id(bk)]
                    pji[0] += 1
                    fm_mm(bk, bn, 4, wf, "wf", 1024 + 4 * gi, xT)
                    if gi == 0:
                        P.op("act", lambda e, bk=bk: e.activation(out=ig, in_=bk[0:4, :], func=AF.Identity, bias=bi), [bn, "bi"], ["ig"])
                    else:
                        P.op("act", lambda e, bk=bk: e.activation(out=lg, in_=bk[0:4, :], func=AF.Sigmoid, bias=bfm), [bn, "bfm"], ["lg"])
                        P.op("act", lambda e: e.activation(out=lg, in_=lg, func=AF.Ln), ["lg"], ["lg"])
                P.op("dve", lambda e: e.tensor_tensor_scan(out=Bc, data0=ones[0:4, :], data1=lg, initial=cB, op0=ALU.mult, op1=ALU.add), ["lg", "ones", "cB"], ["Bc"])
                P.op("dve", lambda e: e.tensor_tensor(out=aa, in0=ig, in1=Bc, op=ALU.subtract), ["ig", "Bc"], ["aa"])
                P.op("pool", lambda e: e.tensor_copy(out=Rpe[:, 0:1], in_=cR), ["cR"], ["Rpe"])
                P.op("dve", lambda e: e.tensor_tensor_scan(out=RR, data0=aa, data1=aa, initial=cR, op0=ALU.max, op1=ALU.max), ["aa", "cR"], ["RR"])
                P.op("pool", lambda e: e.tensor_copy(out=Rpe[:, 1:4], in_=RR[:, 127:384:128]), ["RR"], ["Rpe"])
                P.op("pool", lambda e: e.tensor_copy(out=cB, in_=Bc[:, 511:512]), ["Bc"], ["cB"])
                P.op("pool", lambda e: e.tensor_copy(out=cR, in_=RR[:, 511:512]), ["RR", "Rpe"], ["cR"])
                rpb = Rpe.unsqueeze(2).to_broadcast([4, 4, 128])
                v3 = lambda a: a.rearrange("p (q t) -> p q t", q=4)
                P.op("dve", lambda e: e.tensor_tensor(out=v3(ea), in0=v3(aa), in1=rpb, op=ALU.subtract), ["aa", "Rpe"], ["ea"])
                P.op("act", lambda e: e.activation(out=ea, in_=ea, func=AF.Exp), ["ea"], ["ea"])
                P.op("dve", lambda e: e.tensor_tensor(out=v3(er), in0=v3(RR), in1=rpb, op=ALU.subtract), ["RR", "Rpe"], ["er"])
                P.op("act", lambda e: e.activation(out=er, in_=er, func=AF.Exp, scale=-1.0), ["er"], ["er"])
                P.op("dve", lambda e: e.tensor_tensor(out=th, in0=Bc, in1=RR, op=ALU.add), ["Bc", "RR"], ["th"])
                P.op("act", lambda e: e.activation(out=th, in_=th, func=AF.Exp, scale=-1.0), ["th"], ["th"])
                gtp = mA[:, 448:496].rearrange("p (q c) -> p q c", q=4)
                for q in range(4):
                    for gi, (src, sn) in enumerate(((ea, "ea"), (er, "er"), (th, "th"))):
                        P.op("pe", lambda e, q=q, gi=gi, src=src: e.transpose(out=gtp[:, q, gi * 4:(gi + 1) * 4], in_=src[:, q * 128:(q + 1) * 128], identity=identf[0:4, 0:4]),
                             [sn, "identf"], ["pb0"])
                P.op("act", lambda e: e.copy(out=gtok, in_=gtp), ["pb0"], ["gtok"])
                for h in range(4):
                    P.op("pe", lambda e, h=h: e.matmul(mA[:, 496 + 4 * h:500 + 4 * h], lhsT=sel[:, h * 128:(h + 1) * 128], rhs=er[:, 127:512:128], start=True, stop=True),
                         ["sel", "er"], ["pb0"])
                if st > 0:
                    P.op("pool", lambda e: e.tensor_copy(out=dRl, in_=dRb[:, 3:16:4]), ["dRb"], ["dRl"])
                P.op("act", lambda e: e.copy(out=dRb, in_=mA[:, 496:512]), ["pb0"], ["dRb"])
                for b in range(8):
                    bk = prj[pji[0] % len(prj)]
                    bn = BN[id(bk)]
                    pji[0] += 1
                    fm_mm(bk, bn, 128, wf, "wf", b * 128, xT)
                    P.op("act", lambda e, b=b, bk=bk: e.copy(out=uq[:, b, 3:515], in_=bk[:, :]), [bn], ["uq%d" % b])
                    P.op("dve", lambda e, b=b: e.tensor_scalar(out=cacc, in0=uq[:, b, 0:512], scalar1=cw[:, 4 * b:4 * b + 1], scalar2=cb[:, b:b + 1], op0=ALU.mult, op1=ALU.add),
                         ["uq%d" % b, "cw", "cb"], ["cacc"])
                    for j in range(1, 4):
                        P.op("dve", lambda e, b=b, j=j: e.scalar_tensor_tensor(out=cacc, in0=uq[:, b, j:j + 512], scalar=cw[:, 4 * b + j:4 * b + j + 1], in1=cacc, op0=ALU.mult, op1=ALU.add),
                             ["uq%d" % b, "cw", "cacc"], ["cacc"])
                    P.op("pool", lambda e, b=b: e.tensor_copy(out=uq[:, b, 0:3], in_=uq[:, b, 512:515]), ["uq%d" % b], ["uq%d" % b])
                    if b < 4:
                        P.op("act", lambda e: e.activation(out=sq, in_=cacc, func=AF.Silu), ["cacc"], ["sq"])
                        P.op("act", lambda e, b=b: e.mul(out=qk[:, b, :], in_=sq, mul=SC), ["sq"], ["qk%d" % b])
                    else:
                        P.op("act", lambda e, b=b: e.activation(out=qk[:, b, :], in_=cacc, func=AF.Silu), ["cacc"], ["qk%d" % b])
                for q in range(4):
                    a1_tile(t0, q)

            P.barrier()

        if "A0" in phases:
            phase_a0()
        if "B0" in phases:
            phase_b0()
        if "C0" in phases:
            phase_c(GH, wo0_d, x_d, H1, 0)
        if "A1" in phases:
            phase_a1()
        if "C1" in phases:
            phase_c(G1, wo1_d, H1, out_d, 1)

        P.finish(["DST", "GH", "G1", "FQ", "FK", "FV", "FZ", "CT"])
        P.emit()
        build.stats = P.stats()
    return nc


def host_consts():
    s = np.arange(128)[:, None]
    t = np.arange(128)[None, :]
    triu = (s <= t).astype(np.float32)
    dmask = np.where(s <= t, 0.0, -1.0e9).astype(np.float32)
    rmask = np.ones((128, 512), np.float32)
    rmask[:, 0::128] = 0.0
    sel = np.zeros((4, 512), np.float32)
    for h in range(4):
        sel[h, h * 128:(h + 1) * 128] = 1.0
    return {"ident": np.eye(128, dtype=np.float32), "triu": triu, "dmask": dmask, "rmask": rmask, "sel": sel}


def host_params(hgrn_lb_logits, ab_w_in, ab_fox_bf, ab_hgrn_norm_g, ab_w_out, c_w_in,
                c_conv_w, c_conv_b, c_bi, c_bf, c_norm_g, c_w_out, ln_g, ln_b):
    f = np.float32
    w0 = np.asarray(ab_w_in[0], f)
    sl = lambda a, b: w0[:, a:b]
    wf0 = np.concatenate([sl(0, 512), sl(512, 1024), sl(2048, 2560), sl(2560, 3072), sl(3584, 3588)], axis=1)
    wt0 = np.concatenate([sl(1024, 1536), sl(1536, 2048), sl(3072, 3584), sl(3588, 4100)], axis=1)
    w1 = np.asarray(c_w_in[0], f)
    wf1 = np.concatenate([w1[:, 0:512], w1[:, 512:1024], w1[:, 2048:2052], w1[:, 2052:2056]], axis=1)
    wt1 = np.concatenate([w1[:, 1024:2048], w1[:, 2056:3080], w1[:, 3080:4104]], axis=1)
    lbl = np.asarray(hgrn_lb_logits, f).reshape(3, 4, 128).transpose(2, 1, 0).reshape(128, 12)
    rep = lambda v: np.ascontiguousarray(np.broadcast_to(np.asarray(v, f).reshape(1, -1), (128, np.asarray(v).size)))
    cw = np.asarray(c_conv_w[0], f).reshape(4, 8, 128).transpose(2, 1, 0).reshape(128, 32)
    cb = np.asarray(c_conv_b[0], f).reshape(8, 128).T
    d = {
        "wf0": wf0, "wt0": wt0, "wo0": np.asarray(ab_w_out[0], f),
        "wf1": wf1, "wt1": wt1, "wo1": np.asarray(c_w_out[0], f),
        "lbl": lbl, "fbf": np.asarray(ab_fox_bf[0], f).reshape(4, 1),
        "hng": rep(ab_hgrn_norm_g[0]), "lng": rep(np.asarray(ln_g, f).reshape(-1)), "lnb": rep(np.asarray(ln_b, f).reshape(-1)),
        "cw": cw, "cb": cb, "bi": np.asarray(c_bi[0], f).reshape(4, 1), "bfm": np.asarray(c_bf[0], f).reshape(4, 1),
        "ng1": rep(c_norm_g[0]),
    }
    d.update(host_consts())
    return {k: np.ascontiguousarray(v, dtype=f) for k, v in d.items()}


_NC_CACHE = {}


def kernel(x, hgrn_lb_logits, ab_w_in, ab_fox_bf, ab_hgrn_norm_g, ab_w_out, c_w_in,
           c_conv_w, c_conv_b, c_bi, c_bf, c_norm_g, c_w_out, ln_g, ln_b):
    x = np.asarray(x, np.float32)
    B, T, _ = x.shape
    params = host_params(hgrn_lb_logits, ab_w_in, ab_fox_bf, ab_hgrn_norm_g, ab_w_out, c_w_in,
                         c_conv_w, c_conv_b, c_bi, c_bf, c_norm_g, c_w_out, ln_g, ln_b)
    if T not in _NC_CACHE:
        _NC_CACHE[T] = build(T)
    nc = _NC_CACHE[T]
    owner = {2 * b: b for b in range(B)}
    zx = np.zeros((T, D), np.float32)
    in_maps = []
    for c in range(8):
        m = dict(params)
        m["x"] = np.ascontiguousarray(x[owner[c]]) if c in owner else zx
        in_maps.append(m)
    res = run_bass_kernel_spmd(nc, in_maps, core_ids=list(range(8)))
    return np.stack([np.asarray(res.results[2 * b]["out"], np.float32) for b in range(B)], axis=0)
```

```python
import numpy as np
from contextlib import ExitStack
import concourse.bass as bass
import concourse.mybir as mybir
from concourse.bass_utils import run_bass_kernel_spmd

F32 = mybir.dt.float32
BF16 = mybir.dt.bfloat16
AF = mybir.ActivationFunctionType
ALU = mybir.AluOpType

D = 1024
ALPHA = 4.0 ** 0.25
LN_EPS = 1e-5
RMS_EPS = 1e-6


class _Op:
    __slots__ = ("fn", "waits", "inc", "dma", "val")

    def __init__(self, fn, waits, dma):
        self.fn = fn
        self.waits = waits
        self.inc = False
        self.dma = dma
        self.val = 0


class Prog:
    ENGS = ("pe", "act", "dve", "pool", "sp")

    def __init__(self, nc, es):
        self.nc = nc
        self.es = es
        self.ops = {e: [] for e in self.ENGS}
        self.lastw = {}
        self.readers = {}
        self.known = {e: {} for e in self.ENGS}
        self.dma_count = {}

    def op(self, eng, fn, reads=(), writes=(), dma=None):
        writes = list(writes) + [r for r in reads if r.startswith("pb") and r not in writes]
        reads = [r for r in reads if not r.startswith("pb")]
        if dma is not None:
            cnt = self.dma_count.get(dma, 0) + 1
            self.dma_count[dma] = cnt
            me = ("dma:" + dma, cnt)
        else:
            me = (eng, len(self.ops[eng]))
        deps = []
        for r in reads:
            w = self.lastw.get(r)
            if w is not None:
                deps.append(w)
        for r in writes:
            w = self.lastw.get(r)
            if w is not None and not (w[0] == me[0] == "pe"):
                deps.append(w)
            for rn, ri in self.readers.get(r, {}).items():
                if rn != me[0] or dma is not None:
                    deps.append((rn, ri))
        kn = self.known[eng]
        wm = {}
        for name, idx in deps:
            if name.startswith("dma:"):
                idx = self.dma_count[name[4:]] - (1 if (dma is not None and name[4:] == dma) else 0)
            if kn.get(name, -1) >= idx:
                continue
            kn[name] = idx
            wm[name] = max(wm.get(name, -1), idx)
            if not name.startswith("dma:"):
                self.ops[name][idx].inc = True
        self.ops[eng].append(_Op(fn, list(wm.items()), dma))
        for r in reads:
            self.readers.setdefault(r, {})[me[0]] = me[1]
        for r in writes:
            self.lastw[r] = me
            self.readers[r] = {}
        return me

    def finish(self, resources):
        self.op("sp", None, reads=list(resources))

    def barrier(self):
        lasts = {}
        for e in self.ENGS:
            for i in range(len(self.ops[e]) - 1, -1, -1):
                if self.ops[e][i].fn is not None and self.ops[e][i].dma is None:
                    lasts[e] = i
                    break
        for e in self.ENGS:
            waits = []
            kn = self.known[e]
            for s_, idx in lasts.items():
                if s_ != e and kn.get(s_, -1) < idx:
                    waits.append((s_, idx))
                    kn[s_] = idx
                    self.ops[s_][idx].inc = True
            for k, cnt in self.dma_count.items():
                if kn.get("dma:" + k, -1) < cnt:
                    waits.append(("dma:" + k, cnt))
                    kn["dma:" + k] = cnt
            self.ops[e].append(_Op(None, waits, None))

    SEM_LIM = 4000

    def emit(self):
        nc = self.nc
        ops = self.ops
        LIM = self.SEM_LIM
        nep = {}
        for e in self.ENGS:
            c = 0
            for o in ops[e]:
                if o.inc:
                    c += 1
                o.val = c
            nep[e] = (c + LIM - 1) // LIM + 1
        sems = {e: [self.es.enter_context(nc.semaphore("s_%s%d" % (e, i))) for i in range(nep[e])] for e in self.ENGS}
        dsems = {k: self.es.enter_context(nc.semaphore("d_" + k)) for k in self.dma_count}
        for k, cnt in self.dma_count.items():
            assert 16 * cnt < 2 * LIM, (k, cnt)

        def run(ename, engine):
            for o in ops[ename]:
                for name, idx in o.waits:
                    if name.startswith("dma:"):
                        engine.wait_ge(dsems[name[4:]], 16 * idx)
                    else:
                        tgt = ops[name][idx]
                        assert tgt.inc and tgt.val > 0
                        engine.wait_ge(sems[name][(tgt.val - 1) // LIM], (tgt.val - 1) % LIM + 1)
                if o.fn is None:
                    continue
                ins = o.fn(engine)
                if o.dma is not None:
                    ins.then_inc(dsems[o.dma], 16)
                elif o.inc:
                    ins.then_inc(sems[ename][(o.val - 1) // LIM], 1)

        with nc.Block() as block:
            @block.tensor
            def _(e):
                run("pe", e)

            @block.scalar
            def _(e):
                run("act", e)

            @block.vector
            def _(e):
                run("dve", e)

            @block.gpsimd
            def _(e):
                run("pool", e)

            @block.sync
            def _(e):
                run("sp", e)

    def stats(self):
        return {e: (len(self.ops[e]), sum(1 for o in self.ops[e] if o.inc)) for e in self.ENGS}


class Arena:
    def __init__(self, t, nwords):
        self.t = t
        self.n = nwords
        self.off = 0
        self.base = 0

    def alloc(self, free_shape, dt, parts=128):
        n = int(np.prod(free_shape))
        words = n if dt == F32 else (n + 1) // 2
        words = (words + 7) // 8 * 8
        assert self.off + words <= self.n, ("SBUF arena overflow", self.off, words, self.n)
        ap = self.t[0:parts, self.off:self.off + words]
        self.off += words
        if dt != F32:
            ap = ap.bitcast(dt)
        ap = ap[:, 0:n]
        if len(free_shape) == 2:
            ap = ap.rearrange("p (a b) -> p a b", a=free_shape[0])
        elif len(free_shape) == 3:
            ap = ap.rearrange("p (a b c) -> p a b c", a=free_shape[0], b=free_shape[1])
        return ap

    def mark(self):
        self.base = self.off

    def reset(self):
        self.off = self.base


def build(T, dbg=False, phases=("A0", "B0", "C0", "A1", "C1")):
    NT = T // 128
    NST = T // 512
    nc = bass.Bass("TRN2", target_bir_lowering=False)
    din = lambda name, shape, dt=F32: nc.dram_tensor(name, shape, dt, kind="ExternalInput").ap()
    dsc = lambda name, shape, dt: nc.dram_tensor(name, shape, dt, kind="ExternalOutput" if dbg else "Internal").ap()
    x_d = din("x", [T, D])
    wf0_d = din("wf0", [D, 2052])
    wt0_d = din("wt0", [D, 2048])
    wo0_d = din("wo0", [D, D])
    wf1_d = din("wf1", [D, 1032])
    wt1_d = din("wt1", [D, 3072])
    wo1_d = din("wo1", [D, D])
    lbl_d = din("lbl", [128, 12])
    fbf_d = din("fbf", [4, 1])
    hng_d = din("hng", [128, 512])
    lng_d = din("lng", [128, 2 * D])
    lnb_d = din("lnb", [128, 2 * D])
    cw_d = din("cw", [128, 32])
    cb_d = din("cb", [128, 8])
    bi_d = din("bi", [4, 1])
    bfm_d = din("bfm", [4, 1])
    ng1_d = din("ng1", [128, D])
    ident_d = din("ident", [128, 128])
    triu_d = din("triu", [128, 128])
    dmask_d = din("dmask", [128, 128])
    rmask_d = din("rmask", [128, 512])
    sel_d = din("sel", [4, 512])
    out_d = nc.dram_tensor("out", [T, D], F32, kind="ExternalOutput").ap()
    GH = dsc("GH", [T, D], BF16)
    FQ = dsc("FQ", [4, 128, T], BF16)
    FK = dsc("FK", [4, 128, T], BF16)
    FV = dsc("FV", [4, T, 128], BF16)
    FZ = dsc("FZ", [T, 512], BF16)
    CT = dsc("CT", [4, T], F32)
    H1 = dsc("H1", [T, D], F32)
    G1 = dsc("G1", [T, D], BF16)

    with ExitStack() as es:
        P = Prog(nc, es)
        AW = 50000
        arena_t = es.enter_context(nc.sbuf_tensor("arena", [128, AW], F32))
        A = Arena(arena_t, AW)
        banks = [es.enter_context(nc.psum_tensor("pb%d" % i, [128, 512], F32)) for i in range(8)]
        BN = {id(b): "pb%d" % i for i, b in enumerate(banks)}

        ucnt = [0]

        def dma(out, in_, reads, writes, key, eng="sp"):
            if key == "cst":
                key = "cst%d" % ucnt[0]
                ucnt[0] += 1
            P.op(eng, lambda e: e.dma_start(out=out, in_=in_), reads=reads, writes=writes, dma=key)

        identf = A.alloc([128], F32)
        identb = A.alloc([128], BF16)
        triu = A.alloc([128], F32)
        dmask = A.alloc([128], F32)
        rmask = A.alloc([512], F32)
        ones = A.alloc([512], F32)
        sel = A.alloc([512], F32, parts=4)
        dma(identf, ident_d, [], ["identf"], "cst")
        dma(triu, triu_d, [], ["triu"], "cst")
        dma(dmask, dmask_d, [], ["dmask"], "cst")
        dma(rmask, rmask_d, [], ["rmask"], "cst")
        dma(sel, sel_d, [], ["sel"], "cst")
        P.op("pool", lambda e: e.tensor_copy(out=identb, in_=identf), ["identf"], ["identb"])
        P.op("pool", lambda e: e.memset(ones, 1.0), [], ["ones"])
        mhalf = A.alloc([4], F32)
        P.op("pool", lambda e: e.memset(mhalf, -0.5), [], ["mhalf"])
        A.mark()

        def load_weights(w_d, ncols, wbf, name, wst):
            for k in range(8):
                s = k % 2
                dma(wst[s][:, 0:ncols], w_d[k * 128:(k + 1) * 128, :], [], ["wst%d" % s], "wld%d" % s)
                eng = ("dve", "act", "pool")[k % 3]
                if eng == "act":
                    P.op("act", lambda e, k=k, s=s: e.copy(out=wbf[:, k, :], in_=wst[s][:, 0:ncols]), ["wst%d" % s], [name])
                else:
                    P.op(eng, lambda e, k=k, s=s: e.tensor_copy(out=wbf[:, k, :], in_=wst[s][:, 0:ncols]), ["wst%d" % s], [name])

        def front(src_d, st, xin, xb, xT, tpb):
            for q in range(4):
                r0 = (st * 4 + q) * 128
                s_ = q % 2
                dma(xin[:, s_, :], src_d[r0:r0 + 128, :], [], ["xin%d" % s_], "xin%d" % s_)
                P.op("pool", lambda e, s_=s_: e.tensor_copy(out=xb[:, s_, :], in_=xin[:, s_, :]), ["xin%d" % s_], ["xb%d" % s_])
                tb = tpb[q % len(tpb)]
                tn = BN[id(tb)]
                tv = tb[:, 0:512].bitcast(BF16).rearrange("p (k t) -> p k t", k=8)
                for k in range(8):
                    P.op("pe", lambda e, s_=s_, k=k, tv=tv: e.transpose(out=tv[:, k, :], in_=xb[:, s_, k * 128:(k + 1) * 128], identity=identb),
                         ["xb%d" % s_, "identb"], [tn])
                P.op("act", lambda e, q=q, tv=tv: e.copy(out=xT[:, :, q * 128:(q + 1) * 128], in_=tv), [tn], ["xT"])

        def fm_mm(bank, bname, M, wbf, wname, c0, xT):
            for k in range(8):
                P.op("pe", lambda e, k=k: e.matmul(bank[0:M, :], lhsT=wbf[:, k, c0:c0 + M], rhs=xT[:, k, :], start=(k == 0), stop=(k == 7)),
                     [wname, "xT"], [bname])

        def tm_mm(bank, bname, q, wbf, wname, c0, n, xT):
            for k in range(8):
                P.op("pe", lambda e, k=k: e.matmul(bank[:, 0:n], lhsT=xT[:, k, q * 128:(q + 1) * 128], rhs=wbf[:, k, c0:c0 + n], start=(k == 0), stop=(k == 7)),
                     [wname, "xT"], [bname])

        def phase_a0():
            A.reset()
            big = A.alloc([4112], F32)
            wst = [big[:, 0:2052], big[:, 2056:2056 + 2052]]
            hqT = big[:, 0:2048].rearrange("p (a b) -> p a b", a=4)
            sg = big[:, 2048:4096].rearrange("p (a b) -> p a b", a=4)
            wf = A.alloc([8, 2052], BF16)
            wt = A.alloc([8, 2048], BF16)
            load_weights(wf0_d, 2052, wf, "wf", wst)
            load_weights(wt0_d, 2048, wt, "wt", wst)
            P.barrier()
            xin = A.alloc([2, D], F32)
            xb = A.alloc([2, D], BF16)
            xT = A.alloc([8, 512], BF16)
            hng = A.alloc([512], F32)
            lbl = A.alloc([12], F32)
            lbe = A.alloc([12], F32)
            lb = A.alloc([4], F32)
            oml = A.alloc([4], F32)
            lbm1 = A.alloc([4], F32)
            fbf = A.alloc([1], F32, parts=4)
            dma(hng, hng_d, [], ["hng"], "cst")
            dma(lbl, lbl_d, [], ["lbl"], "cst")
            dma(fbf, fbf_d, [], ["fbf"], "cst")
            P.op("act", lambda e: e.activation(out=lbe, in_=lbl, func=AF.Exp), ["lbl"], ["lbe"])
            lbe3 = lbe.rearrange("p (h r) -> p h r", h=4)
            P.op("dve", lambda e: e.tensor_tensor(out=lb, in0=lbe3[:, :, 0], in1=lbe3[:, :, 1], op=ALU.add), ["lbe"], ["lbs"])
            P.op("dve", lambda e: e.tensor_tensor(out=lb, in0=lb, in1=lbe3[:, :, 2], op=ALU.add), ["lbe", "lbs"], ["lbs"])
            P.op("dve", lambda e: e.reciprocal(out=lb, in_=lb), ["lbs"], ["lbs"])
            P.op("dve", lambda e: e.tensor_tensor(out=lb, in0=lb, in1=lbe3[:, :, 0], op=ALU.mult), ["lbs", "lbe"], ["lb", "lbs"])
            P.op("dve", lambda e: e.tensor_scalar(out=oml, in0=lb, scalar1=-1.0, scalar2=1.0, op0=ALU.mult, op1=ALU.add), ["lb"], ["oml"])
            P.op("dve", lambda e: e.tensor_scalar(out=lbm1, in0=lb, scalar1=-1.0, scalar2=None, op0=ALU.add), ["lb"], ["lbm1"])

            vb = A.alloc([2, 512], BF16)
            gz = A.alloc([2, 512], F32)
            sil = A.alloc([512], F32)
            fvb = A.alloc([2, 512], BF16)
            fzb = A.alloc([2, 512], BF16)
            gl = A.alloc([4, 512], F32)
            bb = A.alloc([4, 512], F32)
            qdec = A.alloc([4, 512], BF16)
            kinv = A.alloc([4, 512], BF16)
            kend = A.alloc([4, 512], BF16)
            fqb = A.alloc([4, 512], BF16)
            fkb = A.alloc([4, 512], BF16)
            dd = A.alloc([4, 4], F32)
            d1 = A.alloc([4, 4], F32)
            d2 = A.alloc([4, 4], F32)
            d3 = A.alloc([4, 4], F32)
            ffs = A.alloc([512], F32, parts=4)
            ls = A.alloc([512], F32, parts=4)
            cT = A.alloc([512], F32, parts=4)
            ccar = A.alloc([1], F32, parts=4)
            S = A.alloc([4, 128], F32)
            Sbf = A.alloc([4, 128], BF16)
            aTm = [A.alloc([128], BF16), A.alloc([128], BF16)]
            kit = [A.alloc([128], BF16), A.alloc([128], BF16)]
            tmpU = [A.alloc([128], F32), A.alloc([128], F32)]
            junk = A.alloc([128], F32)
            ssq = A.alloc([4], F32)
            rstd = A.alloc([4], F32)
            ghb = A.alloc([512], BF16)
            P.op("pool", lambda e: e.memset(S, 0.0), [], ["S0", "S1", "S2", "S3"])
            tpb = [banks[0]]
            prj = [banks[1], banks[2], banks[3]]
            pji = [0]
            HB = [banks[4], banks[5], banks[6], banks[7]]
            aTm4 = [A.alloc([128], BF16) for _ in range(4)]
            kit4 = [A.alloc([128], BF16) for _ in range(4)]
            tmpU4 = [A.alloc([128], F32) for _ in range(4)]
            junk4 = [A.alloc([128], F32) for _ in range(4)]

            def a0_tile(t0, q):
                r0 = t0 + q * 128
                qs = slice(q * 128, (q + 1) * 128)
                v2 = q % 2
                for c in range(4):
                    bk = prj[pji[0] % 3]
                    bn = BN[id(bk)]
                    pji[0] += 1
                    tm_mm(bk, bn, q, wt, "wt", c * 512, 512, xT)
                    if c == 0:
                        P.op("act", lambda e, bk=bk: e.copy(out=vb[:, v2, :], in_=bk[:, :]), [bn], ["vb%d" % v2])
                    elif c == 1:
                        P.op("act", lambda e, bk=bk: e.activation(out=sil, in_=bk[:, :], func=AF.Silu), [bn], ["sil"])
                        P.op("dve", lambda e: e.tensor_tensor(out=gz[:, v2, :], in0=sil, in1=hng, op=ALU.mult), ["sil", "hng"], ["gz%d" % v2])
                    elif c == 2:
                        P.op("act", lambda e, bk=bk: e.copy(out=fvb[:, v2, :], in_=bk[:, :]), [bn], ["fvb%d" % v2])
                        dma(FV[:, r0:r0 + 128, :].rearrange("h p d -> p h d"), fvb[:, v2, :].rearrange("p (h d) -> p h d", h=4),
                            ["fvb%d" % v2], ["FV"], "fvst%d" % v2)
                    else:
                        P.op("act", lambda e, bk=bk: e.activation(out=fzb[:, v2, :], in_=bk[:, :], func=AF.Silu), [bn], ["fzb%d" % v2])
                        dma(FZ[r0:r0 + 128, :], fzb[:, v2, :], ["fzb%d" % v2], ["FZ"], "fzst%d" % v2)
                hs_ = [slice(h * 128, (h + 1) * 128) for h in range(4)]
                for h in range(4):
                    hb = HB[h]
                    nb = BN[id(hb)]
                    kT_ps = hb[:, 384:448].bitcast(BF16)
                    P.op("pe", lambda e, h=h, hb=hb: e.matmul(hb[:, 0:128], lhsT=kinv[:, h, qs], rhs=qdec[:, h, qs], start=True, stop=True),
                         ["kinv%d" % h, "qdec%d" % h], [nb])
                    P.op("pe", lambda e, h=h, kT_ps=kT_ps: e.transpose(out=kT_ps, in_=kend[:, h, qs], identity=identb), ["kend%d" % h, "identb"], [nb])
                for h in range(4):
                    hb = HB[h]
                    nb = BN[id(hb)]
                    kT_ps = hb[:, 384:448].bitcast(BF16)
                    P.op("dve", lambda e, h=h, hb=hb: e.tensor_tensor(out=aTm4[h], in0=hb[:, 0:128], in1=triu, op=ALU.mult), [nb, "triu"], ["aTm%d" % h])
                    P.op("act", lambda e, h=h, kT_ps=kT_ps: e.copy(out=kit4[h], in_=kT_ps), [nb], ["kit%d" % h])
                    P.op("dve", lambda e, h=h: e.tensor_scalar(out=Sbf[:, h, :], in0=S[:, h, :], scalar1=d3[:, h, q:q + 1], scalar2=None, op0=ALU.mult),
                         ["S%d" % h, "d3_%d" % h], ["Sbf%d" % h])
                for h in range(4):
                    hb = HB[h]
                    nb = BN[id(hb)]
                    P.op("pe", lambda e, h=h, hb=hb: e.matmul(hb[:, 256:384], lhsT=aTm4[h], rhs=vb[:, v2, hs_[h]], start=True, stop=False),
                         ["aTm%d" % h, "vb%d" % v2], [nb])
                    P.op("pe", lambda e, h=h, hb=hb: e.matmul(hb[:, 256:384], lhsT=qdec[:, h, qs], rhs=Sbf[:, h, :], start=False, stop=True),
                         ["qdec%d" % h, "Sbf%d" % h], [nb])
                    P.op("pe", lambda e, h=h, hb=hb: e.matmul(hb[:, 128:256], lhsT=kit4[h], rhs=vb[:, v2, hs_[h]], start=True, stop=True),
                         ["kit%d" % h, "vb%d" % v2], [nb])
                for h in range(4):
                    hb = HB[h]
                    nb = BN[id(hb)]
                    P.op("dve", lambda e, h=h, hb=hb: e.scalar_tensor_tensor(out=S[:, h, :], in0=S[:, h, :], scalar=d1[:, h, q:q + 1], in1=hb[:, 128:256], op0=ALU.mult, op1=ALU.add),
                         ["S%d" % h, "d1_%d" % h, nb], ["S%d" % h])
                    P.op("act", lambda e, h=h, hb=hb: e.activation(out=junk4[h], in_=hb[:, 256:384], func=AF.Square, accum_out=ssq[:, h:h + 1]),
                         [nb], ["junk%d" % h, "ssq"])
                P.op("dve", lambda e: e.tensor_scalar(out=rstd, in0=ssq, scalar1=1.0 / 128, scalar2=RMS_EPS, op0=ALU.mult, op1=ALU.add), ["ssq"], ["rstd"])
                P.op("pool", lambda e: e.tensor_tensor(out=rstd, in0=rstd, in1=mhalf, op=ALU.pow), ["rstd", "mhalf"], ["rstd"])
                for h in range(4):
                    hb = HB[h]
                    nb = BN[id(hb)]
                    P.op("dve", lambda e, h=h, hb=hb: e.scalar_tensor_tensor(out=ghb[:, hs_[h]], in0=hb[:, 256:384], scalar=rstd[:, h:h + 1], in1=gz[:, v2, hs_[h]], op0=ALU.mult, op1=ALU.mult),
                         [nb, "rstd", "gz%d" % v2], ["ghb"])
                dma(GH[r0:r0 + 128, 0:512], ghb, ["ghb"], ["GH"], "ghst")

            SC = 128 ** -0.5
            fmi = 0
            tmi = 0
            for st in range(NST):
                front(x_d, st, xin, xb, xT, tpb)
                t0 = st * 512
                for h in range(4):
                    for slot in range(4):
                        bk = prj[pji[0] % 3]
                        bn = BN[id(bk)]
                        pji[0] += 1
                        fm_mm(bk, bn, 128, wf, "wf", slot * 512 + h * 128, xT)
                        if slot == 0:
                            P.op("act", lambda e, h=h, bk=bk: e.copy(out=hqT[:, h, :], in_=bk[:, :]), [bn], ["hqT%d" % h])
                        elif slot == 1:
                            P.op("act", lambda e, h=h, bk=bk: e.activation(out=sg[:, h, :], in_=bk[:, :], func=AF.Sigmoid), [bn], ["sg%d" % h])
                            P.op("act", lambda e, h=h: e.activation(out=gl[:, h, :], in_=sg[:, h, :], func=AF.Ln, scale=oml[:, h:h + 1], bias=lb[:, h:h + 1]),
                                 ["sg%d" % h, "oml", "lb"], ["gl%d" % h])
                            P.op("dve", lambda e, h=h: e.tensor_scalar(out=sg[:, h, :], in0=sg[:, h, :], scalar1=-1.0, scalar2=lbm1[:, h:h + 1], op0=ALU.add, op1=ALU.mult),
                                 ["sg%d" % h, "lbm1"], ["sg%d" % h])
                            P.op("dve", lambda e, h=h: e.tensor_tensor_scan(out=bb[:, h, :], data0=rmask, data1=gl[:, h, :], initial=0.0, op0=ALU.mult, op1=ALU.add),
                                 ["gl%d" % h, "rmask"], ["bb%d" % h])
                            bmid = bb[:, h, 63:512:128]
                            blast = bb[:, h, 127:512:128]
                            P.op("act", lambda e, h=h, bmid=bmid: e.activation(out=d3[:, h, :], in_=bmid, func=AF.Exp), ["bb%d" % h], ["d3_%d" % h])
                            P.op("act", lambda e, h=h, blast=blast: e.activation(out=d1[:, h, :], in_=blast, func=AF.Exp), ["bb%d" % h], ["d1_%d" % h])
                            P.op("dve", lambda e, h=h, bmid=bmid, blast=blast: e.tensor_tensor(out=dd[:, h, :], in0=blast, in1=bmid, op=ALU.subtract), ["bb%d" % h], ["dd%d" % h])
                            P.op("act", lambda e, h=h: e.activation(out=d2[:, h, :], in_=dd[:, h, :], func=AF.Exp), ["dd%d" % h], ["d2_%d" % h])
                            P.op("pool", lambda e, h=h, bmid=bmid: e.tensor_copy(out=dd[:, h, :], in_=bmid), ["bb%d" % h, "d2_%d" % h], ["dd%d" % h])
                            bb3 = bb[:, h, :].rearrange("p (q t) -> p q t", q=4)
                            P.op("dve", lambda e, h=h, bb3=bb3: e.tensor_tensor(out=bb3, in0=bb3, in1=dd[:, h, :].unsqueeze(2).to_broadcast([128, 4, 128]), op=ALU.subtract),
                                 ["bb%d" % h, "d3_%d" % h, "d1_%d" % h, "dd%d" % h], ["bb%d" % h])
                            P.op("act", lambda e, h=h: e.activation(out=gl[:, h, :], in_=bb[:, h, :], func=AF.Exp), ["bb%d" % h, "gl%d" % h], ["gl%d" % h])
                            P.op("act", lambda e, h=h: e.activation(out=bb[:, h, :], in_=bb[:, h, :], func=AF.Exp, scale=-1.0), ["bb%d" % h], ["bb%d" % h])
                            P.op("dve", lambda e, h=h: e.tensor_tensor(out=qdec[:, h, :], in0=hqT[:, h, :], in1=gl[:, h, :], op=ALU.mult),
                                 ["hqT%d" % h, "gl%d" % h], ["qdec%d" % h])
                            P.op("dve", lambda e, h=h: e.tensor_tensor(out=kinv[:, h, :], in0=sg[:, h, :], in1=bb[:, h, :], op=ALU.mult),
                                 ["sg%d" % h, "bb%d" % h], ["kinv%d" % h])
                            P.op("dve", lambda e, h=h: e.tensor_tensor(out=kend[:, h, :].rearrange("p (q t) -> p q t", q=4), in0=kinv[:, h, :].rearrange("p (q t) -> p q t", q=4),
                                                                     in1=d2[:, h, :].unsqueeze(2).to_broadcast([128, 4, 128]), op=ALU.mult),
                                 ["kinv%d" % h, "d2_%d" % h], ["kend%d" % h])
                        elif slot == 2:
                            P.op("act", lambda e, h=h, bk=bk: e.mul(out=fqb[:, h, :], in_=bk[:, :], mul=SC), [bn], ["fqb%d" % h])
                            dma(FQ[h, :, t0:t0 + 512], fqb[:, h, :], ["fqb%d" % h], ["FQ"], "fqst%d" % h)
                        else:
                            P.op("act", lambda e, h=h, bk=bk: e.copy(out=fkb[:, h, :], in_=bk[:, :]), [bn], ["fkb%d" % h])
                            dma(FK[h, :, t0:t0 + 512], fkb[:, h, :], ["fkb%d" % h], ["FK"], "fkst%d" % h)
                bk = prj[pji[0] % 3]
                bn = BN[id(bk)]
                pji[0] += 1
                fm_mm(bk, bn, 4, wf, "wf", 2048, xT)
                P.op("act", lambda e, bk=bk: e.activation(out=ffs, in_=bk[0:4, :], func=AF.Sigmoid, bias=fbf), [bn, "fbf"], ["ffs"])
                P.op("act", lambda e: e.activation(out=ls, in_=ffs, func=AF.Ln), ["ffs"], ["ls"])
                if st == 0:
                    P.op("dve", lambda e: e.tensor_tensor_scan(out=cT, data0=ones[0:4, :], data1=ls, initial=0.0, op0=ALU.mult, op1=ALU.add), ["ls", "ones"], ["cT"])
                else:
                    P.op("dve", lambda e: e.tensor_tensor_scan(out=cT, data0=ones[0:4, :], data1=ls, initial=ccar, op0=ALU.mult, op1=ALU.add), ["ls", "ones", "ccar"], ["cT"])
                P.op("pool", lambda e: e.tensor_copy(out=ccar, in_=cT[:, 511:512]), ["cT"], ["ccar"])
                dma(CT[:, t0:t0 + 512], cT, ["cT"], ["CT"], "ctst")
                for q in range(4):
                    a0_tile(t0, q)
            P.barrier()

        def phase_b0():
            A.reset()
            kTh = A.alloc([T], BF16)
            qTh = A.alloc([T], BF16)
            vaug = A.alloc([NT, 130], BF16)
            cq = A.alloc([T], F32)
            cqd = A.alloc([T], F32)
            ctr = A.alloc([128], F32)
            negc = A.alloc([NT], F32)
            fzg = A.alloc([4, 128], BF16)
            tmp = [A.alloc([512], F32) for _ in range(3)]
            pT = [A.alloc([512], BF16) for _ in range(3)]
            rden = A.alloc([4], F32)
            gfb = A.alloc([4, 128], BF16)
            P.op("pool", lambda e: e.memset(vaug, 1.0), [], ["vaug"])
            sTb = [banks[0], banks[1], banks[7]]
            accb = [banks[2], banks[3], banks[4], banks[5]]
            ngb = banks[6]
            for h in range(4):
                dma(kTh, FK[h], ["FK"], ["kTh"], "bk")
                dma(qTh, FQ[h], ["FQ"], ["qTh"], "bq")
                dma(vaug[:, :, 0:128], FV[h].rearrange("(j p) d -> p j d", p=128), ["FV"], ["vaug"], "bv")
                dma(cq, CT[h].partition_broadcast(128), ["CT"], ["cq"], "bc")
                dma(ctr[0:NT, :], CT[h].rearrange("(j p) -> j p", p=128), ["CT"], ["ctr"], "br")
                P.op("pe", lambda e: e.transpose(out=ngb[:, 0:NT], in_=ctr[0:NT, :], identity=identf[0:NT, 0:NT]), ["ctr", "identf"], ["pb6"])
                P.op("act", lambda e: e.mul(out=negc, in_=ngb[:, 0:NT], mul=-1.0), ["pb6"], ["negc"])
                P.op("pool", lambda e: e.tensor_tensor(out=cqd.rearrange("p (j t) -> p j t", t=128), in0=cq.rearrange("p (j t) -> p j t", t=128),
                                                       in1=dmask.unsqueeze(1).to_broadcast([128, NT, 128]), op=ALU.add), ["cq", "dmask"], ["cqd"])
                steps = [(I, j) for I in range(NST) for j in range(4 * I + 4)]
                gstep = [0]

                def geom(I, j):
                    r = j - 4 * I
                    tq0 = 128 * max(r, 0)
                    return r, tq0, 512 - tq0, I * 512 + tq0

                def emit_qk(n):
                    I, j = steps[n]
                    r, tq0, N, c0 = geom(I, j)
                    sb_ = (gbase + n) % 3
                    sT = sTb[sb_]
                    P.op("pe", lambda e: e.matmul(sT[:, 0:N], lhsT=kTh[:, j * 128:(j + 1) * 128], rhs=qTh[:, c0:c0 + N], start=True, stop=True),
                         ["kTh", "qTh"], [BN[id(sT)]])

                def emit_soft(n):
                    I, j = steps[n]
                    r, tq0, N, c0 = geom(I, j)
                    sb_ = (gbase + n) % 3
                    sT = sTb[sb_]
                    bn = BN[id(sT)]
                    if r < 0:
                        P.op("dve", lambda e: e.tensor_tensor(out=tmp[sb_][:, 0:N], in0=sT[:, 0:N], in1=cq[:, c0:c0 + N], op=ALU.add),
                             [bn, "cq"], ["tmp%d" % sb_])
                    else:
                        P.op("dve", lambda e: e.tensor_tensor(out=tmp[sb_][:, 0:128], in0=sT[:, 0:128], in1=cqd[:, c0:c0 + 128], op=ALU.add),
                             [bn, "cqd"], ["tmp%d" % sb_])
                        if N > 128:
                            P.op("dve", lambda e: e.tensor_tensor(out=tmp[sb_][:, 128:N], in0=sT[:, 128:N], in1=cq[:, c0 + 128:c0 + N], op=ALU.add),
                                 [bn, "cq"], ["tmp%d" % sb_])
                    P.op("act", lambda e: e.activation(out=pT[sb_][:, 0:N], in_=tmp[sb_][:, 0:N], func=AF.Exp, bias=negc[:, j:j + 1]),
                         ["tmp%d" % sb_, "negc"], ["pT%d" % sb_])

                def emit_pv(n):
                    I, j = steps[n]
                    r, tq0, N, c0 = geom(I, j)
                    sb_ = (gbase + n) % 3
                    if j == 0:
                        for qq in range(4):
                            r0 = (I * 4 + qq) * 128
                            dma(fzg[:, qq, :], FZ[r0:r0 + 128, h * 128:(h + 1) * 128], ["FZ"], ["fzg"], "fzld")
                    for qq in range(max(r, 0), 4):
                        o0 = qq * 128 - tq0
                        P.op("pe", lambda e, qq=qq, o0=o0: e.matmul(accb[qq][:, 0:129], lhsT=pT[sb_][:, o0:o0 + 128], rhs=vaug[:, j, 0:129],
                                                                  start=(j == 0), stop=(j == 4 * I + qq)),
                             ["pT%d" % sb_, "vaug"], ["pb%d" % (2 + qq)])
                    if j == 4 * I + 3:
                        for qq in range(4):
                            r0 = (I * 4 + qq) * 128
                            P.op("dve", lambda e, qq=qq: e.reciprocal(out=rden[:, qq:qq + 1], in_=accb[qq][:, 128:129]), ["pb%d" % (2 + qq)], ["rden%d" % qq])
                            P.op("dve", lambda e, qq=qq: e.scalar_tensor_tensor(out=gfb[:, qq, :], in0=accb[qq][:, 0:128], scalar=rden[:, qq:qq + 1], in1=fzg[:, qq, :], op0=ALU.mult, op1=ALU.mult),
                                 ["pb%d" % (2 + qq), "rden%d" % qq, "fzg"], ["gfb%d" % qq])
                            dma(GH[r0:r0 + 128, 512 + h * 128:512 + (h + 1) * 128], gfb[:, qq, :], ["gfb%d" % qq], ["GH"], "gfst%d" % qq)

                gbase = h * len(steps)
                NS = len(steps)
                LOOK = 2
                for n in range(min(LOOK, NS)):
                    emit_qk(n)
                for n in range(NS):
                    emit_soft(n)
                    if n + LOOK < NS:
                        emit_qk(n + LOOK)
                    emit_pv(n)
            P.barrier()

        def phase_c(G_d, wo_d, res_d, dst_d, layer):
            A.reset()
            wst = [A.alloc([D], F32), A.alloc([D], F32)]
            wo = A.alloc([8, D], BF16)
            load_weights(wo_d, D, wo, "wo", wst)
            gt = [A.alloc([D], BF16), A.alloc([D], BF16)]
            gT = [A.alloc([8, 128], BF16), A.alloc([8, 128], BF16)]
            xr = [A.alloc([D], F32), A.alloc([D], F32)]
            z = [A.alloc([D], F32), A.alloc([D], F32)]
            st6 = A.alloc([2, 6], F32)
            mv = A.alloc([2], F32)
            rs = A.alloc([1], F32)
            nmr = A.alloc([1], F32)
            tpb = [banks[0], banks[1]]
            yb = [[banks[2], banks[3]], [banks[4], banks[5]]]
            g_ = A.alloc([D], F32)
            b_ = A.alloc([D], F32)
            dma(g_, lng_d[:, layer * D:(layer + 1) * D], [], ["lng"], "cst")
            dma(b_, lnb_d[:, layer * D:(layer + 1) * D], [], ["lnb"], "cst")
            def loads(i):
                s = i % 2
                r0 = i * 128
                dma(gt[s], G_d[r0:r0 + 128, :], [], ["gt%d" % s], "cg%d" % s)
                dma(xr[s], res_d[r0:r0 + 128, :], [], ["xr%d" % s], "cx%d" % s)

            loads(0)
            for i in range(NT):
                s = i % 2
                r0 = i * 128
                if i + 1 < NT:
                    loads(i + 1)
                tv = tpb[s][:, 0:512].bitcast(BF16).rearrange("p (k t) -> p k t", k=8)
                for k in range(8):
                    P.op("pe", lambda e, k=k, s=s, tv=tv: e.transpose(out=tv[:, k, :], in_=gt[s][:, k * 128:(k + 1) * 128], identity=identb),
                         ["gt%d" % s, "identb"], ["pb%d" % s])
                P.op("act", lambda e, s=s, tv=tv: e.copy(out=gT[s], in_=tv), ["pb%d" % s], ["gT%d" % s])
                for c in range(2):
                    for k in range(8):
                        P.op("pe", lambda e, k=k, c=c, s=s: e.matmul(yb[s][c][:, :], lhsT=gT[s][:, k, :], rhs=wo[:, k, c * 512:(c + 1) * 512], start=(k == 0), stop=(k == 7)),
                             ["gT%d" % s, "wo"], ["pb%d" % (2 + 2 * s + c)])
                    P.op("dve", lambda e, c=c, s=s: e.scalar_tensor_tensor(out=z[s][:, c * 512:(c + 1) * 512], in0=xr[s][:, c * 512:(c + 1) * 512], scalar=ALPHA,
                                                                             in1=yb[s][c][:, :], op0=ALU.mult, op1=ALU.add),
                         ["xr%d" % s, "pb%d" % (2 + 2 * s + c)], ["z%d" % s])
                    P.op("dve", lambda e, c=c, s=s: e.bn_stats(out=st6[:, c, :], in_=z[s][:, c * 512:(c + 1) * 512]), ["z%d" % s], ["st6"])
                P.op("dve", lambda e: e.bn_aggr(out=mv, in_=st6.rearrange("p a b -> p (a b)")), ["st6"], ["mv"])
                P.op("dve", lambda e: e.tensor_scalar(out=rs, in0=mv[:, 1:2], scalar1=LN_EPS, scalar2=None, op0=ALU.add), ["mv"], ["rs"])
                P.op("act", lambda e: e.activation(out=rs, in_=rs, func=AF.Sqrt), ["rs"], ["rs"])
                P.op("dve", lambda e: e.reciprocal(out=rs, in_=rs), ["rs"], ["rs"])
                P.op("dve", lambda e: e.scalar_tensor_tensor(out=nmr, in0=mv[:, 0:1], scalar=-1.0, in1=rs, op0=ALU.mult, op1=ALU.mult), ["mv", "rs"], ["nmr"])
                P.op("act", lambda e, s=s: e.activation(out=z[s], in_=z[s], func=AF.Identity, scale=rs, bias=nmr), ["z%d" % s, "rs", "nmr"], ["z%d" % s])
                P.op("pool", lambda e, s=s: e.tensor_tensor(out=z[s], in0=z[s], in1=g_, op=ALU.mult), ["z%d" % s, "lng"], ["z%d" % s])
                P.op("pool", lambda e, s=s: e.tensor_tensor(out=z[s], in0=z[s], in1=b_, op=ALU.add), ["z%d" % s, "lnb"], ["z%d" % s])
                dma(dst_d[r0:r0 + 128, :], z[s], ["z%d" % s], ["DST"], "co%d" % s, eng="pool")
            P.barrier()


        def phase_a1():
            A.reset()
            big1 = A.alloc([6144], F32)
            wst = [big1[:, 0:3072], big1[:, 3072:6144]]
            wf = A.alloc([8, 1032], BF16)
            wt = A.alloc([8, 3072], BF16)
            load_weights(wf1_d, 1032, wf, "wf", wst)
            load_weights(wt1_d, 3072, wt, "wt", wst)
            P.barrier()
            so = big1[:, 0:2048].rearrange("p (a b) -> p a b", a=2)
            gz1 = big1[:, 2048:4096].rearrange("p (a b) -> p a b", a=2)
            ho = big1[:, 4096:5120]
            sil = big1[:, 5120:5632]
            cacc2 = [big1[:, 5632:6144], A.alloc([512], F32)]
            xin = A.alloc([2, D], F32)
            xb = A.alloc([2, D], BF16)
            xT = A.alloc([8, 512], BF16)
            ng1 = A.alloc([D], F32)
            cw = A.alloc([32], F32)
            cb = A.alloc([8], F32)
            bi = A.alloc([1], F32, parts=4)
            bfm = A.alloc([1], F32, parts=4)
            dma(ng1, ng1_d, [], ["ng1"], "cst")
            dma(cw, cw_d, [], ["cw"], "cst")
            dma(cb, cb_d, [], ["cb"], "cst")
            dma(bi, bi_d, [], ["bi"], "cst")
            dma(bfm, bfm_d, [], ["bfm"], "cst")
            uq = A.alloc([8, 515], F32)
            sq2 = [A.alloc([512], F32), A.alloc([512], F32)]
            qk = A.alloc([8, 512], BF16)
            vf = A.alloc([2, D], BF16)
            ig = A.alloc([512], F32, parts=4)
            lg = A.alloc([512], F32, parts=4)
            Bc = A.alloc([512], F32, parts=4)
            aa = A.alloc([512], F32, parts=4)
            RR = A.alloc([512], F32, parts=4)
            Rpe = A.alloc([4], F32, parts=4)
            ea = A.alloc([512], F32, parts=4)
            er = A.alloc([512], F32, parts=4)
            th = A.alloc([512], F32, parts=4)
            cB = A.alloc([1], F32, parts=4)
            cR = A.alloc([1], F32, parts=4)
            gtok = A.alloc([4, 12], F32)
            dRb = A.alloc([16], F32)
            dRl = A.alloc([4], F32)
            PTm = [A.alloc([128], BF16) for _ in range(4)]
            ktk = [A.alloc([128], BF16) for _ in range(4)]
            vau = [A.alloc([258], BF16) for _ in range(4)]
            Cst = A.alloc([4, 257], F32)
            Cbf = A.alloc([4, 258], BF16)
            junk = A.alloc([256], F32)
            ssq = A.alloc([4], F32)
            rstd = A.alloc([4], F32)
            t1 = A.alloc([4], F32)
            rsv = A.alloc([4], F32)
            g1b = A.alloc([D], BF16)
            P.op("pool", lambda e: e.memset(Cst, 0.0), [], ["Cst0", "Cst1", "Cst2", "Cst3"])
            P.op("pool", lambda e: e.memset(Cbf, 0.0), [], ["Cbf0", "Cbf1", "Cbf2", "Cbf3"])
            P.op("pool", lambda e: e.memset(uq, 0.0), [], ["uq%d" % b for b in range(8)])
            P.op("pool", lambda e: e.memset(dRl, 1.0), [], ["dRl"])
            P.op("pool", lambda e: e.memset(cB, 0.0), [], ["cB"])
            P.op("pool", lambda e: e.memset(cR, 0.0), [], ["cR"])
            tpb = [banks[0]]
            prj = [banks[1], banks[2]]
            fmb = prj
            tmb = prj
            mA = banks[0]
            HB4 = [banks[3], banks[4], banks[5], banks[6]]
            UB = banks[7]
            SC = 128 ** -0.5
            pji = [0]

            class _Ctr:
                pass
            def conv_block(b):
                bk = prj[pji[0] % len(prj)]
                bn = BN[id(bk)]
                pji[0] += 1
                cacc = cacc2[b % 2]
                sq = sq2[b % 2]
                cn = "cacc%d" % (b % 2)
                sn_ = "sq%d" % (b % 2)
                fm_mm(bk, bn, 128, wf, "wf", b * 128, xT)
                P.op("act", lambda e: e.copy(out=uq[:, b, 3:515], in_=bk[:, :]), [bn], ["uq%d" % b])
                P.op("dve", lambda e: e.tensor_scalar(out=cacc, in0=uq[:, b, 0:512], scalar1=cw[:, 4 * b:4 * b + 1], scalar2=cb[:, b:b + 1], op0=ALU.mult, op1=ALU.add),
                     ["uq%d" % b, "cw", "cb"], [cn])
                for j in range(1, 4):
                    P.op("dve", lambda e, j=j: e.scalar_tensor_tensor(out=cacc, in0=uq[:, b, j:j + 512], scalar=cw[:, 4 * b + j:4 * b + j + 1], in1=cacc, op0=ALU.mult, op1=ALU.add),
                         ["uq%d" % b, "cw", cn], [cn])
                P.op("pool", lambda e: e.tensor_copy(out=uq[:, b, 0:3], in_=uq[:, b, 512:515]), ["uq%d" % b], ["uq%d" % b])
                if b < 4:
                    P.op("act", lambda e: e.activation(out=sq, in_=cacc, func=AF.Silu), [cn], [sn_])
                    P.op("act", lambda e: e.mul(out=qk[:, b, :], in_=sq, mul=SC), [sn_], ["qk%d" % b])
                else:
                    P.op("act", lambda e: e.activation(out=qk[:, b, :], in_=cacc, func=AF.Silu), [cn], ["qk%d" % b])

            def a1_tile(t0, q):
                if True:
                    r0 = t0 + q * 128
                    qs = slice(q * 128, (q + 1) * 128)
                    v2 = q % 2
                    for c in range(6):
                        bk = prj[pji[0] % len(prj)]
                        bn = BN[id(bk)]
                        pji[0] += 1
                        tm_mm(bk, bn, q, wt, "wt", c * 512, 512, xT)
                        cs = slice((c % 2) * 512, (c % 2) * 512 + 512)
                        if c < 2:
                            P.op("act", lambda e, v2=v2, bk=bk, cs=cs: e.copy(out=vf[:, v2, cs], in_=bk[:, :]), [bn], ["vf%d" % v2])
                        elif c < 4:
                            P.op("act", lambda e, v2=v2, bk=bk, cs=cs: e.activation(out=so[:, v2, cs], in_=bk[:, :], func=AF.Sigmoid), [bn], ["so%d" % v2])
                        else:
                            P.op("act", lambda e, bk=bk: e.activation(out=sil, in_=bk[:, :], func=AF.Silu), [bn], ["sil"])
                            P.op("dve", lambda e, v2=v2, cs=cs: e.tensor_tensor(out=gz1[:, v2, cs], in0=sil, in1=ng1[:, cs], op=ALU.mult), ["sil", "ng1"], ["gz1_%d" % v2])
                    for pair in ((0, 1, 2, 3),):
                        for h in pair:
                            bA = HB4[h]
                            nA = BN[id(bA)]
                            kT_ps = bA[:, 128:192].bitcast(BF16)
                            P.op("pe", lambda e, h=h, bA=bA: e.matmul(bA[:, 0:128], lhsT=qk[:, 4 + h, qs], rhs=qk[:, h, qs], start=True, stop=True),
                                 ["qk%d" % h, "qk%d" % (4 + h)], [nA])
                            P.op("pe", lambda e, h=h, kT_ps=kT_ps: e.transpose(out=kT_ps, in_=qk[:, 4 + h, qs], identity=identb), ["qk%d" % (4 + h), "identb"], [nA])
                        for h in pair:
                            bA = HB4[h]
                            nA = BN[id(bA)]
                            kT_ps = bA[:, 128:192].bitcast(BF16)
                            hv = slice(h * 256, (h + 1) * 256)
                            P.op("dve", lambda e, h=h, bA=bA: e.tensor_tensor(out=PTm[h], in0=bA[:, 0:128], in1=triu, op=ALU.mult), [nA, "triu"], ["PTm%d" % h])
                            P.op("act", lambda e, h=h, kT_ps=kT_ps: e.copy(out=ktk[h], in_=kT_ps), [nA], ["ktk%d" % h])
                            P.op("pool", lambda e, h=h, hv=hv: e.tensor_scalar(out=vau[h][:, 0:256], in0=vf[:, v2, hv], scalar1=gtok[:, q, h:h + 1], scalar2=1.0, op0=ALU.mult, op1=ALU.mult),
                                 ["vf%d" % v2, "gtok"], ["vau%d" % h])
                            P.op("pool", lambda e, h=h: e.tensor_copy(out=vau[h][:, 256:257], in_=gtok[:, q, h:h + 1]), ["gtok", "vau%d" % h], ["vau%d" % h])
                        for h in pair:
                            bA = HB4[h]
                            bB = UB
                            nA = BN[id(bA)]
                            nB = BN[id(bB)]
                            s2 = h
                            P.op("pe", lambda e, s2=s2, bA=bA: e.matmul(bA[:, 192:449], lhsT=PTm[s2], rhs=vau[s2][:, 0:257], start=True, stop=False), ["PTm%d" % s2, "vau%d" % s2], [nA])
                            P.op("pe", lambda e, h=h, bA=bA: e.matmul(bA[:, 192:449], lhsT=qk[:, h, qs], rhs=Cbf[:, h, 0:257], start=False, stop=True), ["qk%d" % h, "Cbf%d" % h], [nA])
                            P.op("pe", lambda e, s2=s2, bB=bB: e.matmul(bB[:, 0:257], lhsT=ktk[s2], rhs=vau[s2][:, 0:257], start=True, stop=True), ["ktk%d" % s2, "vau%d" % s2], [nB])
                            dprev = dRl[:, h:h + 1] if q == 0 else dRb[:, 4 * h + q - 1:4 * h + q]
                            P.op("dve", lambda e, h=h, bB=bB, dprev=dprev: e.scalar_tensor_tensor(out=Cst[:, h, :], in0=Cst[:, h, :], scalar=dprev, in1=bB[:, 0:257], op0=ALU.mult, op1=ALU.add),
                                 ["Cst%d" % h, nB, "dRb", "dRl"], ["Cst%d" % h])
                        for h in pair:
                            bA = HB4[h]
                            bB = UB
                            nA = BN[id(bA)]
                            nB = BN[id(bB)]
                            hv = slice(h * 256, (h + 1) * 256)
                            P.op("act", lambda e, h=h: e.activation(out=Cbf[:, h, 0:257], in_=Cst[:, h, :], func=AF.Copy, scale=dRb[:, 4 * h + q:4 * h + q + 1]),
                                 ["Cst%d" % h, "dRb"], ["Cbf%d" % h])
                            P.op("dve", lambda e, h=h, bA=bA: e.tensor_scalar(out=t1[:, h:h + 1], in0=bA[:, 448:449], scalar1=gtok[:, q, 4 + h:5 + h], scalar2=None, op0=ALU.mult),
                                 [nA, "gtok"], ["t1_%d" % h])
                            P.op("dve", lambda e, h=h: e.scalar_tensor_tensor(out=t1[:, h:h + 1], in0=t1[:, h:h + 1], scalar=-1.0, in1=t1[:, h:h + 1], op0=ALU.mult, op1=ALU.max),
                                 ["t1_%d" % h], ["t1_%d" % h])
                            P.op("dve", lambda e, h=h: e.tensor_tensor(out=t1[:, h:h + 1], in0=t1[:, h:h + 1], in1=gtok[:, q, 8 + h:9 + h], op=ALU.max), ["t1_%d" % h, "gtok"], ["t1_%d" % h])
                            P.op("dve", lambda e, h=h: e.reciprocal(out=t1[:, h:h + 1], in_=t1[:, h:h + 1]), ["t1_%d" % h], ["t1_%d" % h])
                            P.op("dve", lambda e, h=h: e.tensor_tensor(out=rsv[:, h:h + 1], in0=t1[:, h:h + 1], in1=gtok[:, q, 4 + h:5 + h], op=ALU.mult), ["t1_%d" % h, "gtok"], ["rsv%d" % h])
                            P.op("dve", lambda e, h=h, hv=hv, bA=bA: e.scalar_tensor_tensor(out=ho[:, hv], in0=bA[:, 192:448], scalar=rsv[:, h:h + 1], in1=so[:, v2, hv], op0=ALU.mult, op1=ALU.mult),
                                 [nA, "rsv%d" % h, "so%d" % v2], ["ho%d" % h])
                            P.op("act", lambda e, h=h, hv=hv: e.activation(out=junk, in_=ho[:, hv], func=AF.Square, accum_out=ssq[:, h:h + 1]), ["ho%d" % h], ["junk", "ssq"])
                    P.op("dve", lambda e: e.tensor_scalar(out=rstd, in0=ssq, scalar1=1.0 / 256, scalar2=RMS_EPS, op0=ALU.mult, op1=ALU.add), ["ssq"], ["rstd"])
                    P.op("pool", lambda e: e.tensor_tensor(out=rstd, in0=rstd, in1=mhalf, op=ALU.pow), ["rstd", "mhalf"], ["rstd"])
                    for h in range(4):
                        hv = slice(h * 256, (h + 1) * 256)
                        P.op("dve", lambda e, h=h, hv=hv: e.scalar_tensor_tensor(out=g1b[:, hv], in0=ho[:, hv], scalar=rstd[:, h:h + 1], in1=gz1[:, v2, hv], op0=ALU.mult, op1=ALU.mult),
                             ["ho%d" % h, "rstd", "gz1_%d" % v2], ["g1b"])
                    dma(G1[r0:r0 + 128, :], g1b, ["g1b"], ["G1"], "g1st")
            fmi = 0
            tmi = 0
            for st in range(NST):
                front(H1, st, xin, xb, xT, tpb)
                t0 = st * 512
                for gi in range(2):
                    bk = prj[pji[0] % len(prj)]
                    bn = BN[id(bk)]
                    pji[0] += 1
                    fm_mm(bk, bn, 4, wf, "wf", 1024 + 4 * gi, xT)
                    if gi == 0:
                        P.op("act", lambda e, bk=bk: e.activation(out=ig, in_=bk[0:4, :], func=AF.Identity, bias=bi), [bn, "bi"], ["ig"])
                    else:
                        P.op("act", lambda e, bk=bk: e.activation(out=lg, in_=bk[0:4, :], func=AF.Sigmoid, bias=bfm), [bn, "bfm"], ["lg"])
                        P.op("act", lambda e: e.activation(out=lg, in_=lg, func=AF.Ln), ["lg"], ["lg"])
                P.op("dve", lambda e: e.tensor_tensor_scan(out=Bc, data0=ones[0:4, :], data1=lg, initial=cB, op0=ALU.mult, op1=ALU.add), ["lg", "ones", "cB"], ["Bc"])
                P.op("dve", lambda e: e.tensor_tensor(out=aa, in0=ig, in1=Bc, op=ALU.subtract), ["ig", "Bc"], ["aa"])
                P.op("pool", lambda e: e.tensor_copy(out=Rpe[:, 0:1], in_=cR), ["cR"], ["Rpe"])
                P.op("dve", lambda e: e.tensor_tensor_scan(out=RR, data0=aa, data1=aa, initial=cR, op0=ALU.max, op1=ALU.max), ["aa", "cR"], ["RR"])
                P.op("pool", lambda e: e.tensor_copy(out=Rpe[:, 1:4], in_=RR[:, 127:384:128]), ["RR"], ["Rpe"])
                P.op("pool", lambda e: e.tensor_copy(out=cB, in_=Bc[:, 511:512]), ["Bc"], ["cB"])
                P.op("pool", lambda e: e.tensor_copy(out=cR, in_=RR[:, 511:512]), ["RR", "Rpe"], ["cR"])
                rpb = Rpe.unsqueeze(2).to_broadcast([4, 4, 128])
                v3 = lambda a: a.rearrange("p (q t) -> p q t", q=4)
                P.op("dve", lambda e: e.tensor_tensor(out=v3(ea), in0=v3(aa), in1=rpb, op=ALU.subtract), ["aa", "Rpe"], ["ea"])
                P.op("act", lambda e: e.activation(out=ea, in_=ea, func=AF.Exp), ["ea"], ["ea"])
                P.op("dve", lambda e: e.tensor_tensor(out=v3(er), in0=v3(RR), in1=rpb, op=ALU.subtract), ["RR", "Rpe"], ["er"])
                P.op("act", lambda e: e.activation(out=er, in_=er, func=AF.Exp, scale=-1.0), ["er"], ["er"])
                P.op("dve", lambda e: e.tensor_tensor(out=th, in0=Bc, in1=RR, op=ALU.add), ["Bc", "RR"], ["th"])
                P.op("act", lambda e: e.activation(out=th, in_=th, func=AF.Exp, scale=-1.0), ["th"], ["th"])
                gtp = mA[:, 448:496].rearrange("p (q c) -> p q c", q=4)
                for q in range(4):
                    for gi, (src, sn) in enumerate(((ea, "ea"), (er, "er"), (th, "th"))):
                        P.op("pe", lambda e, q=q, gi=gi, src=src: e.transpose(out=gtp[:, q, gi * 4:(gi + 1) * 4], in_=src[:, q * 128:(q + 1) * 128], identity=identf[0:4, 0:4]),
                             [sn, "identf"], ["pb0"])
                P.op("act", lambda e: e.copy(out=gtok, in_=gtp), ["pb0"], ["gtok"])
                for h in range(4):
                    P.op("pe", lambda e, h=h: e.matmul(mA[:, 496 + 4 * h:500 + 4 * h], lhsT=sel[:, h * 128:(h + 1) * 128], rhs=er[:, 127:512:128], start=True, stop=True),
                         ["sel", "er"], ["pb0"])
                if st > 0:
                    P.op("pool", lambda e: e.tensor_copy(out=dRl, in_=dRb[:, 3:16:4]), ["dRb"], ["dRl"])
                P.op("act", lambda e: e.copy(out=dRb, in_=mA[:, 496:512]), ["pb0"], ["dRb"])
                for b in range(8):
                    conv_block(b)
                for q in range(4):
                    a1_tile(t0, q)

            P.barrier()

        if "A0" in phases:
            phase_a0()
        if "B0" in phases:
            phase_b0()
        if "C0" in phases:
            phase_c(GH, wo0_d, x_d, H1, 0)
        if "A1" in phases:
            phase_a1()
        if "C1" in phases:
            phase_c(G1, wo1_d, H1, out_d, 1)

        P.finish(["DST", "GH", "G1", "FQ", "FK", "FV", "FZ", "CT"])
        P.emit()
        build.stats = P.stats()
    return nc


def host_consts():
    s = np.arange(128)[:, None]
    t = np.arange(128)[None, :]
    triu = (s <= t).astype(np.float32)
    dmask = np.where(s <= t, 0.0, -1.0e9).astype(np.float32)
    rmask = np.ones((128, 512), np.float32)
    rmask[:, 0::128] = 0.0
    sel = np.zeros((4, 512), np.float32)
    for h in range(4):
        sel[h, h * 128:(h + 1) * 128] = 1.0
    return {"ident": np.eye(128, dtype=np.float32), "triu": triu, "dmask": dmask, "rmask": rmask, "sel": sel}


def host_params(hgrn_lb_logits, ab_w_in, ab_fox_bf, ab_hgrn_norm_g, ab_w_out, c_w_in,
                c_conv_w, c_conv_b, c_bi, c_bf, c_norm_g, c_w_out, ln_g, ln_b):
    f = np.float32
    w0 = np.asarray(ab_w_in[0], f)
    sl = lambda a, b: w0[:, a:b]
    wf0 = np.concatenate([sl(0, 512), sl(512, 1024), sl(2048, 2560), sl(2560, 3072), sl(3584, 3588)], axis=1)
    wt0 = np.concatenate([sl(1024, 1536), sl(1536, 2048), sl(3072, 3584), sl(3588, 4100)], axis=1)
    w1 = np.asarray(c_w_in[0], f)
    wf1 = np.concatenate([w1[:, 0:512], w1[:, 512:1024], w1[:, 2048:2052], w1[:, 2052:2056]], axis=1)
    wt1 = np.concatenate([w1[:, 1024:2048], w1[:, 2056:3080], w1[:, 3080:4104]], axis=1)
    lbl = np.asarray(hgrn_lb_logits, f).reshape(3, 4, 128).transpose(2, 1, 0).reshape(128, 12)
    rep = lambda v: np.ascontiguousarray(np.broadcast_to(np.asarray(v, f).reshape(1, -1), (128, np.asarray(v).size)))
    cw = np.asarray(c_conv_w[0], f).reshape(4, 8, 128).transpose(2, 1, 0).reshape(128, 32)
    cb = np.asarray(c_conv_b[0], f).reshape(8, 128).T
    d = {
        "wf0": wf0, "wt0": wt0, "wo0": np.asarray(ab_w_out[0], f),
        "wf1": wf1, "wt1": wt1, "wo1": np.asarray(c_w_out[0], f),
        "lbl": lbl, "fbf": np.asarray(ab_fox_bf[0], f).reshape(4, 1),
        "hng": rep(ab_hgrn_norm_g[0]), "lng": rep(np.asarray(ln_g, f).reshape(-1)), "lnb": rep(np.asarray(ln_b, f).reshape(-1)),
        "cw": cw, "cb": cb, "bi": np.asarray(c_bi[0], f).reshape(4, 1), "bfm": np.asarray(c_bf[0], f).reshape(4, 1),
        "ng1": rep(c_norm_g[0]),
    }
    d.update(host_consts())
    return {k: np.ascontiguousarray(v, dtype=f) for k, v in d.items()}


_NC_CACHE = {}


def kernel(x, hgrn_lb_logits, ab_w_in, ab_fox_bf, ab_hgrn_norm_g, ab_w_out, c_w_in,
           c_conv_w, c_conv_b, c_bi, c_bf, c_norm_g, c_w_out, ln_g, ln_b):
    x = np.asarray(x, np.float32)
    B, T, _ = x.shape
    params = host_params(hgrn_lb_logits, ab_w_in, ab_fox_bf, ab_hgrn_norm_g, ab_w_out, c_w_in,
                         c_conv_w, c_conv_b, c_bi, c_bf, c_norm_g, c_w_out, ln_g, ln_b)
    if T not in _NC_CACHE:
        _NC_CACHE[T] = build(T)
    nc = _NC_CACHE[T]
    owner = {2 * b: b for b in range(B)}
    zx = np.zeros((T, D), np.float32)
    in_maps = []
    for c in range(8):
        m = dict(params)
        m["x"] = np.ascontiguousarray(x[owner[c]]) if c in owner else zx
        in_maps.append(m)
    res = run_bass_kernel_spmd(nc, in_maps, core_ids=list(range(8)))
    return np.stack([np.asarray(res.results[2 * b]["out"], np.float32) for b in range(B)], axis=0)
```

```python
import numpy as np
from contextlib import ExitStack
import concourse.bass as bass
import concourse.mybir as mybir
from concourse.bass_utils import run_bass_kernel_spmd

F32 = mybir.dt.float32
BF16 = mybir.dt.bfloat16
AF = mybir.ActivationFunctionType
ALU = mybir.AluOpType

D = 1024
ALPHA = 4.0 ** 0.25
LN_EPS = 1e-5
RMS_EPS = 1e-6


class _Op:
    __slots__ = ("fn", "waits", "inc", "dma", "val")

    def __init__(self, fn, waits, dma):
        self.fn = fn
        self.waits = waits
        self.inc = False
        self.dma = dma
        self.val = 0


class Prog:
    ENGS = ("pe", "act", "dve", "pool", "sp")

    def __init__(self, nc, es):
        self.nc = nc
        self.es = es
        self.ops = {e: [] for e in self.ENGS}
        self.lastw = {}
        self.readers = {}
        self.known = {e: {} for e in self.ENGS}
        self.dma_count = {}

    def op(self, eng, fn, reads=(), writes=(), dma=None):
        writes = list(writes) + [r for r in reads if r.startswith("pb") and r not in writes]
        reads = [r for r in reads if not r.startswith("pb")]
        if dma is not None:
            cnt = self.dma_count.get(dma, 0) + 1
            self.dma_count[dma] = cnt
            me = ("dma:" + dma, cnt)
        else:
            me = (eng, len(self.ops[eng]))
        deps = []
        for r in reads:
            w = self.lastw.get(r)
            if w is not None:
                deps.append(w)
        for r in writes:
            w = self.lastw.get(r)
            if w is not None and not (w[0] == me[0] == "pe"):
                deps.append(w)
            for rn, ri in self.readers.get(r, {}).items():
                if rn != me[0] or dma is not None:
                    deps.append((rn, ri))
        kn = self.known[eng]
        wm = {}
        for name, idx in deps:
            if name.startswith("dma:"):
                idx = self.dma_count[name[4:]] - (1 if (dma is not None and name[4:] == dma) else 0)
            if kn.get(name, -1) >= idx:
                continue
            kn[name] = idx
            wm[name] = max(wm.get(name, -1), idx)
            if not name.startswith("dma:"):
                self.ops[name][idx].inc = True
        self.ops[eng].append(_Op(fn, list(wm.items()), dma))
        for r in reads:
            self.readers.setdefault(r, {})[me[0]] = me[1]
        for r in writes:
            self.lastw[r] = me
            self.readers[r] = {}
        return me

    def finish(self, resources):
        self.op("sp", None, reads=list(resources))

    def barrier(self):
        lasts = {}
        for e in self.ENGS:
            for i in range(len(self.ops[e]) - 1, -1, -1):
                if self.ops[e][i].fn is not None and self.ops[e][i].dma is None:
                    lasts[e] = i
                    break
        for e in self.ENGS:
            waits = []
            kn = self.known[e]
            for s_, idx in lasts.items():
                if s_ != e and kn.get(s_, -1) < idx:
                    waits.append((s_, idx))
                    kn[s_] = idx
                    self.ops[s_][idx].inc = True
            for k, cnt in self.dma_count.items():
                if kn.get("dma:" + k, -1) < cnt:
                    waits.append(("dma:" + k, cnt))
                    kn["dma:" + k] = cnt
            self.ops[e].append(_Op(None, waits, None))

    SEM_LIM = 4000

    def emit(self):
        nc = self.nc
        ops = self.ops
        LIM = self.SEM_LIM
        nep = {}
        for e in self.ENGS:
            c = 0
            for o in ops[e]:
                if o.inc:
                    c += 1
                o.val = c
            nep[e] = (c + LIM - 1) // LIM + 1
        sems = {e: [self.es.enter_context(nc.semaphore("s_%s%d" % (e, i))) for i in range(nep[e])] for e in self.ENGS}
        dsems = {k: self.es.enter_context(nc.semaphore("d_" + k)) for k in self.dma_count}
        for k, cnt in self.dma_count.items():
            assert 16 * cnt < 2 * LIM, (k, cnt)

        def run(ename, engine):
            for o in ops[ename]:
                for name, idx in o.waits:
                    if name.startswith("dma:"):
                        engine.wait_ge(dsems[name[4:]], 16 * idx)
                    else:
                        tgt = ops[name][idx]
                        assert tgt.inc and tgt.val > 0
                        engine.wait_ge(sems[name][(tgt.val - 1) // LIM], (tgt.val - 1) % LIM + 1)
                if o.fn is None:
                    continue
                ins = o.fn(engine)
                if o.dma is not None:
                    ins.then_inc(dsems[o.dma], 16)
                elif o.inc:
                    ins.then_inc(sems[ename][(o.val - 1) // LIM], 1)

        with nc.Block() as block:
            @block.tensor
            def _(e):
                run("pe", e)

            @block.scalar
            def _(e):
                run("act", e)

            @block.vector
            def _(e):
                run("dve", e)

            @block.gpsimd
            def _(e):
                run("pool", e)

            @block.sync
            def _(e):
                run("sp", e)

    def stats(self):
        return {e: (len(self.ops[e]), sum(1 for o in self.ops[e] if o.inc)) for e in self.ENGS}


class Arena:
    def __init__(self, t, nwords):
        self.t = t
        self.n = nwords
        self.off = 0
        self.base = 0

    def alloc(self, free_shape, dt, parts=128):
        n = int(np.prod(free_shape))
        words = n if dt == F32 else (n + 1) // 2
        words = (words + 7) // 8 * 8
        assert self.off + words <= self.n, ("SBUF arena overflow", self.off, words, self.n)
        ap = self.t[0:parts, self.off:self.off + words]
        self.off += words
        if dt != F32:
            ap = ap.bitcast(dt)
        ap = ap[:, 0:n]
        if len(free_shape) == 2:
            ap = ap.rearrange("p (a b) -> p a b", a=free_shape[0])
        elif len(free_shape) == 3:
            ap = ap.rearrange("p (a b c) -> p a b c", a=free_shape[0], b=free_shape[1])
        return ap

    def mark(self):
        self.base = self.off

    def reset(self):
        self.off = self.base


def build(T, dbg=False, phases=("A0", "B0", "C0", "A1", "C1")):
    NT = T // 128
    NST = T // 512
    nc = bass.Bass("TRN2", target_bir_lowering=False)
    din = lambda name, shape, dt=F32: nc.dram_tensor(name, shape, dt, kind="ExternalInput").ap()
    dsc = lambda name, shape, dt: nc.dram_tensor(name, shape, dt, kind="ExternalOutput" if dbg else "Internal").ap()
    x_d = din("x", [T, D])
    wf0_d = din("wf0", [D, 2052])
    wt0_d = din("wt0", [D, 2048])
    wo0_d = din("wo0", [D, D])
    wf1_d = din("wf1", [D, 1032])
    wt1_d = din("wt1", [D, 3072])
    wo1_d = din("wo1", [D, D])
    lbl_d = din("lbl", [128, 12])
    fbf_d = din("fbf", [4, 1])
    hng_d = din("hng", [128, 512])
    lng_d = din("lng", [128, 2 * D])
    lnb_d = din("lnb", [128, 2 * D])
    cw_d = din("cw", [128, 32])
    cb_d = din("cb", [128, 8])
    bi_d = din("bi", [4, 1])
    bfm_d = din("bfm", [4, 1])
    ng1_d = din("ng1", [128, D])
    ident_d = din("ident", [128, 128])
    triu_d = din("triu", [128, 128])
    dmask_d = din("dmask", [128, 128])
    rmask_d = din("rmask", [128, 512])
    sel_d = din("sel", [4, 512])
    out_d = nc.dram_tensor("out", [T, D], F32, kind="ExternalOutput").ap()
    GH = dsc("GH", [T, D], BF16)
    FQ = dsc("FQ", [4, 128, T], BF16)
    FK = dsc("FK", [4, 128, T], BF16)
    FV = dsc("FV", [4, T, 128], BF16)
    FZ = dsc("FZ", [T, 512], BF16)
    CT = dsc("CT", [4, T], F32)
    H1 = dsc("H1", [T, D], F32)
    G1 = dsc("G1", [T, D], BF16)

    with ExitStack() as es:
        P = Prog(nc, es)
        AW = 50000
        arena_t = es.enter_context(nc.sbuf_tensor("arena", [128, AW], F32))
        A = Arena(arena_t, AW)
        banks = [es.enter_context(nc.psum_tensor("pb%d" % i, [128, 512], F32)) for i in range(8)]
        BN = {id(b): "pb%d" % i for i, b in enumerate(banks)}

        ucnt = [0]

        def dma(out, in_, reads, writes, key, eng="sp"):
            if key == "cst":
                key = "cst%d" % ucnt[0]
                ucnt[0] += 1
            P.op(eng, lambda e: e.dma_start(out=out, in_=in_), reads=reads, writes=writes, dma=key)

        identf = A.alloc([128], F32)
        identb = A.alloc([128], BF16)
        triu = A.alloc([128], F32)
        dmask = A.alloc([128], F32)
        rmask = A.alloc([512], F32)
        ones = A.alloc([512], F32)
        sel = A.alloc([512], F32, parts=4)
        dma(identf, ident_d, [], ["identf"], "cst")
        dma(triu, triu_d, [], ["triu"], "cst")
        dma(dmask, dmask_d, [], ["dmask"], "cst")
        dma(rmask, rmask_d, [], ["rmask"], "cst")
        dma(sel, sel_d, [], ["sel"], "cst")
        P.op("pool", lambda e: e.tensor_copy(out=identb, in_=identf), ["identf"], ["identb"])
        P.op("pool", lambda e: e.memset(ones, 1.0), [], ["ones"])
        mhalf = A.alloc([4], F32)
        P.op("pool", lambda e: e.memset(mhalf, -0.5), [], ["mhalf"])
        A.mark()

        def load_weights(w_d, ncols, wbf, name, wst):
            for k in range(8):
                s = k % 2
                dma(wst[s][:, 0:ncols], w_d[k * 128:(k + 1) * 128, :], [], ["wst%d" % s], "wld%d" % s)
                eng = ("dve", "act", "pool")[k % 3]
                if eng == "act":
                    P.op("act", lambda e, k=k, s=s: e.copy(out=wbf[:, k, :], in_=wst[s][:, 0:ncols]), ["wst%d" % s], [name])
                else:
                    P.op(eng, lambda e, k=k, s=s: e.tensor_copy(out=wbf[:, k, :], in_=wst[s][:, 0:ncols]), ["wst%d" % s], [name])

        def front(src_d, st, xin, xb, xT, tpb):
            for q in range(4):
                r0 = (st * 4 + q) * 128
                s_ = q % 2
                dma(xin[:, s_, :], src_d[r0:r0 + 128, :], [], ["xin%d" % s_], "xin%d" % s_)
                P.op("pool", lambda e, s_=s_: e.tensor_copy(out=xb[:, s_, :], in_=xin[:, s_, :]), ["xin%d" % s_], ["xb%d" % s_])
                tb = tpb[q % len(tpb)]
                tn = BN[id(tb)]
                tv = tb[:, 0:512].bitcast(BF16).rearrange("p (k t) -> p k t", k=8)
                for k in range(8):
                    P.op("pe", lambda e, s_=s_, k=k, tv=tv: e.transpose(out=tv[:, k, :], in_=xb[:, s_, k * 128:(k + 1) * 128], identity=identb),
                         ["xb%d" % s_, "identb"], [tn])
                P.op("act", lambda e, q=q, tv=tv: e.copy(out=xT[:, :, q * 128:(q + 1) * 128], in_=tv), [tn], ["xT"])

        def fm_mm(bank, bname, M, wbf, wname, c0, xT):
            for k in range(8):
                P.op("pe", lambda e, k=k: e.matmul(bank[0:M, :], lhsT=wbf[:, k, c0:c0 + M], rhs=xT[:, k, :], start=(k == 0), stop=(k == 7)),
                     [wname, "xT"], [bname])

        def tm_mm(bank, bname, q, wbf, wname, c0, n, xT):
            for k in range(8):
                P.op("pe", lambda e, k=k: e.matmul(bank[:, 0:n], lhsT=xT[:, k, q * 128:(q + 1) * 128], rhs=wbf[:, k, c0:c0 + n], start=(k == 0), stop=(k == 7)),
                     [wname, "xT"], [bname])

        def phase_a0():
            A.reset()
            big = A.alloc([4112], F32)
            wst = [big[:, 0:2052], big[:, 2056:2056 + 2052]]
            hqT = big[:, 0:2048].rearrange("p (a b) -> p a b", a=4)
            sg = big[:, 2048:4096].rearrange("p (a b) -> p a b", a=4)
            wf = A.alloc([8, 2052], BF16)
            wt = A.alloc([8, 2048], BF16)
            load_weights(wf0_d, 2052, wf, "wf", wst)
            load_weights(wt0_d, 2048, wt, "wt", wst)
            P.barrier()
            xin = A.alloc([2, D], F32)
            xb = A.alloc([2, D], BF16)
            xT = A.alloc([8, 512], BF16)
            hng = A.alloc([512], F32)
            lbl = A.alloc([12], F32)
            lbe = A.alloc([12], F32)
            lb = A.alloc([4], F32)
            oml = A.alloc([4], F32)
            lbm1 = A.alloc([4], F32)
            fbf = A.alloc([1], F32, parts=4)
            dma(hng, hng_d, [], ["hng"], "cst")
            dma(lbl, lbl_d, [], ["lbl"], "cst")
            dma(fbf, fbf_d, [], ["fbf"], "cst")
            P.op("act", lambda e: e.activation(out=lbe, in_=lbl, func=AF.Exp), ["lbl"], ["lbe"])
            lbe3 = lbe.rearrange("p (h r) -> p h r", h=4)
            P.op("dve", lambda e: e.tensor_tensor(out=lb, in0=lbe3[:, :, 0], in1=lbe3[:, :, 1], op=ALU.add), ["lbe"], ["lbs"])
            P.op("dve", lambda e: e.tensor_tensor(out=lb, in0=lb, in1=lbe3[:, :, 2], op=ALU.add), ["lbe", "lbs"], ["lbs"])
            P.op("dve", lambda e: e.reciprocal(out=lb, in_=lb), ["lbs"], ["lbs"])
            P.op("dve", lambda e: e.tensor_tensor(out=lb, in0=lb, in1=lbe3[:, :, 0], op=ALU.mult), ["lbs", "lbe"], ["lb", "lbs"])
            P.op("dve", lambda e: e.tensor_scalar(out=oml, in0=lb, scalar1=-1.0, scalar2=1.0, op0=ALU.mult, op1=ALU.add), ["lb"], ["oml"])
            P.op("dve", lambda e: e.tensor_scalar(out=lbm1, in0=lb, scalar1=-1.0, scalar2=None, op0=ALU.add), ["lb"], ["lbm1"])

            vb = A.alloc([2, 512], BF16)
            gz = A.alloc([2, 512], F32)
            sil = A.alloc([512], F32)
            fvb = A.alloc([2, 512], BF16)
            fzb = A.alloc([2, 512], BF16)
            gl = A.alloc([4, 512], F32)
            bb = A.alloc([4, 512], F32)
            qdec = A.alloc([4, 512], BF16)
            kinv = A.alloc([4, 512], BF16)
            kend = A.alloc([4, 512], BF16)
            fqb = A.alloc([4, 512], BF16)
            fkb = A.alloc([4, 512], BF16)
            dd = A.alloc([4, 4], F32)
            d1 = A.alloc([4, 4], F32)
            d2 = A.alloc([4, 4], F32)
            d3 = A.alloc([4, 4], F32)
            ffs = A.alloc([512], F32, parts=4)
            ls = A.alloc([512], F32, parts=4)
            cT = A.alloc([512], F32, parts=4)
            ccar = A.alloc([1], F32, parts=4)
            S = A.alloc([4, 128], F32)
            Sbf = A.alloc([4, 128], BF16)
            aTm = [A.alloc([128], BF16), A.alloc([128], BF16)]
            kit = [A.alloc([128], BF16), A.alloc([128], BF16)]
            tmpU = [A.alloc([128], F32), A.alloc([128], F32)]
            junk = A.alloc([128], F32)
            ssq = A.alloc([4], F32)
            rstd = A.alloc([4], F32)
            ghb = A.alloc([512], BF16)
            P.op("pool", lambda e: e.memset(S, 0.0), [], ["S0", "S1", "S2", "S3"])
            tpb = [banks[0]]
            prj = [banks[1], banks[2], banks[3]]
            pji = [0]
            HB = [banks[4], banks[5], banks[6], banks[7]]
            aTm4 = [A.alloc([128], BF16) for _ in range(4)]
            kit4 = [A.alloc([128], BF16) for _ in range(4)]
            tmpU4 = [A.alloc([128], F32) for _ in range(4)]
            junk4 = [A.alloc([128], F32) for _ in range(4)]

            def a0_tile(t0, q):
                r0 = t0 + q * 128
                qs = slice(q * 128, (q + 1) * 128)
                v2 = q % 2
                for c in range(4):
                    bk = prj[pji[0] % 3]
                    bn = BN[id(bk)]
                    pji[0] += 1
                    tm_mm(bk, bn, q, wt, "wt", c * 512, 512, xT)
                    if c == 0:
                        P.op("act", lambda e, bk=bk: e.copy(out=vb[:, v2, :], in_=bk[:, :]), [bn], ["vb%d" % v2])
                    elif c == 1:
                        P.op("act", lambda e, bk=bk: e.activation(out=sil, in_=bk[:, :], func=AF.Silu), [bn], ["sil"])
                        P.op("dve", lambda e: e.tensor_tensor(out=gz[:, v2, :], in0=sil, in1=hng, op=ALU.mult), ["sil", "hng"], ["gz%d" % v2])
                    elif c == 2:
                        P.op("act", lambda e, bk=bk: e.copy(out=fvb[:, v2, :], in_=bk[:, :]), [bn], ["fvb%d" % v2])
                        dma(FV[:, r0:r0 + 128, :].rearrange("h p d -> p h d"), fvb[:, v2, :].rearrange("p (h d) -> p h d", h=4),
                            ["fvb%d" % v2], ["FV"], "fvst%d" % v2)
                    else:
                        P.op("act", lambda e, bk=bk: e.activation(out=fzb[:, v2, :], in_=bk[:, :], func=AF.Silu), [bn], ["fzb%d" % v2])
                        dma(FZ[r0:r0 + 128, :], fzb[:, v2, :], ["fzb%d" % v2], ["FZ"], "fzst%d" % v2)
                hs_ = [slice(h * 128, (h + 1) * 128) for h in range(4)]
                for h in range(4):
                    hb = HB[h]
                    nb = BN[id(hb)]
                    kT_ps = hb[:, 384:448].bitcast(BF16)
                    P.op("pe", lambda e, h=h, hb=hb: e.matmul(hb[:, 0:128], lhsT=kinv[:, h, qs], rhs=qdec[:, h, qs], start=True, stop=True),
                         ["kinv%d" % h, "qdec%d" % h], [nb])
                    P.op("pe", lambda e, h=h, kT_ps=kT_ps: e.transpose(out=kT_ps, in_=kend[:, h, qs], identity=identb), ["kend%d" % h, "identb"], [nb])
                for h in range(4):
                    hb = HB[h]
                    nb = BN[id(hb)]
                    kT_ps = hb[:, 384:448].bitcast(BF16)
                    P.op("dve", lambda e, h=h, hb=hb: e.tensor_tensor(out=aTm4[h], in0=hb[:, 0:128], in1=triu, op=ALU.mult), [nb, "triu"], ["aTm%d" % h])
                    P.op("act", lambda e, h=h, kT_ps=kT_ps: e.copy(out=kit4[h], in_=kT_ps), [nb], ["kit%d" % h])
                    P.op("dve", lambda e, h=h: e.tensor_scalar(out=Sbf[:, h, :], in0=S[:, h, :], scalar1=d3[:, h, q:q + 1], scalar2=None, op0=ALU.mult),
                         ["S%d" % h, "d3_%d" % h], ["Sbf%d" % h])
                for h in range(4):
                    hb = HB[h]
                    nb = BN[id(hb)]
                    P.op("pe", lambda e, h=h, hb=hb: e.matmul(hb[:, 256:384], lhsT=aTm4[h], rhs=vb[:, v2, hs_[h]], start=True, stop=False),
                         ["aTm%d" % h, "vb%d" % v2], [nb])
                    P.op("pe", lambda e, h=h, hb=hb: e.matmul(hb[:, 256:384], lhsT=qdec[:, h, qs], rhs=Sbf[:, h, :], start=False, stop=True),
                         ["qdec%d" % h, "Sbf%d" % h], [nb])
                    P.op("pe", lambda e, h=h, hb=hb: e.matmul(hb[:, 128:256], lhsT=kit4[h], rhs=vb[:, v2, hs_[h]], start=True, stop=True),
                         ["kit%d" % h, "vb%d" % v2], [nb])
                for h in range(4):
                    hb = HB[h]
                    nb = BN[id(hb)]
                    P.op("dve", lambda e, h=h, hb=hb: e.scalar_tensor_tensor(out=S[:, h, :], in0=S[:, h, :], scalar=d1[:, h, q:q + 1], in1=hb[:, 128:256], op0=ALU.mult, op1=ALU.add),
                         ["S%d" % h, "d1_%d" % h, nb], ["S%d" % h])
                    P.op("act", lambda e, h=h, hb=hb: e.activation(out=junk4[h], in_=hb[:, 256:384], func=AF.Square, accum_out=ssq[:, h:h + 1]),
                         [nb], ["junk%d" % h, "ssq"])
                P.op("dve", lambda e: e.tensor_scalar(out=rstd, in0=ssq, scalar1=1.0 / 128, scalar2=RMS_EPS, op0=ALU.mult, op1=ALU.add), ["ssq"], ["rstd"])
                P.op("pool", lambda e: e.tensor_tensor(out=rstd, in0=rstd, in1=mhalf, op=ALU.pow), ["rstd", "mhalf"], ["rstd"])
                for h in range(4):
                    hb = HB[h]
                    nb = BN[id(hb)]
                    P.op("dve", lambda e, h=h, hb=hb: e.scalar_tensor_tensor(out=ghb[:, hs_[h]], in0=hb[:, 256:384], scalar=rstd[:, h:h + 1], in1=gz[:, v2, hs_[h]], op0=ALU.mult, op1=ALU.mult),
                         [nb, "rstd", "gz%d" % v2], ["ghb"])
                dma(GH[r0:r0 + 128, 0:512], ghb, ["ghb"], ["GH"], "ghst")

            SC = 128 ** -0.5
            fmi = 0
            tmi = 0
            for st in range(NST):
                front(x_d, st, xin, xb, xT, tpb)
                t0 = st * 512
                for h in range(4):
                    for slot in range(4):
                        bk = prj[pji[0] % 3]
                        bn = BN[id(bk)]
                        pji[0] += 1
                        fm_mm(bk, bn, 128, wf, "wf", slot * 512 + h * 128, xT)
                        if slot == 0:
                            P.op("act", lambda e, h=h, bk=bk: e.copy(out=hqT[:, h, :], in_=bk[:, :]), [bn], ["hqT%d" % h])
                        elif slot == 1:
                            P.op("act", lambda e, h=h, bk=bk: e.activation(out=sg[:, h, :], in_=bk[:, :], func=AF.Sigmoid), [bn], ["sg%d" % h])
                            P.op("act", lambda e, h=h: e.activation(out=gl[:, h, :], in_=sg[:, h, :], func=AF.Ln, scale=oml[:, h:h + 1], bias=lb[:, h:h + 1]),
                                 ["sg%d" % h, "oml", "lb"], ["gl%d" % h])
                            P.op("dve", lambda e, h=h: e.tensor_scalar(out=sg[:, h, :], in0=sg[:, h, :], scalar1=-1.0, scalar2=lbm1[:, h:h + 1], op0=ALU.add, op1=ALU.mult),
                                 ["sg%d" % h, "lbm1"], ["sg%d" % h])
                            P.op("dve", lambda e, h=h: e.tensor_tensor_scan(out=bb[:, h, :], data0=rmask, data1=gl[:, h, :], initial=0.0, op0=ALU.mult, op1=ALU.add),
                                 ["gl%d" % h, "rmask"], ["bb%d" % h])
                            bmid = bb[:, h, 63:512:128]
                            blast = bb[:, h, 127:512:128]
                            P.op("act", lambda e, h=h, bmid=bmid: e.activation(out=d3[:, h, :], in_=bmid, func=AF.Exp), ["bb%d" % h], ["d3_%d" % h])
                            P.op("act", lambda e, h=h, blast=blast: e.activation(out=d1[:, h, :], in_=blast, func=AF.Exp), ["bb%d" % h], ["d1_%d" % h])
                            P.op("dve", lambda e, h=h, bmid=bmid, blast=blast: e.tensor_tensor(out=dd[:, h, :], in0=blast, in1=bmid, op=ALU.subtract), ["bb%d" % h], ["dd%d" % h])
                            P.op("act", lambda e, h=h: e.activation(out=d2[:, h, :], in_=dd[:, h, :], func=AF.Exp), ["dd%d" % h], ["d2_%d" % h])
                            P.op("pool", lambda e, h=h, bmid=bmid: e.tensor_copy(out=dd[:, h, :], in_=bmid), ["bb%d" % h, "d2_%d" % h], ["dd%d" % h])
                            bb3 = bb[:, h, :].rearrange("p (q t) -> p q t", q=4)
                            P.op("dve", lambda e, h=h, bb3=bb3: e.tensor_tensor(out=bb3, in0=bb3, in1=dd[:, h, :].unsqueeze(2).to_broadcast([128, 4, 128]), op=ALU.subtract),
                                 ["bb%d" % h, "d3_%d" % h, "d1_%d" % h, "dd%d" % h], ["bb%d" % h])
                            P.op("act", lambda e, h=h: e.activation(out=gl[:, h, :], in_=bb[:, h, :], func=AF.Exp), ["bb%d" % h, "gl%d" % h], ["gl%d" % h])
                            P.op("act", lambda e, h=h: e.activation(out=bb[:, h, :], in_=bb[:, h, :], func=AF.Exp, scale=-1.0), ["bb%d" % h], ["bb%d" % h])
                            P.op("dve", lambda e, h=h: e.tensor_tensor(out=qdec[:, h, :], in0=hqT[:, h, :], in1=gl[:, h, :], op=ALU.mult),
                                 ["hqT%d" % h, "gl%d" % h], ["qdec%d" % h])
                            P.op("dve", lambda e, h=h: e.tensor_tensor(out=kinv[:, h, :], in0=sg[:, h, :], in1=bb[:, h, :], op=ALU.mult),
                                 ["sg%d" % h, "bb%d" % h], ["kinv%d" % h])
                            P.op("dve", lambda e, h=h: e.tensor_tensor(out=kend[:, h, :].rearrange("p (q t) -> p q t", q=4), in0=kinv[:, h, :].rearrange("p (q t) -> p q t", q=4),
                                                                     in1=d2[:, h, :].unsqueeze(2).to_broadcast([128, 4, 128]), op=ALU.mult),
                                 ["kinv%d" % h, "d2_%d" % h], ["kend%d" % h])
                        elif slot == 2:
                            P.op("act", lambda e, h=h, bk=bk: e.mul(out=fqb[:, h, :], in_=bk[:, :], mul=SC), [bn], ["fqb%d" % h])
                            dma(FQ[h, :, t0:t0 + 512], fqb[:, h, :], ["fqb%d" % h], ["FQ"], "fqst%d" % h)
                        else:
                            P.op("act", lambda e, h=h, bk=bk: e.copy(out=fkb[:, h, :], in_=bk[:, :]), [bn], ["fkb%d" % h])
                            dma(FK[h, :, t0:t0 + 512], fkb[:, h, :], ["fkb%d" % h], ["FK"], "fkst%d" % h)
                bk = prj[pji[0] % 3]
                bn = BN[id(bk)]
                pji[0] += 1
                fm_mm(bk, bn, 4, wf, "wf", 2048, xT)
                P.op("act", lambda e, bk=bk: e.activation(out=ffs, in_=bk[0:4, :], func=AF.Sigmoid, bias=fbf), [bn, "fbf"], ["ffs"])
                P.op("act", lambda e: e.activation(out=ls, in_=ffs, func=AF.Ln), ["ffs"], ["ls"])
                if st == 0:
                    P.op("dve", lambda e: e.tensor_tensor_scan(out=cT, data0=ones[0:4, :], data1=ls, initial=0.0, op0=ALU.mult, op1=ALU.add), ["ls", "ones"], ["cT"])
                else:
                    P.op("dve", lambda e: e.tensor_tensor_scan(out=cT, data0=ones[0:4, :], data1=ls, initial=ccar, op0=ALU.mult, op1=ALU.add), ["ls", "ones", "ccar"], ["cT"])
                P.op("pool", lambda e: e.tensor_copy(out=ccar, in_=cT[:, 511:512]), ["cT"], ["ccar"])
                dma(CT[:, t0:t0 + 512], cT, ["cT"], ["CT"], "ctst")
                for q in range(4):
                    a0_tile(t0, q)
            P.barrier()

        def phase_b0():
            A.reset()
            kTh = A.alloc([T], BF16)
            qTh = A.alloc([T], BF16)
            vaug = A.alloc([NT, 130], BF16)
            cq = A.alloc([T], F32)
            cqd = A.alloc([T], F32)
            ctr = A.alloc([128], F32)
            negc = A.alloc([NT], F32)
            fzg = A.alloc([4, 128], BF16)
            tmp = [A.alloc([512], F32) for _ in range(4)]
            pT = [A.alloc([512], BF16) for _ in range(4)]
            rden = A.alloc([4], F32)
            gfb = A.alloc([4, 128], BF16)
            P.op("pool", lambda e: e.memset(vaug, 1.0), [], ["vaug"])
            sTb = [banks[0], banks[1], banks[7], banks[6]]
            accb = [banks[2], banks[3], banks[4], banks[5]]
            ngb = banks[6]
            for h in range(4):
                dma(kTh, FK[h], ["FK"], ["kTh"], "bk")
                dma(qTh, FQ[h], ["FQ"], ["qTh"], "bq")
                dma(vaug[:, :, 0:128], FV[h].rearrange("(j p) d -> p j d", p=128), ["FV"], ["vaug"], "bv")
                dma(cq, CT[h].partition_broadcast(128), ["CT"], ["cq"], "bc")
                dma(ctr[0:NT, :], CT[h].rearrange("(j p) -> j p", p=128), ["CT"], ["ctr"], "br")
                P.op("pe", lambda e: e.transpose(out=ngb[:, 0:NT], in_=ctr[0:NT, :], identity=identf[0:NT, 0:NT]), ["ctr", "identf"], ["pb6"])
                P.op("act", lambda e: e.mul(out=negc, in_=ngb[:, 0:NT], mul=-1.0), ["pb6"], ["negc"])
                P.op("pool", lambda e: e.tensor_tensor(out=cqd.rearrange("p (j t) -> p j t", t=128), in0=cq.rearrange("p (j t) -> p j t", t=128),
                                                       in1=dmask.unsqueeze(1).to_broadcast([128, NT, 128]), op=ALU.add), ["cq", "dmask"], ["cqd"])
                steps = [(I, j) for I in range(NST) for j in range(4 * I + 4)]
                gstep = [0]

                def geom(I, j):
                    r = j - 4 * I
                    tq0 = 128 * max(r, 0)
                    return r, tq0, 512 - tq0, I * 512 + tq0

                def emit_qk(n):
                    I, j = steps[n]
                    r, tq0, N, c0 = geom(I, j)
                    sb_ = (gbase + n) % 4
                    sT = sTb[sb_]
                    P.op("pe", lambda e: e.matmul(sT[:, 0:N], lhsT=kTh[:, j * 128:(j + 1) * 128], rhs=qTh[:, c0:c0 + N], start=True, stop=True),
                         ["kTh", "qTh"], [BN[id(sT)]])

                def emit_soft(n):
                    I, j = steps[n]
                    r, tq0, N, c0 = geom(I, j)
                    sb_ = (gbase + n) % 4
                    sT = sTb[sb_]
                    bn = BN[id(sT)]
                    if r < 0:
                        P.op("dve", lambda e: e.tensor_tensor(out=tmp[sb_][:, 0:N], in0=sT[:, 0:N], in1=cq[:, c0:c0 + N], op=ALU.add),
                             [bn, "cq"], ["tmp%d" % sb_])
                    else:
                        P.op("dve", lambda e: e.tensor_tensor(out=tmp[sb_][:, 0:128], in0=sT[:, 0:128], in1=cqd[:, c0:c0 + 128], op=ALU.add),
                             [bn, "cqd"], ["tmp%d" % sb_])
                        if N > 128:
                            P.op("dve", lambda e: e.tensor_tensor(out=tmp[sb_][:, 128:N], in0=sT[:, 128:N], in1=cq[:, c0 + 128:c0 + N], op=ALU.add),
                                 [bn, "cq"], ["tmp%d" % sb_])
                    P.op("act", lambda e: e.activation(out=pT[sb_][:, 0:N], in_=tmp[sb_][:, 0:N], func=AF.Exp, bias=negc[:, j:j + 1]),
                         ["tmp%d" % sb_, "negc"], ["pT%d" % sb_])

                def emit_pv(n):
                    I, j = steps[n]
                    r, tq0, N, c0 = geom(I, j)
                    sb_ = (gbase + n) % 4
                    if j == 0:
                        for qq in range(4):
                            r0 = (I * 4 + qq) * 128
                            dma(fzg[:, qq, :], FZ[r0:r0 + 128, h * 128:(h + 1) * 128], ["FZ"], ["fzg"], "fzld")
                    for qq in range(max(r, 0), 4):
                        o0 = qq * 128 - tq0
                        P.op("pe", lambda e, qq=qq, o0=o0: e.matmul(accb[qq][:, 0:129], lhsT=pT[sb_][:, o0:o0 + 128], rhs=vaug[:, j, 0:129],
                                                                  start=(j == 0), stop=(j == 4 * I + qq)),
                             ["pT%d" % sb_, "vaug"], ["pb%d" % (2 + qq)])
                    if j == 4 * I + 3:
                        for qq in range(4):
                            r0 = (I * 4 + qq) * 128
                            P.op("dve", lambda e, qq=qq: e.reciprocal(out=rden[:, qq:qq + 1], in_=accb[qq][:, 128:129]), ["pb%d" % (2 + qq)], ["rden%d" % qq])
                            P.op("dve", lambda e, qq=qq: e.scalar_tensor_tensor(out=gfb[:, qq, :], in0=accb[qq][:, 0:128], scalar=rden[:, qq:qq + 1], in1=fzg[:, qq, :], op0=ALU.mult, op1=ALU.mult),
                                 ["pb%d" % (2 + qq), "rden%d" % qq, "fzg"], ["gfb%d" % qq])
                            dma(GH[r0:r0 + 128, 512 + h * 128:512 + (h + 1) * 128], gfb[:, qq, :], ["gfb%d" % qq], ["GH"], "gfst%d" % qq)

                gbase = h * len(steps)
                NS = len(steps)
                LOOK = 3
                for n in range(min(LOOK, NS)):
                    emit_qk(n)
                for n in range(NS):
                    emit_soft(n)
                    if n + LOOK < NS:
                        emit_qk(n + LOOK)
                    emit_pv(n)
            P.barrier()

        def phase_c(G_d, wo_d, res_d, dst_d, layer):
            A.reset()
            wst = [A.alloc([D], F32), A.alloc([D], F32)]
            wo = A.alloc([8, D], BF16)
            load_weights(wo_d, D, wo, "wo", wst)
            gt = [A.alloc([D], BF16), A.alloc([D], BF16)]
            gT = [A.alloc([8, 128], BF16), A.alloc([8, 128], BF16)]
            xr = [A.alloc([D], F32), A.alloc([D], F32)]
            z = [A.alloc([D], F32), A.alloc([D], F32)]
            st6 = A.alloc([2, 6], F32)
            mv = A.alloc([2], F32)
            rs = A.alloc([1], F32)
            nmr = A.alloc([1], F32)
            tpb = [banks[0], banks[1]]
            yb = [[banks[2], banks[3]], [banks[4], banks[5]]]
            g_ = A.alloc([D], F32)
            b_ = A.alloc([D], F32)
            dma(g_, lng_d[:, layer * D:(layer + 1) * D], [], ["lng"], "cst")
            dma(b_, lnb_d[:, layer * D:(layer + 1) * D], [], ["lnb"], "cst")
            def loads(i):
                s = i % 2
                r0 = i * 128
                dma(gt[s], G_d[r0:r0 + 128, :], [], ["gt%d" % s], "cg%d" % s)
                dma(xr[s], res_d[r0:r0 + 128, :], [], ["xr%d" % s], "cx%d" % s)

            loads(0)
            for i in range(NT):
                s = i % 2
                r0 = i * 128
                if i + 1 < NT:
                    loads(i + 1)
                tv = tpb[s][:, 0:512].bitcast(BF16).rearrange("p (k t) -> p k t", k=8)
                for k in range(8):
                    P.op("pe", lambda e, k=k, s=s, tv=tv: e.transpose(out=tv[:, k, :], in_=gt[s][:, k * 128:(k + 1) * 128], identity=identb),
                         ["gt%d" % s, "identb"], ["pb%d" % s])
                P.op("act", lambda e, s=s, tv=tv: e.copy(out=gT[s], in_=tv), ["pb%d" % s], ["gT%d" % s])
                for c in range(2):
                    for k in range(8):
                        P.op("pe", lambda e, k=k, c=c, s=s: e.matmul(yb[s][c][:, :], lhsT=gT[s][:, k, :], rhs=wo[:, k, c * 512:(c + 1) * 512], start=(k == 0), stop=(k == 7)),
                             ["gT%d" % s, "wo"], ["pb%d" % (2 + 2 * s + c)])
                    P.op("dve", lambda e, c=c, s=s: e.scalar_tensor_tensor(out=z[s][:, c * 512:(c + 1) * 512], in0=xr[s][:, c * 512:(c + 1) * 512], scalar=ALPHA,
                                                                             in1=yb[s][c][:, :], op0=ALU.mult, op1=ALU.add),
                         ["xr%d" % s, "pb%d" % (2 + 2 * s + c)], ["z%d" % s])
                    P.op("dve", lambda e, c=c, s=s: e.bn_stats(out=st6[:, c, :], in_=z[s][:, c * 512:(c + 1) * 512]), ["z%d" % s], ["st6"])
                P.op("dve", lambda e: e.bn_aggr(out=mv, in_=st6.rearrange("p a b -> p (a b)")), ["st6"], ["mv"])
                P.op("dve", lambda e: e.tensor_scalar(out=rs, in0=mv[:, 1:2], scalar1=LN_EPS, scalar2=None, op0=ALU.add), ["mv"], ["rs"])
                P.op("act", lambda e: e.activation(out=rs, in_=rs, func=AF.Sqrt), ["rs"], ["rs"])
                P.op("dve", lambda e: e.reciprocal(out=rs, in_=rs), ["rs"], ["rs"])
                P.op("dve", lambda e: e.scalar_tensor_tensor(out=nmr, in0=mv[:, 0:1], scalar=-1.0, in1=rs, op0=ALU.mult, op1=ALU.mult), ["mv", "rs"], ["nmr"])
                P.op("act", lambda e, s=s: e.activation(out=z[s], in_=z[s], func=AF.Identity, scale=rs, bias=nmr), ["z%d" % s, "rs", "nmr"], ["z%d" % s])
                P.op("dve", lambda e, s=s: e.tensor_tensor(out=z[s], in0=z[s], in1=g_, op=ALU.mult), ["z%d" % s, "lng"], ["z%d" % s])
                P.op("pool", lambda e, s=s: e.tensor_tensor(out=z[s], in0=z[s], in1=b_, op=ALU.add), ["z%d" % s, "lnb"], ["z%d" % s])
                dma(dst_d[r0:r0 + 128, :], z[s], ["z%d" % s], ["DST"], "co%d" % s, eng="pool")
            P.barrier()


        def phase_a1():
            A.reset()
            big1 = A.alloc([6144], F32)
            wst = [big1[:, 0:3072], big1[:, 3072:6144]]
            wf = A.alloc([8, 1032], BF16)
            wt = A.alloc([8, 3072], BF16)
            load_weights(wf1_d, 1032, wf, "wf", wst)
            load_weights(wt1_d, 3072, wt, "wt", wst)
            P.barrier()
            so = big1[:, 0:2048].rearrange("p (a b) -> p a b", a=2)
            gz1 = big1[:, 2048:4096].rearrange("p (a b) -> p a b", a=2)
            ho = big1[:, 4096:5120]
            sil = big1[:, 5120:5632]
            cacc2 = [big1[:, 5632:6144], A.alloc([512], F32)]
            xin = A.alloc([2, D], F32)
            xb = A.alloc([2, D], BF16)
            xT = A.alloc([8, 512], BF16)
            ng1 = A.alloc([D], F32)
            cw = A.alloc([32], F32)
            cb = A.alloc([8], F32)
            bi = A.alloc([1], F32, parts=4)
            bfm = A.alloc([1], F32, parts=4)
            dma(ng1, ng1_d, [], ["ng1"], "cst")
            dma(cw, cw_d, [], ["cw"], "cst")
            dma(cb, cb_d, [], ["cb"], "cst")
            dma(bi, bi_d, [], ["bi"], "cst")
            dma(bfm, bfm_d, [], ["bfm"], "cst")
            uq = A.alloc([8, 515], F32)
            sq2 = [A.alloc([512], F32), A.alloc([512], F32)]
            qk = A.alloc([8, 512], BF16)
            vf = A.alloc([2, D], BF16)
            ig = A.alloc([512], F32, parts=4)
            lg = A.alloc([512], F32, parts=4)
            Bc = A.alloc([512], F32, parts=4)
            aa = A.alloc([512], F32, parts=4)
            RR = A.alloc([512], F32, parts=4)
            Rpe = A.alloc([4], F32, parts=4)
            ea = A.alloc([512], F32, parts=4)
            er = A.alloc([512], F32, parts=4)
            th = A.alloc([512], F32, parts=4)
            cB = A.alloc([1], F32, parts=4)
            cR = A.alloc([1], F32, parts=4)
            gtok = A.alloc([4, 12], F32)
            dRb = A.alloc([16], F32)
            dRl = A.alloc([4], F32)
            PTm = [A.alloc([128], BF16) for _ in range(4)]
            ktk = [A.alloc([128], BF16) for _ in range(4)]
            vau = [A.alloc([258], BF16) for _ in range(4)]
            Cst = A.alloc([4, 257], F32)
            Cbf = A.alloc([4, 258], BF16)
            junk = A.alloc([256], F32)
            ssq = A.alloc([4], F32)
            rstd = A.alloc([4], F32)
            t1 = A.alloc([4], F32)
            rsv = A.alloc([4], F32)
            g1b = A.alloc([D], BF16)
            P.op("pool", lambda e: e.memset(Cst, 0.0), [], ["Cst0", "Cst1", "Cst2", "Cst3"])
            P.op("pool", lambda e: e.memset(Cbf, 0.0), [], ["Cbf0", "Cbf1", "Cbf2", "Cbf3"])
            P.op("pool", lambda e: e.memset(uq, 0.0), [], ["uq%d" % b for b in range(8)])
            P.op("pool", lambda e: e.memset(dRl, 1.0), [], ["dRl"])
            P.op("pool", lambda e: e.memset(cB, 0.0), [], ["cB"])
            P.op("pool", lambda e: e.memset(cR, 0.0), [], ["cR"])
            tpb = [banks[0]]
            prj = [banks[1], banks[2]]
            fmb = prj
            tmb = prj
            mA = banks[0]
            HB4 = [banks[3], banks[4], banks[5], banks[6]]
            UB = banks[7]
            SC = 128 ** -0.5
            pji = [0]

            class _Ctr:
                pass
            def conv_block(b):
                bk = prj[pji[0] % len(prj)]
                bn = BN[id(bk)]
                pji[0] += 1
                cacc = cacc2[b % 2]
                sq = sq2[b % 2]
                cn = "cacc%d" % (b % 2)
                sn_ = "sq%d" % (b % 2)
                fm_mm(bk, bn, 128, wf, "wf", b * 128, xT)
                P.op("act", lambda e: e.copy(out=uq[:, b, 3:515], in_=bk[:, :]), [bn], ["uq%d" % b])
                P.op("dve", lambda e: e.tensor_scalar(out=cacc, in0=uq[:, b, 0:512], scalar1=cw[:, 4 * b:4 * b + 1], scalar2=cb[:, b:b + 1], op0=ALU.mult, op1=ALU.add),
                     ["uq%d" % b, "cw", "cb"], [cn])
                for j in range(1, 4):
                    P.op("dve", lambda e, j=j: e.scalar_tensor_tensor(out=cacc, in0=uq[:, b, j:j + 512], scalar=cw[:, 4 * b + j:4 * b + j + 1], in1=cacc, op0=ALU.mult, op1=ALU.add),
                         ["uq%d" % b, "cw", cn], [cn])
                P.op("pool", lambda e: e.tensor_copy(out=uq[:, b, 0:3], in_=uq[:, b, 512:515]), ["uq%d" % b], ["uq%d" % b])
                if b < 4:
                    P.op("act", lambda e: e.activation(out=sq, in_=cacc, func=AF.Silu), [cn], [sn_])
                    P.op("act", lambda e: e.mul(out=qk[:, b, :], in_=sq, mul=SC), [sn_], ["qk%d" % b])
                else:
                    P.op("act", lambda e: e.activation(out=qk[:, b, :], in_=cacc, func=AF.Silu), [cn], ["qk%d" % b])

            def a1_tile(t0, q):
                if True:
                    r0 = t0 + q * 128
                    qs = slice(q * 128, (q + 1) * 128)
                    v2 = q % 2
                    for c in range(6):
                        bk = prj[pji[0] % len(prj)]
                        bn = BN[id(bk)]
                        pji[0] += 1
                        tm_mm(bk, bn, q, wt, "wt", c * 512, 512, xT)
                        cs = slice((c % 2) * 512, (c % 2) * 512 + 512)
                        if c < 2:
                            P.op("act", lambda e, v2=v2, bk=bk, cs=cs: e.copy(out=vf[:, v2, cs], in_=bk[:, :]), [bn], ["vf%d" % v2])
                        elif c < 4:
                            P.op("act", lambda e, v2=v2, bk=bk, cs=cs: e.activation(out=so[:, v2, cs], in_=bk[:, :], func=AF.Sigmoid), [bn], ["so%d" % v2])
                        else:
                            P.op("act", lambda e, bk=bk: e.activation(out=sil, in_=bk[:, :], func=AF.Silu), [bn], ["sil"])
                            P.op("dve", lambda e, v2=v2, cs=cs: e.tensor_tensor(out=gz1[:, v2, cs], in0=sil, in1=ng1[:, cs], op=ALU.mult), ["sil", "ng1"], ["gz1_%d" % v2])
                    for pair in ((0, 1, 2, 3),):
                        for h in pair:
                            bA = HB4[h]
                            nA = BN[id(bA)]
                            kT_ps = bA[:, 128:192].bitcast(BF16)
                            P.op("pe", lambda e, h=h, bA=bA: e.matmul(bA[:, 0:128], lhsT=qk[:, 4 + h, qs], rhs=qk[:, h, qs], start=True, stop=True),
                                 ["qk%d" % h, "qk%d" % (4 + h)], [nA])
                            P.op("pe", lambda e, h=h, kT_ps=kT_ps: e.transpose(out=kT_ps, in_=qk[:, 4 + h, qs], identity=identb), ["qk%d" % (4 + h), "identb"], [nA])
                        for h in pair:
                            bA = HB4[h]
                            nA = BN[id(bA)]
                            kT_ps = bA[:, 128:192].bitcast(BF16)
                            hv = slice(h * 256, (h + 1) * 256)
                            P.op("dve", lambda e, h=h, bA=bA: e.tensor_tensor(out=PTm[h], in0=bA[:, 0:128], in1=triu, op=ALU.mult), [nA, "triu"], ["PTm%d" % h])
                            P.op("act", lambda e, h=h, kT_ps=kT_ps: e.copy(out=ktk[h], in_=kT_ps), [nA], ["ktk%d" % h])
                            P.op("pool", lambda e, h=h, hv=hv: e.tensor_scalar(out=vau[h][:, 0:256], in0=vf[:, v2, hv], scalar1=gtok[:, q, h:h + 1], scalar2=1.0, op0=ALU.mult, op1=ALU.mult),
                                 ["vf%d" % v2, "gtok"], ["vau%d" % h])
                            P.op("pool", lambda e, h=h: e.tensor_copy(out=vau[h][:, 256:257], in_=gtok[:, q, h:h + 1]), ["gtok", "vau%d" % h], ["vau%d" % h])
                        for h in pair:
                            bA = HB4[h]
                            bB = UB
                            nA = BN[id(bA)]
                            nB = BN[id(bB)]
                            s2 = h
                            P.op("pe", lambda e, s2=s2, bA=bA: e.matmul(bA[:, 192:449], lhsT=PTm[s2], rhs=vau[s2][:, 0:257], start=True, stop=False), ["PTm%d" % s2, "vau%d" % s2], [nA])
                            P.op("pe", lambda e, h=h, bA=bA: e.matmul(bA[:, 192:449], lhsT=qk[:, h, qs], rhs=Cbf[:, h, 0:257], start=False, stop=True), ["qk%d" % h, "Cbf%d" % h], [nA])
                            P.op("pe", lambda e, s2=s2, bB=bB: e.matmul(bB[:, 0:257], lhsT=ktk[s2], rhs=vau[s2][:, 0:257], start=True, stop=True), ["ktk%d" % s2, "vau%d" % s2], [nB])
                            dprev = dRl[:, h:h + 1] if q == 0 else dRb[:, 4 * h + q - 1:4 * h + q]
                            P.op("dve", lambda e, h=h, bB=bB, dprev=dprev: e.scalar_tensor_tensor(out=Cst[:, h, :], in0=Cst[:, h, :], scalar=dprev, in1=bB[:, 0:257], op0=ALU.mult, op1=ALU.add),
                                 ["Cst%d" % h, nB, "dRb", "dRl"], ["Cst%d" % h])
                        for h in pair:
                            bA = HB4[h]
                            bB = UB
                            nA = BN[id(bA)]
                            nB = BN[id(bB)]
                            hv = slice(h * 256, (h + 1) * 256)
                            P.op("act", lambda e, h=h: e.activation(out=Cbf[:, h, 0:257], in_=Cst[:, h, :], func=AF.Copy, scale=dRb[:, 4 * h + q:4 * h + q + 1]),
                                 ["Cst%d" % h, "dRb"], ["Cbf%d" % h])
                            P.op("dve", lambda e, h=h, bA=bA: e.tensor_scalar(out=t1[:, h:h + 1], in0=bA[:, 448:449], scalar1=gtok[:, q, 4 + h:5 + h], scalar2=None, op0=ALU.mult),
                                 [nA, "gtok"], ["t1_%d" % h])
                            P.op("dve", lambda e, h=h: e.scalar_tensor_tensor(out=t1[:, h:h + 1], in0=t1[:, h:h + 1], scalar=-1.0, in1=t1[:, h:h + 1], op0=ALU.mult, op1=ALU.max),
                                 ["t1_%d" % h], ["t1_%d" % h])
                            P.op("dve", lambda e, h=h: e.tensor_tensor(out=t1[:, h:h + 1], in0=t1[:, h:h + 1], in1=gtok[:, q, 8 + h:9 + h], op=ALU.max), ["t1_%d" % h, "gtok"], ["t1_%d" % h])
                            P.op("dve", lambda e, h=h: e.reciprocal(out=t1[:, h:h + 1], in_=t1[:, h:h + 1]), ["t1_%d" % h], ["t1_%d" % h])
                            P.op("dve", lambda e, h=h: e.tensor_tensor(out=rsv[:, h:h + 1], in0=t1[:, h:h + 1], in1=gtok[:, q, 4 + h:5 + h], op=ALU.mult), ["t1_%d" % h, "gtok"], ["rsv%d" % h])
                            P.op("dve", lambda e, h=h, hv=hv, bA=bA: e.scalar_tensor_tensor(out=ho[:, hv], in0=bA[:, 192:448], scalar=rsv[:, h:h + 1], in1=so[:, v2, hv], op0=ALU.mult, op1=ALU.mult),
                                 [nA, "rsv%d" % h, "so%d" % v2], ["ho%d" % h])
                            P.op("act", lambda e, h=h, hv=hv: e.activation(out=junk, in_=ho[:, hv], func=AF.Square, accum_out=ssq[:, h:h + 1]), ["ho%d" % h], ["junk", "ssq"])
                    P.op("dve", lambda e: e.tensor_scalar(out=rstd, in0=ssq, scalar1=1.0 / 256, scalar2=RMS_EPS, op0=ALU.mult, op1=ALU.add), ["ssq"], ["rstd"])
                    P.op("pool", lambda e: e.tensor_tensor(out=rstd, in0=rstd, in1=mhalf, op=ALU.pow), ["rstd", "mhalf"], ["rstd"])
                    for h in range(4):
                        hv = slice(h * 256, (h + 1) * 256)
                        P.op("dve", lambda e, h=h, hv=hv: e.scalar_tensor_tensor(out=g1b[:, hv], in0=ho[:, hv], scalar=rstd[:, h:h + 1], in1=gz1[:, v2, hv], op0=ALU.mult, op1=ALU.mult),
                             ["ho%d" % h, "rstd", "gz1_%d" % v2], ["g1b"])
                    dma(G1[r0:r0 + 128, :], g1b, ["g1b"], ["G1"], "g1st")
            fmi = 0
            tmi = 0
            for st in range(NST):
                front(H1, st, xin, xb, xT, tpb)
                t0 = st * 512
                for gi in range(2):
                    bk = prj[pji[0] % len(prj)]
                    bn = BN[id(bk)]
                    pji[0] += 1
                    fm_mm(bk, bn, 4, wf, "wf", 1024 + 4 * gi, xT)
                    if gi == 0:
                        P.op("act", lambda e, bk=bk: e.activation(out=ig, in_=bk[0:4, :], func=AF.Identity, bias=bi), [bn, "bi"], ["ig"])
                    else:
                        P.op("act", lambda e, bk=bk: e.activation(out=lg, in_=bk[0:4, :], func=AF.Sigmoid, bias=bfm), [bn, "bfm"], ["lg"])
                        P.op("act", lambda e: e.activation(out=lg, in_=lg, func=AF.Ln), ["lg"], ["lg"])
                P.op("dve", lambda e: e.tensor_tensor_scan(out=Bc, data0=ones[0:4, :], data1=lg, initial=cB, op0=ALU.mult, op1=ALU.add), ["lg", "ones", "cB"], ["Bc"])
                P.op("dve", lambda e: e.tensor_tensor(out=aa, in0=ig, in1=Bc, op=ALU.subtract), ["ig", "Bc"], ["aa"])
                P.op("pool", lambda e: e.tensor_copy(out=Rpe[:, 0:1], in_=cR), ["cR"], ["Rpe"])
                P.op("dve", lambda e: e.tensor_tensor_scan(out=RR, data0=aa, data1=aa, initial=cR, op0=ALU.max, op1=ALU.max), ["aa", "cR"], ["RR"])
                P.op("pool", lambda e: e.tensor_copy(out=Rpe[:, 1:4], in_=RR[:, 127:384:128]), ["RR"], ["Rpe"])
                P.op("pool", lambda e: e.tensor_copy(out=cB, in_=Bc[:, 511:512]), ["Bc"], ["cB"])
                P.op("pool", lambda e: e.tensor_copy(out=cR, in_=RR[:, 511:512]), ["RR", "Rpe"], ["cR"])
                rpb = Rpe.unsqueeze(2).to_broadcast([4, 4, 128])
                v3 = lambda a: a.rearrange("p (q t) -> p q t", q=4)
                P.op("dve", lambda e: e.tensor_tensor(out=v3(ea), in0=v3(aa), in1=rpb, op=ALU.subtract), ["aa", "Rpe"], ["ea"])
                P.op("act", lambda e: e.activation(out=ea, in_=ea, func=AF.Exp), ["ea"], ["ea"])
                P.op("dve", lambda e: e.tensor_tensor(out=v3(er), in0=v3(RR), in1=rpb, op=ALU.subtract), ["RR", "Rpe"], ["er"])
                P.op("act", lambda e: e.activation(out=er, in_=er, func=AF.Exp, scale=-1.0), ["er"], ["er"])
                P.op("dve", lambda e: e.tensor_tensor(out=th, in0=Bc, in1=RR, op=ALU.add), ["Bc", "RR"], ["th"])
                P.op("act", lambda e: e.activation(out=th, in_=th, func=AF.Exp, scale=-1.0), ["th"], ["th"])
                gtp = mA[:, 448:496].rearrange("p (q c) -> p q c", q=4)
                for q in range(4):
                    for gi, (src, sn) in enumerate(((ea, "ea"), (er, "er"), (th, "th"))):
                        P.op("pe", lambda e, q=q, gi=gi, src=src: e.transpose(out=gtp[:, q, gi * 4:(gi + 1) * 4], in_=src[:, q * 128:(q + 1) * 128], identity=identf[0:4, 0:4]),
                             [sn, "identf"], ["pb0"])
                P.op("act", lambda e: e.copy(out=gtok, in_=gtp), ["pb0"], ["gtok"])
                for h in range(4):
                    P.op("pe", lambda e, h=h: e.matmul(mA[:, 496 + 4 * h:500 + 4 * h], lhsT=sel[:, h * 128:(h + 1) * 128], rhs=er[:, 127:512:128], start=True, stop=True),
                         ["sel", "er"], ["pb0"])
                if st > 0:
                    P.op("pool", lambda e: e.tensor_copy(out=dRl, in_=dRb[:, 3:16:4]), ["dRb"], ["dRl"])
                P.op("act", lambda e: e.copy(out=dRb, in_=mA[:, 496:512]), ["pb0"], ["dRb"])
                for b in range(8):
                    conv_block(b)
                for q in range(4):
                    a1_tile(t0, q)

            P.barrier()

        if "A0" in phases:
            phase_a0()
        if "B0" in phases:
            phase_b0()
        if "C0" in phases:
            phase_c(GH, wo0_d, x_d, H1, 0)
        if "A1" in phases:
            phase_a1()
        if "C1" in phases:
            phase_c(G1, wo1_d, H1, out_d, 1)

        P.finish(["DST", "GH", "G1", "FQ", "FK", "FV", "FZ", "CT"])
        P.emit()
        build.stats = P.stats()
    return nc


def host_consts():
    s = np.arange(128)[:, None]
    t = np.arange(128)[None, :]
    triu = (s <= t).astype(np.float32)
    dmask = np.where(s <= t, 0.0, -1.0e9).astype(np.float32)
    rmask = np.ones((128, 512), np.float32)
    rmask[:, 0::128] = 0.0
    sel = np.zeros((4, 512), np.float32)
    for h in range(4):
        sel[h, h * 128:(h + 1) * 128] = 1.0
    return {"ident": np.eye(128, dtype=np.float32), "triu": triu, "dmask": dmask, "rmask": rmask, "sel": sel}


def host_params(hgrn_lb_logits, ab_w_in, ab_fox_bf, ab_hgrn_norm_g, ab_w_out, c_w_in,
                c_conv_w, c_conv_b, c_bi, c_bf, c_norm_g, c_w_out, ln_g, ln_b):
    f = np.float32
    w0 = np.asarray(ab_w_in[0], f)
    sl = lambda a, b: w0[:, a:b]
    wf0 = np.concatenate([sl(0, 512), sl(512, 1024), sl(2048, 2560), sl(2560, 3072), sl(3584, 3588)], axis=1)
    wt0 = np.concatenate([sl(1024, 1536), sl(1536, 2048), sl(3072, 3584), sl(3588, 4100)], axis=1)
    w1 = np.asarray(c_w_in[0], f)
    wf1 = np.concatenate([w1[:, 0:512], w1[:, 512:1024], w1[:, 2048:2052], w1[:, 2052:2056]], axis=1)
    wt1 = np.concatenate([w1[:, 1024:2048], w1[:, 2056:3080], w1[:, 3080:4104]], axis=1)
    lbl = np.asarray(hgrn_lb_logits, f).reshape(3, 4, 128).transpose(2, 1, 0).reshape(128, 12)
    rep = lambda v: np.ascontiguousarray(np.broadcast_to(np.asarray(v, f).reshape(1, -1), (128, np.asarray(v).size)))
    cw = np.asarray(c_conv_w[0], f).reshape(4, 8, 128).transpose(2, 1, 0).reshape(128, 32)
    cb = np.asarray(c_conv_b[0], f).reshape(8, 128).T
    d = {
        "wf0": wf0, "wt0": wt0, "wo0": np.asarray(ab_w_out[0], f),
        "wf1": wf1, "wt1": wt1, "wo1": np.asarray(c_w_out[0], f),
        "lbl": lbl, "fbf": np.asarray(ab_fox_bf[0], f).reshape(4, 1),
        "hng": rep(ab_hgrn_norm_g[0]), "lng": rep(np.asarray(ln_g, f).reshape(-1)), "lnb": rep(np.asarray(ln_b, f).reshape(-1)),
        "cw": cw, "cb": cb, "bi": np.asarray(c_bi[0], f).reshape(4, 1), "bfm": np.asarray(c_bf[0], f).reshape(4, 1),
        "ng1": rep(c_norm_g[0]),
    }
    d.update(host_consts())
    return {k: np.ascontiguousarray(v, dtype=f) for k, v in d.items()}


_NC_CACHE = {}


def kernel(x, hgrn_lb_logits, ab_w_in, ab_fox_bf, ab_hgrn_norm_g, ab_w_out, c_w_in,
           c_conv_w, c_conv_b, c_bi, c_bf, c_norm_g, c_w_out, ln_g, ln_b):
    x = np.asarray(x, np.float32)
    B, T, _ = x.shape
    params = host_params(hgrn_lb_logits, ab_w_in, ab_fox_bf, ab_hgrn_norm_g, ab_w_out, c_w_in,
                         c_conv_w, c_conv_b, c_bi, c_bf, c_norm_g, c_w_out, ln_g, ln_b)
    if T not in _NC_CACHE:
        _NC_CACHE[T] = build(T)
    nc = _NC_CACHE[T]
    owner = {2 * b: b for b in range(B)}
    zx = np.zeros((T, D), np.float32)
    in_maps = []
    for c in range(8):
        m = dict(params)
        m["x"] = np.ascontiguousarray(x[owner[c]]) if c in owner else zx
        in_maps.append(m)
    res = run_bass_kernel_spmd(nc, in_maps, core_ids=list(range(8)))
    return np.stack([np.asarray(res.results[2 * b]["out"], np.float32) for b in range(B)], axis=0)
```

```python
import numpy as np
from contextlib import ExitStack
import concourse.bass as bass
import concourse.mybir as mybir
from concourse.bass_utils import run_bass_kernel_spmd

F32 = mybir.dt.float32
BF16 = mybir.dt.bfloat16
AF = mybir.ActivationFunctionType
ALU = mybir.AluOpType

D = 1024
ALPHA = 4.0 ** 0.25
LN_EPS = 1e-5
RMS_EPS = 1e-6


class _Op:
    __slots__ = ("fn", "waits", "inc", "dma", "val")

    def __init__(self, fn, waits, dma):
        self.fn = fn
        self.waits = waits
        self.inc = False
        self.dma = dma
        self.val = 0


class Prog:
    ENGS = ("pe", "act", "dve", "pool", "sp")

    def __init__(self, nc, es):
        self.nc = nc
        self.es = es
        self.ops = {e: [] for e in self.ENGS}
        self.lastw = {}
        self.readers = {}
        self.known = {e: {} for e in self.ENGS}
        self.dma_count = {}

    def op(self, eng, fn, reads=(), writes=(), dma=None):
        writes = list(writes) + [r for r in reads if r.startswith("pb") and r not in writes]
        reads = [r for r in reads if not r.startswith("pb")]
        if dma is not None:
            cnt = self.dma_count.get(dma, 0) + 1
            self.dma_count[dma] = cnt
            me = ("dma:" + dma, cnt)
        else:
            me = (eng, len(self.ops[eng]))
        deps = []
        for r in reads:
            w = self.lastw.get(r)
            if w is not None:
                deps.append(w)
        for r in writes:
            w = self.lastw.get(r)
            if w is not None and not (w[0] == me[0] == "pe"):
                deps.append(w)
            for rn, ri in self.readers.get(r, {}).items():
                if rn != me[0] or dma is not None:
                    deps.append((rn, ri))
        kn = self.known[eng]
        wm = {}
        for name, idx in deps:
            if name.startswith("dma:"):
                idx = self.dma_count[name[4:]] - (1 if (dma is not None and name[4:] == dma) else 0)
            if kn.get(name, -1) >= idx:
                continue
            kn[name] = idx
            wm[name] = max(wm.get(name, -1), idx)
            if not name.startswith("dma:"):
                self.ops[name][idx].inc = True
        self.ops[eng].append(_Op(fn, list(wm.items()), dma))
        for r in reads:
            self.readers.setdefault(r, {})[me[0]] = me[1]
        for r in writes:
            self.lastw[r] = me
            self.readers[r] = {}
        return me

    def finish(self, resources):
        self.op("sp", None, reads=list(resources))

    def barrier(self):
        lasts = {}
        for e in self.ENGS:
            for i in range(len(self.ops[e]) - 1, -1, -1):
                if self.ops[e][i].fn is not None and self.ops[e][i].dma is None:
                    lasts[e] = i
                    break
        for e in self.ENGS:
            waits = []
            kn = self.known[e]
            for s_, idx in lasts.items():
                if s_ != e and kn.get(s_, -1) < idx:
                    waits.append((s_, idx))
                    kn[s_] = idx
                    self.ops[s_][idx].inc = True
            for k, cnt in self.dma_count.items():
                if kn.get("dma:" + k, -1) < cnt:
                    waits.append(("dma:" + k, cnt))
                    kn["dma:" + k] = cnt
            self.ops[e].append(_Op(None, waits, None))

    SEM_LIM = 4000

    def emit(self):
        nc = self.nc
        ops = self.ops
        LIM = self.SEM_LIM
        nep = {}
        for e in self.ENGS:
            c = 0
            for o in ops[e]:
                if o.inc:
                    c += 1
                o.val = c
            nep[e] = (c + LIM - 1) // LIM + 1
        sems = {e: [self.es.enter_context(nc.semaphore("s_%s%d" % (e, i))) for i in range(nep[e])] for e in self.ENGS}
        dsems = {k: self.es.enter_context(nc.semaphore("d_" + k)) for k in self.dma_count}
        for k, cnt in self.dma_count.items():
            assert 16 * cnt < 2 * LIM, (k, cnt)

        def run(ename, engine):
            for o in ops[ename]:
                for name, idx in o.waits:
                    if name.startswith("dma:"):
                        engine.wait_ge(dsems[name[4:]], 16 * idx)
                    else:
                        tgt = ops[name][idx]
                        assert tgt.inc and tgt.val > 0
                        engine.wait_ge(sems[name][(tgt.val - 1) // LIM], (tgt.val - 1) % LIM + 1)
                if o.fn is None:
                    continue
                ins = o.fn(engine)
                if o.dma is not None:
                    ins.then_inc(dsems[o.dma], 16)
                elif o.inc:
                    ins.then_inc(sems[ename][(o.val - 1) // LIM], 1)

        with nc.Block() as block:
            @block.tensor
            def _(e):
                run("pe", e)

            @block.scalar
            def _(e):
                run("act", e)

            @block.vector
            def _(e):
                run("dve", e)

            @block.gpsimd
            def _(e):
                run("pool", e)

            @block.sync
            def _(e):
                run("sp", e)

    def stats(self):
        return {e: (len(self.ops[e]), sum(1 for o in self.ops[e] if o.inc)) for e in self.ENGS}


class Arena:
    def __init__(self, t, nwords):
        self.t = t
        self.n = nwords
        self.off = 0
        self.base = 0

    def alloc(self, free_shape, dt, parts=128):
        n = int(np.prod(free_shape))
        words = n if dt == F32 else (n + 1) // 2
        words = (words + 7) // 8 * 8
        assert self.off + words <= self.n, ("SBUF arena overflow", self.off, words, self.n)
        ap = self.t[0:parts, self.off:self.off + words]
        self.off += words
        if dt != F32:
            ap = ap.bitcast(dt)
        ap = ap[:, 0:n]
        if len(free_shape) == 2:
            ap = ap.rearrange("p (a b) -> p a b", a=free_shape[0])
        elif len(free_shape) == 3:
            ap = ap.rearrange("p (a b c) -> p a b c", a=free_shape[0], b=free_shape[1])
        return ap

    def mark(self):
        self.base = self.off

    def reset(self):
        self.off = self.base


def build(T, dbg=False, phases=("A0", "B0", "C0", "A1", "C1")):
    NT = T // 128
    NST = T // 512
    nc = bass.Bass("TRN2", target_bir_lowering=False)
    din = lambda name, shape, dt=F32: nc.dram_tensor(name, shape, dt, kind="ExternalInput").ap()
    dsc = lambda name, shape, dt: nc.dram_tensor(name, shape, dt, kind="ExternalOutput" if dbg else "Internal").ap()
    x_d = din("x", [T, D])
    wf0_d = din("wf0", [D, 2052])
    wt0_d = din("wt0", [D, 2048])
    wo0_d = din("wo0", [D, D])
    wf1_d = din("wf1", [D, 1032])
    wt1_d = din("wt1", [D, 3072])
    wo1_d = din("wo1", [D, D])
    lbl_d = din("lbl", [128, 12])
    fbf_d = din("fbf", [4, 1])
    hng_d = din("hng", [128, 512])
    lng_d = din("lng", [128, 2 * D])
    lnb_d = din("lnb", [128, 2 * D])
    cw_d = din("cw", [128, 32])
    cb_d = din("cb", [128, 8])
    bi_d = din("bi", [4, 1])
    bfm_d = din("bfm", [4, 1])
    ng1_d = din("ng1", [128, D])
    ident_d = din("ident", [128, 128])
    triu_d = din("triu", [128, 128])
    dmask_d = din("dmask", [128, 128])
    rmask_d = din("rmask", [128, 512])
    sel_d = din("sel", [4, 512])
    out_d = nc.dram_tensor("out", [T, D], F32, kind="ExternalOutput").ap()
    GH = dsc("GH", [T, D], BF16)
    FQ = dsc("FQ", [4, 128, T], BF16)
    FK = dsc("FK", [4, 128, T], BF16)
    FV = dsc("FV", [4, T, 128], BF16)
    FZ = dsc("FZ", [T, 512], BF16)
    CT = dsc("CT", [4, T], F32)
    H1 = dsc("H1", [T, D], F32)
    G1 = dsc("G1", [T, D], BF16)

    with ExitStack() as es:
        P = Prog(nc, es)
        AW = 50000
        arena_t = es.enter_context(nc.sbuf_tensor("arena", [128, AW], F32))
        A = Arena(arena_t, AW)
        banks = [es.enter_context(nc.psum_tensor("pb%d" % i, [128, 512], F32)) for i in range(8)]
        BN = {id(b): "pb%d" % i for i, b in enumerate(banks)}

        ucnt = [0]

        def dma(out, in_, reads, writes, key, eng="sp"):
            if key == "cst":
                key = "cst%d" % ucnt[0]
                ucnt[0] += 1
            P.op(eng, lambda e: e.dma_start(out=out, in_=in_), reads=reads, writes=writes, dma=key)

        identf = A.alloc([128], F32)
        identb = A.alloc([128], BF16)
        triu = A.alloc([128], F32)
        dmask = A.alloc([128], F32)
        rmask = A.alloc([512], F32)
        ones = A.alloc([512], F32)
        sel = A.alloc([512], F32, parts=4)
        dma(identf, ident_d, [], ["identf"], "cst")
        dma(triu, triu_d, [], ["triu"], "cst")
        dma(dmask, dmask_d, [], ["dmask"], "cst")
        dma(rmask, rmask_d, [], ["rmask"], "cst")
        dma(sel, sel_d, [], ["sel"], "cst")
        P.op("pool", lambda e: e.tensor_copy(out=identb, in_=identf), ["identf"], ["identb"])
        P.op("pool", lambda e: e.memset(ones, 1.0), [], ["ones"])
        mhalf = A.alloc([4], F32)
        P.op("pool", lambda e: e.memset(mhalf, -0.5), [], ["mhalf"])
        A.mark()

        def load_weights(w_d, ncols, wbf, name, wst):
            for k in range(8):
                s = k % 2
                dma(wst[s][:, 0:ncols], w_d[k * 128:(k + 1) * 128, :], [], ["wst%d" % s], "wld%d" % s)
                eng = ("dve", "act", "pool")[k % 3]
                if eng == "act":
                    P.op("act", lambda e, k=k, s=s: e.copy(out=wbf[:, k, :], in_=wst[s][:, 0:ncols]), ["wst%d" % s], [name])
                else:
                    P.op(eng, lambda e, k=k, s=s: e.tensor_copy(out=wbf[:, k, :], in_=wst[s][:, 0:ncols]), ["wst%d" % s], [name])

        def front(src_d, st, xin, xb, xT, tpb):
            for q in range(4):
                r0 = (st * 4 + q) * 128
                s_ = q % 2
                dma(xin[:, s_, :], src_d[r0:r0 + 128, :], [], ["xin%d" % s_], "xin%d" % s_)
                P.op("pool", lambda e, s_=s_: e.tensor_copy(out=xb[:, s_, :], in_=xin[:, s_, :]), ["xin%d" % s_], ["xb%d" % s_])
                tb = tpb[q % len(tpb)]
                tn = BN[id(tb)]
                tv = tb[:, 0:512].bitcast(BF16).rearrange("p (k t) -> p k t", k=8)
                for k in range(8):
                    P.op("pe", lambda e, s_=s_, k=k, tv=tv: e.transpose(out=tv[:, k, :], in_=xb[:, s_, k * 128:(k + 1) * 128], identity=identb),
                         ["xb%d" % s_, "identb"], [tn])
                P.op("act", lambda e, q=q, tv=tv: e.copy(out=xT[:, :, q * 128:(q + 1) * 128], in_=tv), [tn], ["xT"])

        def fm_mm(bank, bname, M, wbf, wname, c0, xT):
            for k in range(8):
                P.op("pe", lambda e, k=k: e.matmul(bank[0:M, :], lhsT=wbf[:, k, c0:c0 + M], rhs=xT[:, k, :], start=(k == 0), stop=(k == 7)),
                     [wname, "xT"], [bname])

        def tm_mm(bank, bname, q, wbf, wname, c0, n, xT):
            for k in range(8):
                P.op("pe", lambda e, k=k: e.matmul(bank[:, 0:n], lhsT=xT[:, k, q * 128:(q + 1) * 128], rhs=wbf[:, k, c0:c0 + n], start=(k == 0), stop=(k == 7)),
                     [wname, "xT"], [bname])

        def phase_a0():
            A.reset()
            big = A.alloc([4112], F32)
            wst = [big[:, 0:2052], big[:, 2056:2056 + 2052]]
            hqT = big[:, 0:2048].rearrange("p (a b) -> p a b", a=4)
            sg = big[:, 2048:4096].rearrange("p (a b) -> p a b", a=4)
            wf = A.alloc([8, 2052], BF16)
            wt = A.alloc([8, 2048], BF16)
            load_weights(wf0_d, 2052, wf, "wf", wst)
            load_weights(wt0_d, 2048, wt, "wt", wst)
            P.barrier()
            xin = A.alloc([2, D], F32)
            xb = A.alloc([2, D], BF16)
            xT = A.alloc([8, 512], BF16)
            hng = A.alloc([512], F32)
            lbl = A.alloc([12], F32)
            lbe = A.alloc([12], F32)
            lb = A.alloc([4], F32)
            oml = A.alloc([4], F32)
            lbm1 = A.alloc([4], F32)
            fbf = A.alloc([1], F32, parts=4)
            dma(hng, hng_d, [], ["hng"], "cst")
            dma(lbl, lbl_d, [], ["lbl"], "cst")
            dma(fbf, fbf_d, [], ["fbf"], "cst")
            P.op("act", lambda e: e.activation(out=lbe, in_=lbl, func=AF.Exp), ["lbl"], ["lbe"])
            lbe3 = lbe.rearrange("p (h r) -> p h r", h=4)
            P.op("dve", lambda e: e.tensor_tensor(out=lb, in0=lbe3[:, :, 0], in1=lbe3[:, :, 1], op=ALU.add), ["lbe"], ["lbs"])
            P.op("dve", lambda e: e.tensor_tensor(out=lb, in0=lb, in1=lbe3[:, :, 2], op=ALU.add), ["lbe", "lbs"], ["lbs"])
            P.op("dve", lambda e: e.reciprocal(out=lb, in_=lb), ["lbs"], ["lbs"])
            P.op("dve", lambda e: e.tensor_tensor(out=lb, in0=lb, in1=lbe3[:, :, 0], op=ALU.mult), ["lbs", "lbe"], ["lb", "lbs"])
            P.op("dve", lambda e: e.tensor_scalar(out=oml, in0=lb, scalar1=-1.0, scalar2=1.0, op0=ALU.mult, op1=ALU.add), ["lb"], ["oml"])
            P.op("dve", lambda e: e.tensor_scalar(out=lbm1, in0=lb, scalar1=-1.0, scalar2=None, op0=ALU.add), ["lb"], ["lbm1"])

            vb = A.alloc([2, 512], BF16)
            gz = A.alloc([2, 512], F32)
            sil = A.alloc([512], F32)
            fvb = A.alloc([2, 512], BF16)
            fzb = A.alloc([2, 512], BF16)
            gl = A.alloc([4, 512], F32)
            bb = A.alloc([4, 512], F32)
            qdec = A.alloc([4, 512], BF16)
            kinv = A.alloc([4, 512], BF16)
            kend = A.alloc([4, 512], BF16)
            fqb = A.alloc([4, 512], BF16)
            fkb = A.alloc([4, 512], BF16)
            dd = A.alloc([4, 4], F32)
            d1 = A.alloc([4, 4], F32)
            d2 = A.alloc([4, 4], F32)
            d3 = A.alloc([4, 4], F32)
            ffs = A.alloc([512], F32, parts=4)
            ls = A.alloc([512], F32, parts=4)
            cT = A.alloc([512], F32, parts=4)
            ccar = A.alloc([1], F32, parts=4)
            S = A.alloc([4, 128], F32)
            Sbf = A.alloc([4, 128], BF16)
            aTm = [A.alloc([128], BF16), A.alloc([128], BF16)]
            kit = [A.alloc([128], BF16), A.alloc([128], BF16)]
            tmpU = [A.alloc([128], F32), A.alloc([128], F32)]
            junk = A.alloc([128], F32)
            ssq = A.alloc([4], F32)
            rstd = A.alloc([4], F32)
            ghb = A.alloc([512], BF16)
            P.op("pool", lambda e: e.memset(S, 0.0), [], ["S0", "S1", "S2", "S3"])
            tpb = [banks[0]]
            prj = [banks[1], banks[2], banks[3]]
            pji = [0]
            HB = [banks[4], banks[5], banks[6], banks[7]]
            aTm4 = [A.alloc([128], BF16) for _ in range(4)]
            kit4 = [A.alloc([128], BF16) for _ in range(4)]
            tmpU4 = [A.alloc([128], F32) for _ in range(4)]
            junk4 = [A.alloc([128], F32) for _ in range(4)]

            def a0_tile(t0, q):
                r0 = t0 + q * 128
                qs = slice(q * 128, (q + 1) * 128)
                v2 = q % 2
                for c in range(4):
                    bk = prj[pji[0] % 3]
                    bn = BN[id(bk)]
                    pji[0] += 1
                    tm_mm(bk, bn, q, wt, "wt", c * 512, 512, xT)
                    if c == 0:
                        P.op("act", lambda e, bk=bk: e.copy(out=vb[:, v2, :], in_=bk[:, :]), [bn], ["vb%d" % v2])
                    elif c == 1:
                        P.op("act", lambda e, bk=bk: e.activation(out=sil, in_=bk[:, :], func=AF.Silu), [bn], ["sil"])
                        P.op("dve", lambda e: e.tensor_tensor(out=gz[:, v2, :], in0=sil, in1=hng, op=ALU.mult), ["sil", "hng"], ["gz%d" % v2])
                    elif c == 2:
                        P.op("act", lambda e, bk=bk: e.copy(out=fvb[:, v2, :], in_=bk[:, :]), [bn], ["fvb%d" % v2])
                        dma(FV[:, r0:r0 + 128, :].rearrange("h p d -> p h d"), fvb[:, v2, :].rearrange("p (h d) -> p h d", h=4),
                            ["fvb%d" % v2], ["FV"], "fvst%d" % v2)
                    else:
                        P.op("act", lambda e, bk=bk: e.activation(out=fzb[:, v2, :], in_=bk[:, :], func=AF.Silu), [bn], ["fzb%d" % v2])
                        dma(FZ[r0:r0 + 128, :], fzb[:, v2, :], ["fzb%d" % v2], ["FZ"], "fzst%d" % v2)
                hs_ = [slice(h * 128, (h + 1) * 128) for h in range(4)]
                for h in range(4):
                    hb = HB[h]
                    nb = BN[id(hb)]
                    kT_ps = hb[:, 384:448].bitcast(BF16)
                    P.op("pe", lambda e, h=h, hb=hb: e.matmul(hb[:, 0:128], lhsT=kinv[:, h, qs], rhs=qdec[:, h, qs], start=True, stop=True),
                         ["kinv%d" % h, "qdec%d" % h], [nb])
                    P.op("pe", lambda e, h=h, kT_ps=kT_ps: e.transpose(out=kT_ps, in_=kend[:, h, qs], identity=identb), ["kend%d" % h, "identb"], [nb])
                for h in range(4):
                    hb = HB[h]
                    nb = BN[id(hb)]
                    kT_ps = hb[:, 384:448].bitcast(BF16)
                    P.op("dve", lambda e, h=h, hb=hb: e.tensor_tensor(out=aTm4[h], in0=hb[:, 0:128], in1=triu, op=ALU.mult), [nb, "triu"], ["aTm%d" % h])
                    P.op("act", lambda e, h=h, kT_ps=kT_ps: e.copy(out=kit4[h], in_=kT_ps), [nb], ["kit%d" % h])
                    P.op("dve", lambda e, h=h: e.tensor_scalar(out=Sbf[:, h, :], in0=S[:, h, :], scalar1=d3[:, h, q:q + 1], scalar2=None, op0=ALU.mult),
                         ["S%d" % h, "d3_%d" % h], ["Sbf%d" % h])
                for h in range(4):
                    hb = HB[h]
                    nb = BN[id(hb)]
                    P.op("pe", lambda e, h=h, hb=hb: e.matmul(hb[:, 256:384], lhsT=aTm4[h], rhs=vb[:, v2, hs_[h]], start=True, stop=False),
                         ["aTm%d" % h, "vb%d" % v2], [nb])
                    P.op("pe", lambda e, h=h, hb=hb: e.matmul(hb[:, 256:384], lhsT=qdec[:, h, qs], rhs=Sbf[:, h, :], start=False, stop=True),
                         ["qdec%d" % h, "Sbf%d" % h], [nb])
                    P.op("pe", lambda e, h=h, hb=hb: e.matmul(hb[:, 128:256], lhsT=kit4[h], rhs=vb[:, v2, hs_[h]], start=True, stop=True),
                         ["kit%d" % h, "vb%d" % v2], [nb])
                for h in range(4):
                    hb = HB[h]
                    nb = BN[id(hb)]
                    P.op("dve", lambda e, h=h, hb=hb: e.scalar_tensor_tensor(out=S[:, h, :], in0=S[:, h, :], scalar=d1[:, h, q:q + 1], in1=hb[:, 128:256], op0=ALU.mult, op1=ALU.add),
                         ["S%d" % h, "d1_%d" % h, nb], ["S%d" % h])
                    P.op("act", lambda e, h=h, hb=hb: e.activation(out=junk4[h], in_=hb[:, 256:384], func=AF.Square, accum_out=ssq[:, h:h + 1]),
                         [nb], ["junk%d" % h, "ssq"])
                P.op("dve", lambda e: e.tensor_scalar(out=rstd, in0=ssq, scalar1=1.0 / 128, scalar2=RMS_EPS, op0=ALU.mult, op1=ALU.add), ["ssq"], ["rstd"])
                P.op("pool", lambda e: e.tensor_tensor(out=rstd, in0=rstd, in1=mhalf, op=ALU.pow), ["rstd", "mhalf"], ["rstd"])
                for h in range(4):
                    hb = HB[h]
                    nb = BN[id(hb)]
                    P.op("dve", lambda e, h=h, hb=hb: e.scalar_tensor_tensor(out=ghb[:, hs_[h]], in0=hb[:, 256:384], scalar=rstd[:, h:h + 1], in1=gz[:, v2, hs_[h]], op0=ALU.mult, op1=ALU.mult),
                         [nb, "rstd", "gz%d" % v2], ["ghb"])
                dma(GH[r0:r0 + 128, 0:512], ghb, ["ghb"], ["GH"], "ghst")

            SC = 128 ** -0.5
            fmi = 0
            tmi = 0
            for st in range(NST):
                front(x_d, st, xin, xb, xT, tpb)
                t0 = st * 512
                for h in range(4):
                    for slot in range(4):
                        bk = prj[pji[0] % 3]
                        bn = BN[id(bk)]
                        pji[0] += 1
                        fm_mm(bk, bn, 128, wf, "wf", slot * 512 + h * 128, xT)
                        if slot == 0:
                            P.op("act", lambda e, h=h, bk=bk: e.copy(out=hqT[:, h, :], in_=bk[:, :]), [bn], ["hqT%d" % h])
                        elif slot == 1:
                            P.op("act", lambda e, h=h, bk=bk: e.activation(out=sg[:, h, :], in_=bk[:, :], func=AF.Sigmoid), [bn], ["sg%d" % h])
                            P.op("act", lambda e, h=h: e.activation(out=gl[:, h, :], in_=sg[:, h, :], func=AF.Ln, scale=oml[:, h:h + 1], bias=lb[:, h:h + 1]),
                                 ["sg%d" % h, "oml", "lb"], ["gl%d" % h])
                            P.op("dve", lambda e, h=h: e.tensor_scalar(out=sg[:, h, :], in0=sg[:, h, :], scalar1=-1.0, scalar2=lbm1[:, h:h + 1], op0=ALU.add, op1=ALU.mult),
                                 ["sg%d" % h, "lbm1"], ["sg%d" % h])
                            P.op("dve", lambda e, h=h: e.tensor_tensor_scan(out=bb[:, h, :], data0=rmask, data1=gl[:, h, :], initial=0.0, op0=ALU.mult, op1=ALU.add),
                                 ["gl%d" % h, "rmask"], ["bb%d" % h])
                            bmid = bb[:, h, 63:512:128]
                            blast = bb[:, h, 127:512:128]
                            P.op("act", lambda e, h=h, bmid=bmid: e.activation(out=d3[:, h, :], in_=bmid, func=AF.Exp), ["bb%d" % h], ["d3_%d" % h])
                            P.op("act", lambda e, h=h, blast=blast: e.activation(out=d1[:, h, :], in_=blast, func=AF.Exp), ["bb%d" % h], ["d1_%d" % h])
                            P.op("dve", lambda e, h=h, bmid=bmid, blast=blast: e.tensor_tensor(out=dd[:, h, :], in0=blast, in1=bmid, op=ALU.subtract), ["bb%d" % h], ["dd%d" % h])
                            P.op("act", lambda e, h=h: e.activation(out=d2[:, h, :], in_=dd[:, h, :], func=AF.Exp), ["dd%d" % h], ["d2_%d" % h])
                            P.op("pool", lambda e, h=h, bmid=bmid: e.tensor_copy(out=dd[:, h, :], in_=bmid), ["bb%d" % h, "d2_%d" % h], ["dd%d" % h])
                            bb3 = bb[:, h, :].rearrange("p (q t) -> p q t", q=4)
                            P.op("dve", lambda e, h=h, bb3=bb3: e.tensor_tensor(out=bb3, in0=bb3, in1=dd[:, h, :].unsqueeze(2).to_broadcast([128, 4, 128]), op=ALU.subtract),
                                 ["bb%d" % h, "d3_%d" % h, "d1_%d" % h, "dd%d" % h], ["bb%d" % h])
                            P.op("act", lambda e, h=h: e.activation(out=gl[:, h, :], in_=bb[:, h, :], func=AF.Exp), ["bb%d" % h, "gl%d" % h], ["gl%d" % h])
                            P.op("act", lambda e, h=h: e.activation(out=bb[:, h, :], in_=bb[:, h, :], func=AF.Exp, scale=-1.0), ["bb%d" % h], ["bb%d" % h])
                            P.op("dve", lambda e, h=h: e.tensor_tensor(out=qdec[:, h, :], in0=hqT[:, h, :], in1=gl[:, h, :], op=ALU.mult),
                                 ["hqT%d" % h, "gl%d" % h], ["qdec%d" % h])
                            P.op("dve", lambda e, h=h: e.tensor_tensor(out=kinv[:, h, :], in0=sg[:, h, :], in1=bb[:, h, :], op=ALU.mult),
                                 ["sg%d" % h, "bb%d" % h], ["kinv%d" % h])
                            P.op("dve", lambda e, h=h: e.tensor_tensor(out=kend[:, h, :].rearrange("p (q t) -> p q t", q=4), in0=kinv[:, h, :].rearrange("p (q t) -> p q t", q=4),
                                                                     in1=d2[:, h, :].unsqueeze(2).to_broadcast([128, 4, 128]), op=ALU.mult),
                                 ["kinv%d" % h, "d2_%d" % h], ["kend%d" % h])
                        elif slot == 2:
                            P.op("act", lambda e, h=h, bk=bk: e.mul(out=fqb[:, h, :], in_=bk[:, :], mul=SC), [bn], ["fqb%d" % h])
                            dma(FQ[h, :, t0:t0 + 512], fqb[:, h, :], ["fqb%d" % h], ["FQ"], "fqst%d" % h)
                        else:
                            P.op("act", lambda e, h=h, bk=bk: e.copy(out=fkb[:, h, :], in_=bk[:, :]), [bn], ["fkb%d" % h])
                            dma(FK[h, :, t0:t0 + 512], fkb[:, h, :], ["fkb%d" % h], ["FK"], "fkst%d" % h)
                bk = prj[pji[0] % 3]
                bn = BN[id(bk)]
                pji[0] += 1
                fm_mm(bk, bn, 4, wf, "wf", 2048, xT)
                P.op("act", lambda e, bk=bk: e.activation(out=ffs, in_=bk[0:4, :], func=AF.Sigmoid, bias=fbf), [bn, "fbf"], ["ffs"])
                P.op("act", lambda e: e.activation(out=ls, in_=ffs, func=AF.Ln), ["ffs"], ["ls"])
                if st == 0:
                    P.op("dve", lambda e: e.tensor_tensor_scan(out=cT, data0=ones[0:4, :], data1=ls, initial=0.0, op0=ALU.mult, op1=ALU.add), ["ls", "ones"], ["cT"])
                else:
                    P.op("dve", lambda e: e.tensor_tensor_scan(out=cT, data0=ones[0:4, :], data1=ls, initial=ccar, op0=ALU.mult, op1=ALU.add), ["ls", "ones", "ccar"], ["cT"])
                P.op("pool", lambda e: e.tensor_copy(out=ccar, in_=cT[:, 511:512]), ["cT"], ["ccar"])
                dma(CT[:, t0:t0 + 512], cT, ["cT"], ["CT"], "ctst")
                for q in range(4):
                    a0_tile(t0, q)
            P.barrier()

        def phase_b0():
            A.reset()
            kTh = A.alloc([T], BF16)
            qTh = A.alloc([T], BF16)
            vaug = A.alloc([NT, 130], BF16)
            cq = A.alloc([T], F32)
            cqd = A.alloc([T], F32)
            ctr = A.alloc([128], F32)
            negc = A.alloc([NT], F32)
            fzg = A.alloc([4, 128], BF16)
            tmp = [A.alloc([512], F32) for _ in range(4)]
            pT = [A.alloc([512], BF16) for _ in range(4)]
            rden = A.alloc([4], F32)
            gfb = A.alloc([4, 128], BF16)
            P.op("pool", lambda e: e.memset(vaug, 1.0), [], ["vaug"])
            sTb = [banks[0], banks[1], banks[7], banks[6]]
            accb = [banks[2], banks[3], banks[4], banks[5]]
            ngb = banks[6]
            for h in range(4):
                dma(kTh, FK[h], ["FK"], ["kTh"], "bk")
                dma(qTh, FQ[h], ["FQ"], ["qTh"], "bq")
                dma(vaug[:, :, 0:128], FV[h].rearrange("(j p) d -> p j d", p=128), ["FV"], ["vaug"], "bv")
                dma(cq, CT[h].partition_broadcast(128), ["CT"], ["cq"], "bc")
                dma(ctr[0:NT, :], CT[h].rearrange("(j p) -> j p", p=128), ["CT"], ["ctr"], "br")
                P.op("pe", lambda e: e.transpose(out=ngb[:, 0:NT], in_=ctr[0:NT, :], identity=identf[0:NT, 0:NT]), ["ctr", "identf"], ["pb6"])
                P.op("act", lambda e: e.mul(out=negc, in_=ngb[:, 0:NT], mul=-1.0), ["pb6"], ["negc"])
                P.op("pool", lambda e: e.tensor_tensor(out=cqd.rearrange("p (j t) -> p j t", t=128), in0=cq.rearrange("p (j t) -> p j t", t=128),
                                                       in1=dmask.unsqueeze(1).to_broadcast([128, NT, 128]), op=ALU.add), ["cq", "dmask"], ["cqd"])
                steps = [(I, j) for I in range(NST) for j in range(4 * I + 4)]
                gstep = [0]

                def geom(I, j):
                    r = j - 4 * I
                    tq0 = 128 * max(r, 0)
                    return r, tq0, 512 - tq0, I * 512 + tq0

                def emit_qk(n):
                    I, j = steps[n]
                    r, tq0, N, c0 = geom(I, j)
                    sb_ = (gbase + n) % 4
                    sT = sTb[sb_]
                    P.op("pe", lambda e: e.matmul(sT[:, 0:N], lhsT=kTh[:, j * 128:(j + 1) * 128], rhs=qTh[:, c0:c0 + N], start=True, stop=True),
                         ["kTh", "qTh"], [BN[id(sT)]])

                def emit_soft(n):
                    I, j = steps[n]
                    r, tq0, N, c0 = geom(I, j)
                    sb_ = (gbase + n) % 4
                    sT = sTb[sb_]
                    bn = BN[id(sT)]
                    if r < 0:
                        P.op("dve", lambda e: e.tensor_tensor(out=tmp[sb_][:, 0:N], in0=sT[:, 0:N], in1=cq[:, c0:c0 + N], op=ALU.add),
                             [bn, "cq"], ["tmp%d" % sb_])
                    else:
                        P.op("dve", lambda e: e.tensor_tensor(out=tmp[sb_][:, 0:128], in0=sT[:, 0:128], in1=cqd[:, c0:c0 + 128], op=ALU.add),
                             [bn, "cqd"], ["tmp%d" % sb_])
                        if N > 128:
                            P.op("dve", lambda e: e.tensor_tensor(out=tmp[sb_][:, 128:N], in0=sT[:, 128:N], in1=cq[:, c0 + 128:c0 + N], op=ALU.add),
                                 [bn, "cq"], ["tmp%d" % sb_])
                    P.op("act", lambda e: e.activation(out=pT[sb_][:, 0:N], in_=tmp[sb_][:, 0:N], func=AF.Exp, bias=negc[:, j:j + 1]),
                         ["tmp%d" % sb_, "negc"], ["pT%d" % sb_])

                def emit_pv(n):
                    I, j = steps[n]
                    r, tq0, N, c0 = geom(I, j)
                    sb_ = (gbase + n) % 4
                    if j == 0:
                        for qq in range(4):
                            r0 = (I * 4 + qq) * 128
                            dma(fzg[:, qq, :], FZ[r0:r0 + 128, h * 128:(h + 1) * 128], ["FZ"], ["fzg"], "fzld")
                    for qq in range(max(r, 0), 4):
                        o0 = qq * 128 - tq0
                        P.op("pe", lambda e, qq=qq, o0=o0: e.matmul(accb[qq][:, 0:129], lhsT=pT[sb_][:, o0:o0 + 128], rhs=vaug[:, j, 0:129],
                                                                  start=(j == 0), stop=(j == 4 * I + qq)),
                             ["pT%d" % sb_, "vaug"], ["pb%d" % (2 + qq)])
                    if j == 4 * I + 3:
                        for qq in range(4):
                            r0 = (I * 4 + qq) * 128
                            P.op("dve", lambda e, qq=qq: e.reciprocal(out=rden[:, qq:qq + 1], in_=accb[qq][:, 128:129]), ["pb%d" % (2 + qq)], ["rden%d" % qq])
                            P.op("dve", lambda e, qq=qq: e.scalar_tensor_tensor(out=gfb[:, qq, :], in0=accb[qq][:, 0:128], scalar=rden[:, qq:qq + 1], in1=fzg[:, qq, :], op0=ALU.mult, op1=ALU.mult),
                                 ["pb%d" % (2 + qq), "rden%d" % qq, "fzg"], ["gfb%d" % qq])
                            dma(GH[r0:r0 + 128, 512 + h * 128:512 + (h + 1) * 128], gfb[:, qq, :], ["gfb%d" % qq], ["GH"], "gfst%d" % qq)

                gbase = h * len(steps)
                NS = len(steps)
                LOOK = 3
                for n in range(min(LOOK, NS)):
                    emit_qk(n)
                for n in range(NS):
                    emit_soft(n)
                    if n + LOOK < NS:
                        emit_qk(n + LOOK)
                    emit_pv(n)
            P.barrier()

        def phase_c(G_d, wo_d, res_d, dst_d, layer):
            A.reset()
            wst = [A.alloc([D], F32), A.alloc([D], F32)]
            wo = A.alloc([8, D], BF16)
            load_weights(wo_d, D, wo, "wo", wst)
            gt = [A.alloc([D], BF16) for _ in range(3)]
            gT = [A.alloc([8, 128], BF16), A.alloc([8, 128], BF16)]
            xr = [A.alloc([D], F32), A.alloc([D], F32)]
            z = [A.alloc([D], F32), A.alloc([D], F32)]
            st6 = A.alloc([2, 6], F32)
            mv = A.alloc([2], F32)
            rs = A.alloc([1], F32)
            nmr = A.alloc([1], F32)
            tpb = [banks[0], banks[1]]
            yb = [[banks[2], banks[3]], [banks[4], banks[5]]]
            g_ = A.alloc([D], F32)
            b_ = A.alloc([D], F32)
            dma(g_, lng_d[:, layer * D:(layer + 1) * D], [], ["lng"], "cst")
            dma(b_, lnb_d[:, layer * D:(layer + 1) * D], [], ["lnb"], "cst")
            def loads_g(i):
                s3 = i % 3
                r0 = i * 128
                dma(gt[s3], G_d[r0:r0 + 128, :], [], ["gt%d" % s3], "cg%d" % s3)

            def loads_x(i):
                s = i % 2
                r0 = i * 128
                dma(xr[s], res_d[r0:r0 + 128, :], [], ["xr%d" % s], "cx%d" % s)

            def trans(i):
                s = i % 2
                s3 = i % 3
                tv = tpb[s][:, 0:512].bitcast(BF16).rearrange("p (k t) -> p k t", k=8)
                for k in range(8):
                    P.op("pe", lambda e, k=k: e.transpose(out=tv[:, k, :], in_=gt[s3][:, k * 128:(k + 1) * 128], identity=identb),
                         ["gt%d" % s3, "identb"], ["pb%d" % s])
                P.op("act", lambda e: e.copy(out=gT[s], in_=tv), ["pb%d" % s], ["gT%d" % s])

            loads_g(0)
            if NT > 1:
                loads_g(1)
            loads_x(0)
            trans(0)
            for i in range(NT):
                s = i % 2
                r0 = i * 128
                if i + 2 < NT:
                    loads_g(i + 2)
                if i + 1 < NT:
                    loads_x(i + 1)
                    trans(i + 1)
                for c in range(2):
                    for k in range(8):
                        P.op("pe", lambda e, k=k, c=c, s=s: e.matmul(yb[s][c][:, :], lhsT=gT[s][:, k, :], rhs=wo[:, k, c * 512:(c + 1) * 512], start=(k == 0), stop=(k == 7)),
                             ["gT%d" % s, "wo"], ["pb%d" % (2 + 2 * s + c)])
                    P.op("dve", lambda e, c=c, s=s: e.scalar_tensor_tensor(out=z[s][:, c * 512:(c + 1) * 512], in0=xr[s][:, c * 512:(c + 1) * 512], scalar=ALPHA,
                                                                             in1=yb[s][c][:, :], op0=ALU.mult, op1=ALU.add),
                         ["xr%d" % s, "pb%d" % (2 + 2 * s + c)], ["z%d" % s])
                    P.op("dve", lambda e, c=c, s=s: e.bn_stats(out=st6[:, c, :], in_=z[s][:, c * 512:(c + 1) * 512]), ["z%d" % s], ["st6"])
                P.op("dve", lambda e: e.bn_aggr(out=mv, in_=st6.rearrange("p a b -> p (a b)")), ["st6"], ["mv"])
                P.op("dve", lambda e: e.tensor_scalar(out=rs, in0=mv[:, 1:2], scalar1=LN_EPS, scalar2=None, op0=ALU.add), ["mv"], ["rs"])
                P.op("act", lambda e: e.activation(out=rs, in_=rs, func=AF.Sqrt), ["rs"], ["rs"])
                P.op("dve", lambda e: e.reciprocal(out=rs, in_=rs), ["rs"], ["rs"])
                P.op("dve", lambda e: e.scalar_tensor_tensor(out=nmr, in0=mv[:, 0:1], scalar=-1.0, in1=rs, op0=ALU.mult, op1=ALU.mult), ["mv", "rs"], ["nmr"])
                P.op("act", lambda e, s=s: e.activation(out=z[s], in_=z[s], func=AF.Identity, scale=rs, bias=nmr), ["z%d" % s, "rs", "nmr"], ["z%d" % s])
                P.op("dve", lambda e, s=s: e.tensor_tensor(out=z[s], in0=z[s], in1=g_, op=ALU.mult), ["z%d" % s, "lng"], ["z%d" % s])
                P.op("pool", lambda e, s=s: e.tensor_tensor(out=z[s], in0=z[s], in1=b_, op=ALU.add), ["z%d" % s, "lnb"], ["z%d" % s])
                dma(dst_d[r0:r0 + 128, :], z[s], ["z%d" % s], ["DST"], "co%d" % s, eng="pool")
            P.barrier()


        def phase_a1():
            A.reset()
            big1 = A.alloc([6144], F32)
            wst = [big1[:, 0:3072], big1[:, 3072:6144]]
            wf = A.alloc([8, 1032], BF16)
            wt = A.alloc([8, 3072], BF16)
            load_weights(wf1_d, 1032, wf, "wf", wst)
            load_weights(wt1_d, 3072, wt, "wt", wst)
            P.barrier()
            so = big1[:, 0:2048].rearrange("p (a b) -> p a b", a=2)
            gz1 = big1[:, 2048:4096].rearrange("p (a b) -> p a b", a=2)
            ho = big1[:, 4096:5120]
            sil = big1[:, 5120:5632]
            cacc2 = [big1[:, 5632:6144], A.alloc([512], F32)]
            xin = A.alloc([2, D], F32)
            xb = A.alloc([2, D], BF16)
            xT = A.alloc([8, 512], BF16)
            ng1 = A.alloc([D], F32)
            cw = A.alloc([32], F32)
            cb = A.alloc([8], F32)
            bi = A.alloc([1], F32, parts=4)
            bfm = A.alloc([1], F32, parts=4)
            dma(ng1, ng1_d, [], ["ng1"], "cst")
            dma(cw, cw_d, [], ["cw"], "cst")
            dma(cb, cb_d, [], ["cb"], "cst")
            dma(bi, bi_d, [], ["bi"], "cst")
            dma(bfm, bfm_d, [], ["bfm"], "cst")
            uq = A.alloc([8, 515], F32)
            sq2 = [A.alloc([512], F32), A.alloc([512], F32)]
            qk = A.alloc([8, 512], BF16)
            vf = A.alloc([2, D], BF16)
            ig = A.alloc([512], F32, parts=4)
            lg = A.alloc([512], F32, parts=4)
            Bc = A.alloc([512], F32, parts=4)
            aa = A.alloc([512], F32, parts=4)
            RR = A.alloc([512], F32, parts=4)
            Rpe = A.alloc([4], F32, parts=4)
            ea = A.alloc([512], F32, parts=4)
            er = A.alloc([512], F32, parts=4)
            th = A.alloc([512], F32, parts=4)
            cB = A.alloc([1], F32, parts=4)
            cR = A.alloc([1], F32, parts=4)
            gtok = A.alloc([4, 12], F32)
            dRb = A.alloc([16], F32)
            dRl = A.alloc([4], F32)
            PTm = [A.alloc([128], BF16) for _ in range(4)]
            ktk = [A.alloc([128], BF16) for _ in range(4)]
            vau = [A.alloc([258], BF16) for _ in range(4)]
            Cst = A.alloc([4, 257], F32)
            Cbf = A.alloc([4, 258], BF16)
            junk = A.alloc([256], F32)
            ssq = A.alloc([4], F32)
            rstd = A.alloc([4], F32)
            t1 = A.alloc([4], F32)
            rsv = A.alloc([4], F32)
            g1b = A.alloc([D], BF16)
            P.op("pool", lambda e: e.memset(Cst, 0.0), [], ["Cst0", "Cst1", "Cst2", "Cst3"])
            P.op("pool", lambda e: e.memset(Cbf, 0.0), [], ["Cbf0", "Cbf1", "Cbf2", "Cbf3"])
            P.op("pool", lambda e: e.memset(uq, 0.0), [], ["uq%d" % b for b in range(8)])
            P.op("pool", lambda e: e.memset(dRl, 1.0), [], ["dRl"])
            P.op("pool", lambda e: e.memset(cB, 0.0), [], ["cB"])
            P.op("pool", lambda e: e.memset(cR, 0.0), [], ["cR"])
            tpb = [banks[0]]
            prj = [banks[1], banks[2]]
            fmb = prj
            tmb = prj
            mA = banks[0]
            HB4 = [banks[3], banks[4], banks[5], banks[6]]
            UB = banks[7]
            SC = 128 ** -0.5
            pji = [0]

            class _Ctr:
                pass
            def conv_block(b):
                bk = prj[pji[0] % len(prj)]
                bn = BN[id(bk)]
                pji[0] += 1
                cacc = cacc2[b % 2]
                sq = sq2[b % 2]
                cn = "cacc%d" % (b % 2)
                sn_ = "sq%d" % (b % 2)
                fm_mm(bk, bn, 128, wf, "wf", b * 128, xT)
                P.op("act", lambda e: e.copy(out=uq[:, b, 3:515], in_=bk[:, :]), [bn], ["uq%d" % b])
                P.op("dve", lambda e: e.tensor_scalar(out=cacc, in0=uq[:, b, 0:512], scalar1=cw[:, 4 * b:4 * b + 1], scalar2=cb[:, b:b + 1], op0=ALU.mult, op1=ALU.add),
                     ["uq%d" % b, "cw", "cb"], [cn])
                for j in range(1, 4):
                    P.op("dve", lambda e, j=j: e.scalar_tensor_tensor(out=cacc, in0=uq[:, b, j:j + 512], scalar=cw[:, 4 * b + j:4 * b + j + 1], in1=cacc, op0=ALU.mult, op1=ALU.add),
                         ["uq%d" % b, "cw", cn], [cn])
                P.op("pool", lambda e: e.tensor_copy(out=uq[:, b, 0:3], in_=uq[:, b, 512:515]), ["uq%d" % b], ["uq%d" % b])
                if b < 4:
                    P.op("act", lambda e: e.activation(out=sq, in_=cacc, func=AF.Silu), [cn], [sn_])
                    P.op("act", lambda e: e.mul(out=qk[:, b, :], in_=sq, mul=SC), [sn_], ["qk%d" % b])
                else:
                    P.op("act", lambda e: e.activation(out=qk[:, b, :], in_=cacc, func=AF.Silu), [cn], ["qk%d" % b])

            def a1_tile(t0, q):
                if True:
                    r0 = t0 + q * 128
                    qs = slice(q * 128, (q + 1) * 128)
                    v2 = q % 2
                    for c in range(6):
                        bk = prj[pji[0] % len(prj)]
                        bn = BN[id(bk)]
                        pji[0] += 1
                        tm_mm(bk, bn, q, wt, "wt", c * 512, 512, xT)
                        cs = slice((c % 2) * 512, (c % 2) * 512 + 512)
                        if c < 2:
                            P.op("act", lambda e, v2=v2, bk=bk, cs=cs: e.copy(out=vf[:, v2, cs], in_=bk[:, :]), [bn], ["vf%d" % v2])
                        elif c < 4:
                            P.op("act", lambda e, v2=v2, bk=bk, cs=cs: e.activation(out=so[:, v2, cs], in_=bk[:, :], func=AF.Sigmoid), [bn], ["so%d" % v2])
                        else:
                            P.op("act", lambda e, bk=bk: e.activation(out=sil, in_=bk[:, :], func=AF.Silu), [bn], ["sil"])
                            P.op("dve", lambda e, v2=v2, cs=cs: e.tensor_tensor(out=gz1[:, v2, cs], in0=sil, in1=ng1[:, cs], op=ALU.mult), ["sil", "ng1"], ["gz1_%d" % v2])
                    for pair in ((0, 1, 2, 3),):
                        for h in pair:
                            bA = HB4[h]
                            nA = BN[id(bA)]
                            kT_ps = bA[:, 128:192].bitcast(BF16)
                            P.op("pe", lambda e, h=h, bA=bA: e.matmul(bA[:, 0:128], lhsT=qk[:, 4 + h, qs], rhs=qk[:, h, qs], start=True, stop=True),
                                 ["qk%d" % h, "qk%d" % (4 + h)], [nA])
                            P.op("pe", lambda e, h=h, kT_ps=kT_ps: e.transpose(out=kT_ps, in_=qk[:, 4 + h, qs], identity=identb), ["qk%d" % (4 + h), "identb"], [nA])
                        for h in pair:
                            bA = HB4[h]
                            nA = BN[id(bA)]
                            kT_ps = bA[:, 128:192].bitcast(BF16)
                            hv = slice(h * 256, (h + 1) * 256)
                            P.op("dve", lambda e, h=h, bA=bA: e.tensor_tensor(out=PTm[h], in0=bA[:, 0:128], in1=triu, op=ALU.mult), [nA, "triu"], ["PTm%d" % h])
                            P.op("act", lambda e, h=h, kT_ps=kT_ps: e.copy(out=ktk[h], in_=kT_ps), [nA], ["ktk%d" % h])
                            P.op("pool", lambda e, h=h, hv=hv: e.tensor_scalar(out=vau[h][:, 0:256], in0=vf[:, v2, hv], scalar1=gtok[:, q, h:h + 1], scalar2=1.0, op0=ALU.mult, op1=ALU.mult),
                                 ["vf%d" % v2, "gtok"], ["vau%d" % h])
                            P.op("pool", lambda e, h=h: e.tensor_copy(out=vau[h][:, 256:257], in_=gtok[:, q, h:h + 1]), ["gtok", "vau%d" % h], ["vau%d" % h])
                        for h in pair:
                            bA = HB4[h]
                            bB = UB
                            nA = BN[id(bA)]
                            nB = BN[id(bB)]
                            s2 = h
                            P.op("pe", lambda e, s2=s2, bA=bA: e.matmul(bA[:, 192:449], lhsT=PTm[s2], rhs=vau[s2][:, 0:257], start=True, stop=False), ["PTm%d" % s2, "vau%d" % s2], [nA])
                            P.op("pe", lambda e, h=h, bA=bA: e.matmul(bA[:, 192:449], lhsT=qk[:, h, qs], rhs=Cbf[:, h, 0:257], start=False, stop=True), ["qk%d" % h, "Cbf%d" % h], [nA])
                            P.op("pe", lambda e, s2=s2, bB=bB: e.matmul(bB[:, 0:257], lhsT=ktk[s2], rhs=vau[s2][:, 0:257], start=True, stop=True), ["ktk%d" % s2, "vau%d" % s2], [nB])
                            dprev = dRl[:, h:h + 1] if q == 0 else dRb[:, 4 * h + q - 1:4 * h + q]
                            P.op("dve", lambda e, h=h, bB=bB, dprev=dprev: e.scalar_tensor_tensor(out=Cst[:, h, :], in0=Cst[:, h, :], scalar=dprev, in1=bB[:, 0:257], op0=ALU.mult, op1=ALU.add),
                                 ["Cst%d" % h, nB, "dRb", "dRl"], ["Cst%d" % h])
                        for h in pair:
                            bA = HB4[h]
                            bB = UB
                            nA = BN[id(bA)]
                            nB = BN[id(bB)]
                            hv = slice(h * 256, (h + 1) * 256)
                            P.op("act", lambda e, h=h: e.activation(out=Cbf[:, h, 0:257], in_=Cst[:, h, :], func=AF.Copy, scale=dRb[:, 4 * h + q:4 * h + q + 1]),
                                 ["Cst%d" % h, "dRb"], ["Cbf%d" % h])
                            P.op("dve", lambda e, h=h, bA=bA: e.tensor_scalar(out=t1[:, h:h + 1], in0=bA[:, 448:449], scalar1=gtok[:, q, 4 + h:5 + h], scalar2=None, op0=ALU.mult),
                                 [nA, "gtok"], ["t1_%d" % h])
                            P.op("dve", lambda e, h=h: e.scalar_tensor_tensor(out=t1[:, h:h + 1], in0=t1[:, h:h + 1], scalar=-1.0, in1=t1[:, h:h + 1], op0=ALU.mult, op1=ALU.max),
                                 ["t1_%d" % h], ["t1_%d" % h])
                            P.op("dve", lambda e, h=h: e.tensor_tensor(out=t1[:, h:h + 1], in0=t1[:, h:h + 1], in1=gtok[:, q, 8 + h:9 + h], op=ALU.max), ["t1_%d" % h, "gtok"], ["t1_%d" % h])
                            P.op("dve", lambda e, h=h: e.reciprocal(out=t1[:, h:h + 1], in_=t1[:, h:h + 1]), ["t1_%d" % h], ["t1_%d" % h])
                            P.op("dve", lambda e, h=h: e.tensor_tensor(out=rsv[:, h:h + 1], in0=t1[:, h:h + 1], in1=gtok[:, q, 4 + h:5 + h], op=ALU.mult), ["t1_%d" % h, "gtok"], ["rsv%d" % h])
                            P.op("dve", lambda e, h=h, hv=hv, bA=bA: e.scalar_tensor_tensor(out=ho[:, hv], in0=bA[:, 192:448], scalar=rsv[:, h:h + 1], in1=so[:, v2, hv], op0=ALU.mult, op1=ALU.mult),
                                 [nA, "rsv%d" % h, "so%d" % v2], ["ho%d" % h])
                            P.op("act", lambda e, h=h, hv=hv: e.activation(out=junk, in_=ho[:, hv], func=AF.Square, accum_out=ssq[:, h:h + 1]), ["ho%d" % h], ["junk", "ssq"])
                    P.op("dve", lambda e: e.tensor_scalar(out=rstd, in0=ssq, scalar1=1.0 / 256, scalar2=RMS_EPS, op0=ALU.mult, op1=ALU.add), ["ssq"], ["rstd"])
                    P.op("pool", lambda e: e.tensor_tensor(out=rstd, in0=rstd, in1=mhalf, op=ALU.pow), ["rstd", "mhalf"], ["rstd"])
                    for h in range(4):
                        hv = slice(h * 256, (h + 1) * 256)
                        P.op("dve", lambda e, h=h, hv=hv: e.scalar_tensor_tensor(out=g1b[:, hv], in0=ho[:, hv], scalar=rstd[:, h:h + 1], in1=gz1[:, v2, hv], op0=ALU.mult, op1=ALU.mult),
                             ["ho%d" % h, "rstd", "gz1_%d" % v2], ["g1b"])
                    dma(G1[r0:r0 + 128, :], g1b, ["g1b"], ["G1"], "g1st")
            fmi = 0
            tmi = 0
            for st in range(NST):
                front(H1, st, xin, xb, xT, tpb)
                t0 = st * 512
                for gi in range(2):
                    bk = prj[pji[0] % len(prj)]
                    bn = BN[id(bk)]
                    pji[0] += 1
                    fm_mm(bk, bn, 4, wf, "wf", 1024 + 4 * gi, xT)
                    if gi == 0:
                        P.op("act", lambda e, bk=bk: e.activation(out=ig, in_=bk[0:4, :], func=AF.Identity, bias=bi), [bn, "bi"], ["ig"])
                    else:
                        P.op("act", lambda e, bk=bk: e.activation(out=lg, in_=bk[0:4, :], func=AF.Sigmoid, bias=bfm), [bn, "bfm"], ["lg"])
                        P.op("act", lambda e: e.activation(out=lg, in_=lg, func=AF.Ln), ["lg"], ["lg"])
                P.op("dve", lambda e: e.tensor_tensor_scan(out=Bc, data0=ones[0:4, :], data1=lg, initial=cB, op0=ALU.mult, op1=ALU.add), ["lg", "ones", "cB"], ["Bc"])
                P.op("dve", lambda e: e.tensor_tensor(out=aa, in0=ig, in1=Bc, op=ALU.subtract), ["ig", "Bc"], ["aa"])
                P.op("pool", lambda e: e.tensor_copy(out=Rpe[:, 0:1], in_=cR), ["cR"], ["Rpe"])
                P.op("dve", lambda e: e.tensor_tensor_scan(out=RR, data0=aa, data1=aa, initial=cR, op0=ALU.max, op1=ALU.max), ["aa", "cR"], ["RR"])
                P.op("pool", lambda e: e.tensor_copy(out=Rpe[:, 1:4], in_=RR[:, 127:384:128]), ["RR"], ["Rpe"])
                P.op("pool", lambda e: e.tensor_copy(out=cB, in_=Bc[:, 511:512]), ["Bc"], ["cB"])
                P.op("pool", lambda e: e.tensor_copy(out=cR, in_=RR[:, 511:512]), ["RR", "Rpe"], ["cR"])
                rpb = Rpe.unsqueeze(2).to_broadcast([4, 4, 128])
                v3 = lambda a: a.rearrange("p (q t) -> p q t", q=4)
                P.op("dve", lambda e: e.tensor_tensor(out=v3(ea), in0=v3(aa), in1=rpb, op=ALU.subtract), ["aa", "Rpe"], ["ea"])
                P.op("act", lambda e: e.activation(out=ea, in_=ea, func=AF.Exp), ["ea"], ["ea"])
                P.op("dve", lambda e: e.tensor_tensor(out=v3(er), in0=v3(RR), in1=rpb, op=ALU.subtract), ["RR", "Rpe"], ["er"])
                P.op("act", lambda e: e.activation(out=er, in_=er, func=AF.Exp, scale=-1.0), ["er"], ["er"])
                P.op("dve", lambda e: e.tensor_tensor(out=th, in0=Bc, in1=RR, op=ALU.add), ["Bc", "RR"], ["th"])
                P.op("act", lambda e: e.activation(out=th, in_=th, func=AF.Exp, scale=-1.0), ["th"], ["th"])
                gtp = mA[:, 448:496].rearrange("p (q c) -> p q c", q=4)
                for q in range(4):
                    for gi, (src, sn) in enumerate(((ea, "ea"), (er, "er"), (th, "th"))):
                        P.op("pe", lambda e, q=q, gi=gi, src=src: e.transpose(out=gtp[:, q, gi * 4:(gi + 1) * 4], in_=src[:, q * 128:(q + 1) * 128], identity=identf[0:4, 0:4]),
                             [sn, "identf"], ["pb0"])
                P.op("act", lambda e: e.copy(out=gtok, in_=gtp), ["pb0"], ["gtok"])
                for h in range(4):
                    P.op("pe", lambda e, h=h: e.matmul(mA[:, 496 + 4 * h:500 + 4 * h], lhsT=sel[:, h * 128:(h + 1) * 128], rhs=er[:, 127:512:128], start=True, stop=True),
                         ["sel", "er"], ["pb0"])
                if st > 0:
                    P.op("pool", lambda e: e.tensor_copy(out=dRl, in_=dRb[:, 3:16:4]), ["dRb"], ["dRl"])
                P.op("act", lambda e: e.copy(out=dRb, in_=mA[:, 496:512]), ["pb0"], ["dRb"])
                for b in range(8):
                    conv_block(b)
                for q in range(4):
                    a1_tile(t0, q)

            P.barrier()

        if "A0" in phases:
            phase_a0()
        if "B0" in phases:
            phase_b0()
        if "C0" in phases:
            phase_c(GH, wo0_d, x_d, H1, 0)
        if "A1" in phases:
            phase_a1()
        if "C1" in phases:
            phase_c(G1, wo1_d, H1, out_d, 1)

        P.finish(["DST", "GH", "G1", "FQ", "FK", "FV", "FZ", "CT"])
        P.emit()
        build.stats = P.stats()
    return nc


def host_consts():
    s = np.arange(128)[:, None]
    t = np.arange(128)[None, :]
    triu = (s <= t).astype(np.float32)
    dmask = np.where(s <= t, 0.0, -1.0e9).astype(np.float32)
    rmask = np.ones((128, 512), np.float32)
    rmask[:, 0::128] = 0.0
    sel = np.zeros((4, 512), np.float32)
    for h in range(4):
        sel[h, h * 128:(h + 1) * 128] = 1.0
    return {"ident": np.eye(128, dtype=np.float32), "triu": triu, "dmask": dmask, "rmask": rmask, "sel": sel}


def host_params(hgrn_lb_logits, ab_w_in, ab_fox_bf, ab_hgrn_norm_g, ab_w_out, c_w_in,
                c_conv_w, c_conv_b, c_bi, c_bf, c_norm_g, c_w_out, ln_g, ln_b):
    f = np.float32
    w0 = np.asarray(ab_w_in[0], f)
    sl = lambda a, b: w0[:, a:b]
    wf0 = np.concatenate([sl(0, 512), sl(512, 1024), sl(2048, 2560), sl(2560, 3072), sl(3584, 3588)], axis=1)
    wt0 = np.concatenate([sl(1024, 1536), sl(1536, 2048), sl(3072, 3584), sl(3588, 4100)], axis=1)
    w1 = np.asarray(c_w_in[0], f)
    wf1 = np.concatenate([w1[:, 0:512], w1[:, 512:1024], w1[:, 2048:2052], w1[:, 2052:2056]], axis=1)
    wt1 = np.concatenate([w1[:, 1024:2048], w1[:, 2056:3080], w1[:, 3080:4104]], axis=1)
    lbl = np.asarray(hgrn_lb_logits, f).reshape(3, 4, 128).transpose(2, 1, 0).reshape(128, 12)
    rep = lambda v: np.ascontiguousarray(np.broadcast_to(np.asarray(v, f).reshape(1, -1), (128, np.asarray(v).size)))
    cw = np.asarray(c_conv_w[0], f).reshape(4, 8, 128).transpose(2, 1, 0).reshape(128, 32)
    cb = np.asarray(c_conv_b[0], f).reshape(8, 128).T
    d = {
        "wf0": wf0, "wt0": wt0, "wo0": np.asarray(ab_w_out[0], f),
        "wf1": wf1, "wt1": wt1, "wo1": np.asarray(c_w_out[0], f),
        "lbl": lbl, "fbf": np.asarray(ab_fox_bf[0], f).reshape(4, 1),
        "hng": rep(ab_hgrn_norm_g[0]), "lng": rep(np.asarray(ln_g, f).reshape(-1)), "lnb": rep(np.asarray(ln_b, f).reshape(-1)),
        "cw": cw, "cb": cb, "bi": np.asarray(c_bi[0], f).reshape(4, 1), "bfm": np.asarray(c_bf[0], f).reshape(4, 1),
        "ng1": rep(c_norm_g[0]),
    }
    d.update(host_consts())
    return {k: np.ascontiguousarray(v, dtype=f) for k, v in d.items()}


_NC_CACHE = {}


def kernel(x, hgrn_lb_logits, ab_w_in, ab_fox_bf, ab_hgrn_norm_g, ab_w_out, c_w_in,
           c_conv_w, c_conv_b, c_bi, c_bf, c_norm_g, c_w_out, ln_g, ln_b):
    x = np.asarray(x, np.float32)
    B, T, _ = x.shape
    params = host_params(hgrn_lb_logits, ab_w_in, ab_fox_bf, ab_hgrn_norm_g, ab_w_out, c_w_in,
                         c_conv_w, c_conv_b, c_bi, c_bf, c_norm_g, c_w_out, ln_g, ln_b)
    if T not in _NC_CACHE:
        _NC_CACHE[T] = build(T)
    nc = _NC_CACHE[T]
    owner = {2 * b: b for b in range(B)}
    zx = np.zeros((T, D), np.float32)
    in_maps = []
    for c in range(8):
        m = dict(params)
        m["x"] = np.ascontiguousarray(x[owner[c]]) if c in owner else zx
        in_maps.append(m)
    res = run_bass_kernel_spmd(nc, in_maps, core_ids=list(range(8)))
    return np.stack([np.asarray(res.results[2 * b]["out"], np.float32) for b in range(B)], axis=0)
```
